# Optimizing a Trainium2 kernel written in Bass

```python
import math
import jax, jax.numpy as jnp
from jax import lax
import numpy as np

D_MODEL = 1024
BATCH = 8
SEQ = 4096
DEPTH = 4

MEM_LEN = 256
SSM_WIDTH = D_MODEL // 2
SSM_GROUP = 16
SSM_GROUPS = SSM_WIDTH // SSM_GROUP
SSM_STATE = 64
DT_MIN = 1e-3
DT_MAX = 1e-1
MLA_HEADS = 8
MLA_NOPE = 64
MLA_ROPE = 32
MLA_V = 64
MLA_Q_RANK = 256
MLA_KV_RANK = 128
MLA_WIDTH = MLA_HEADS * MLA_V
ROPE_THETA = 10000.0
Q_BLOCK = 128
X_HEADS = 4
X_HEAD_DIM = 128
X_WIDTH = X_HEADS * X_HEAD_DIM
N_BRANCH = 3
IN_WIDTHS = (SSM_WIDTH, SSM_WIDTH, MLA_Q_RANK, MLA_KV_RANK, MLA_ROPE, MLA_WIDTH, X_WIDTH, X_WIDTH, N_BRANCH * D_MODEL)
D_IN = sum(IN_WIDTHS)
ALPHA = (2 * DEPTH) ** 0.25
BETA = (8 * DEPTH) ** -0.25
NORM_EPS = 1e-5
POS_OFFSET_MAX = 1024

kernel_name = 'hybrid_s5_mla_memory_gated_deepnorm'


def _layer_norm(x, g, b):
    xf = x.astype(jnp.float32)
    mu = jnp.mean(xf, axis=-1, keepdims=True)
    var = jnp.mean(jnp.square(xf - mu), axis=-1, keepdims=True)
    y = (xf - mu) * lax.rsqrt(var + NORM_EPS) * g.astype(jnp.float32) + b.astype(jnp.float32)
    return y.astype(x.dtype)


def _rms_norm(x, g):
    xf = x.astype(jnp.float32)
    y = xf * lax.rsqrt(jnp.mean(jnp.square(xf), axis=-1, keepdims=True) + NORM_EPS) * g.astype(jnp.float32)
    return y.astype(x.dtype)


def _rope_tables(positions):
    inv_freq = ROPE_THETA ** (-jnp.arange(0, MLA_ROPE, 2, dtype=jnp.float32) / MLA_ROPE)
    ang = positions.astype(jnp.float32)[..., None] * inv_freq
    return jnp.cos(ang)[:, :, None, :], jnp.sin(ang)[:, :, None, :]


def _apply_rope(t, cos, sin):
    tf = t.astype(jnp.float32)
    t1, t2 = jnp.split(tf, 2, axis=-1)
    out = jnp.concatenate([t1 * cos - t2 * sin, t1 * sin + t2 * cos], axis=-1)
    return out.astype(t.dtype)


def _complex_scan_combine(e1, e2):
    a1r, a1i, b1r, b1i = e1
    a2r, a2i, b2r, b2i = e2
    ar = a1r * a2r - a1i * a2i
    ai = a1r * a2i + a1i * a2r
    br = a2r * b1r - a2i * b1i + b2r
    bi = a2r * b1i + a2i * b1r + b2i
    return ar, ai, br, bi


def _s5_ssm(u, a_re, a_im, log_dt, b_re, b_im, c_re, c_im, d_skip):
    bsz, s, _ = u.shape
    f32 = jnp.float32
    uf = u.astype(f32).reshape(bsz, s, SSM_GROUPS, SSM_GROUP)
    dt = jnp.exp(log_dt.astype(f32))[:, None]
    lr, li = a_re.astype(f32), a_im.astype(f32)
    mag = jnp.exp(lr * dt)
    lb_re = mag * jnp.cos(li * dt)
    lb_im = mag * jnp.sin(li * dt)
    nr, ni = lb_re - 1.0, lb_im
    den = lr * lr + li * li
    f_re = (nr * lr + ni * li) / den
    f_im = (ni * lr - nr * li) / den
    br, bi = b_re.astype(f32), b_im.astype(f32)
    bb_re = f_re[..., None] * br - f_im[..., None] * bi
    bb_im = f_re[..., None] * bi + f_im[..., None] * br
    bu_re = jnp.einsum('bsgc,gpc->bsgp', uf, bb_re)
    bu_im = jnp.einsum('bsgc,gpc->bsgp', uf, bb_im)
    a_re_t = jnp.broadcast_to(lb_re, bu_re.shape)
    a_im_t = jnp.broadcast_to(lb_im, bu_im.shape)
    _, _, h_re, h_im = lax.associative_scan(_complex_scan_combine, (a_re_t, a_im_t, bu_re, bu_im), axis=1)
    y = (jnp.einsum('bsgp,gcp->bsgc', h_re, c_re.astype(f32))
         - jnp.einsum('bsgp,gcp->bsgc', h_im, c_im.astype(f32)))
    y = y.reshape(bsz, s, SSM_WIDTH) + d_skip.astype(f32) * u.astype(f32)
    return y.astype(u.dtype)


def _s5_glu(y, w_glu, b_glu):
    g = jax.nn.gelu(y)
    a, b = jnp.split(g @ w_glu + b_glu, 2, axis=-1)
    return a * jax.nn.sigmoid(b)


def _mla_attention(c_q, c_kv, k_rope_in, q_norm, w_uq, kv_norm, w_ukv, cos, sin):
    bsz, s, _ = c_q.shape
    q = (_rms_norm(c_q, q_norm) @ w_uq).reshape(bsz, s, MLA_HEADS, MLA_NOPE + MLA_ROPE)
    q_nope = q[..., :MLA_NOPE]
    q_rope = _apply_rope(q[..., MLA_NOPE:], cos, sin)
    kv = (_rms_norm(c_kv, kv_norm) @ w_ukv).reshape(bsz, s, MLA_HEADS, MLA_NOPE + MLA_V)
    k_nope, v = kv[..., :MLA_NOPE], kv[..., MLA_NOPE:]
    k_rope = _apply_rope(k_rope_in[:, :, None, :], cos, sin)[:, :, 0, :]
    scale = (MLA_NOPE + MLA_ROPE) ** -0.5
    neg = jnp.finfo(jnp.float32).min
    outs = []
    for blk in range(s // Q_BLOCK):
        q0 = blk * Q_BLOCK
        kend = q0 + Q_BLOCK
        sc = (jnp.einsum('bqhd,bkhd->bhqk', q_nope[:, q0:kend], k_nope[:, :kend])
              + jnp.einsum('bqhr,bkr->bhqk', q_rope[:, q0:kend], k_rope[:, :kend]))
        sc = sc.astype(jnp.float32) * scale
        causal = (q0 + jnp.arange(Q_BLOCK))[:, None] >= jnp.arange(kend)[None, :]
        p = jax.nn.softmax(jnp.where(causal, sc, neg), axis=-1)
        outs.append(jnp.einsum('bhqk,bkhd->bqhd', p.astype(v.dtype), v[:, :kend]))
    return jnp.concatenate(outs, axis=1).reshape(bsz, s, MLA_WIDTH)


def _memory_attention(q_in, mem, w_mem_kv):
    bsz, s, _ = q_in.shape
    m = mem.shape[1]
    kv = (mem @ w_mem_kv).reshape(bsz, m, 2, X_HEADS, X_HEAD_DIM)
    k, v = kv[:, :, 0], kv[:, :, 1]
    q = q_in.reshape(bsz, s, X_HEADS, X_HEAD_DIM)
    sc = jnp.einsum('bshd,bmhd->bhsm', q, k).astype(jnp.float32) * X_HEAD_DIM ** -0.5
    p = jax.nn.softmax(sc, axis=-1)
    return jnp.einsum('bhsm,bmhd->bshd', p.astype(v.dtype), v).reshape(bsz, s, X_WIDTH)


def setup_inputs(seed: int = 0) -> dict:
    key = jax.random.key(seed)
    ks = jax.random.split(key, 26)
    f32 = jnp.float32

    def nrm(k, shape, scale):
        return scale * jax.random.normal(k, shape, f32)

    x = nrm(ks[0], (BATCH, SEQ, D_MODEL), 1.0)
    mem = nrm(ks[1], (BATCH, MEM_LEN, D_MODEL), 1.0)
    offsets = jax.random.randint(ks[2], (BATCH, 1), 0, POS_OFFSET_MAX, dtype=jnp.int32)
    positions = offsets + jnp.arange(SEQ, dtype=jnp.int32)[None, :]
    w_in = nrm(ks[3], (DEPTH, D_MODEL, D_IN), D_MODEL ** -0.5)
    b_gate = nrm(ks[4], (DEPTH, N_BRANCH * D_MODEL), 0.01)
    ssm_a_re = -0.5 + nrm(ks[5], (DEPTH, SSM_GROUPS, SSM_STATE), 0.01)
    ssm_a_im = math.pi * jnp.arange(SSM_STATE, dtype=f32) + nrm(ks[6], (DEPTH, SSM_GROUPS, SSM_STATE), 0.01)
    ssm_log_dt = jax.random.uniform(ks[7], (DEPTH, SSM_GROUPS), f32, math.log(DT_MIN), math.log(DT_MAX))
    ssm_b_re = nrm(ks[8], (DEPTH, SSM_GROUPS, SSM_STATE, SSM_GROUP), (2 * SSM_GROUP) ** -0.5)
    ssm_b_im = nrm(ks[9], (DEPTH, SSM_GROUPS, SSM_STATE, SSM_GROUP), (2 * SSM_GROUP) ** -0.5)
    ssm_c_re = nrm(ks[10], (DEPTH, SSM_GROUPS, SSM_GROUP, SSM_STATE), (2 * SSM_STATE) ** -0.5)
    ssm_c_im = nrm(ks[11], (DEPTH, SSM_GROUPS, SSM_GROUP, SSM_STATE), (2 * SSM_STATE) ** -0.5)
    ssm_d = nrm(ks[12], (DEPTH, SSM_WIDTH), 1.0)
    w_glu = nrm(ks[13], (DEPTH, SSM_WIDTH, 2 * SSM_WIDTH), SSM_WIDTH ** -0.5)
    b_glu = nrm(ks[14], (DEPTH, 2 * SSM_WIDTH), 0.01)
    mla_q_norm = 1.0 + nrm(ks[15], (DEPTH, MLA_Q_RANK), 0.01)
    w_uq = nrm(ks[16], (DEPTH, MLA_Q_RANK, MLA_HEADS * (MLA_NOPE + MLA_ROPE)), MLA_Q_RANK ** -0.5)
    mla_kv_norm = 1.0 + nrm(ks[17], (DEPTH, MLA_KV_RANK), 0.01)
    w_ukv = nrm(ks[18], (DEPTH, MLA_KV_RANK, MLA_HEADS * (MLA_NOPE + MLA_V)), MLA_KV_RANK ** -0.5)
    w_mem_kv = nrm(ks[19], (DEPTH, D_MODEL, 2 * X_WIDTH), D_MODEL ** -0.5)
    p_ssm = nrm(ks[20], (DEPTH, SSM_WIDTH, D_MODEL), BETA * SSM_WIDTH ** -0.5)
    p_mla = nrm(ks[21], (DEPTH, MLA_WIDTH, D_MODEL), BETA * MLA_WIDTH ** -0.5)
    p_mem = nrm(ks[22], (DEPTH, X_WIDTH, D_MODEL), BETA * X_WIDTH ** -0.5)
    w_out = nrm(ks[23], (DEPTH, D_MODEL, D_MODEL), BETA * D_MODEL ** -0.5)
    ln_g = 1.0 + nrm(ks[24], (DEPTH, D_MODEL), 0.01)
    ln_b = nrm(ks[25], (DEPTH, D_MODEL), 0.01)
    return {'x': x, 'mem': mem, 'positions': positions, 'w_in': w_in, 'b_gate': b_gate,
            'ssm_a_re': ssm_a_re, 'ssm_a_im': ssm_a_im, 'ssm_log_dt': ssm_log_dt,
            'ssm_b_re': ssm_b_re, 'ssm_b_im': ssm_b_im, 'ssm_c_re': ssm_c_re, 'ssm_c_im': ssm_c_im,
            'ssm_d': ssm_d, 'w_glu': w_glu, 'b_glu': b_glu,
            'mla_q_norm': mla_q_norm, 'w_uq': w_uq, 'mla_kv_norm': mla_kv_norm, 'w_ukv': w_ukv,
            'w_mem_kv': w_mem_kv, 'p_ssm': p_ssm, 'p_mla': p_mla, 'p_mem': p_mem,
            'w_out': w_out, 'ln_g': ln_g, 'ln_b': ln_b}


def reference(x, mem, positions, w_in, b_gate, ssm_a_re, ssm_a_im, ssm_log_dt, ssm_b_re, ssm_b_im,
              ssm_c_re, ssm_c_im, ssm_d, w_glu, b_glu, mla_q_norm, w_uq, mla_kv_norm, w_ukv,
              w_mem_kv, p_ssm, p_mla, p_mem, w_out, ln_g, ln_b):
    bsz, s, d = x.shape
    cos, sin = _rope_tables(positions)
    split_points = np.cumsum(IN_WIDTHS)[:-1].tolist()
    for l in range(DEPTH):
        proj = x @ w_in[l]
        u, z_ssm, c_q, c_kv, k_rope, z_mla, q_mem, z_mem, gate_logits = jnp.split(proj, split_points, axis=-1)
        gates = jax.nn.sigmoid((gate_logits + b_gate[l]).astype(jnp.float32)).astype(x.dtype)
        gates = gates.reshape(bsz, s, N_BRANCH, d)
        y_ssm = _s5_ssm(u, ssm_a_re[l], ssm_a_im[l], ssm_log_dt[l], ssm_b_re[l], ssm_b_im[l],
                        ssm_c_re[l], ssm_c_im[l], ssm_d[l])
        y_ssm = _s5_glu(y_ssm, w_glu[l], b_glu[l]) * jax.nn.silu(z_ssm)
        y_mla = _mla_attention(c_q, c_kv, k_rope, mla_q_norm[l], w_uq[l], mla_kv_norm[l], w_ukv[l], cos, sin)
        y_mla = y_mla * jax.nn.silu(z_mla)
        y_mem = _memory_attention(q_mem, mem, w_mem_kv[l]) * jax.nn.silu(z_mem)
        merged = (gates[:, :, 0] * (y_ssm @ p_ssm[l])
                  + gates[:, :, 1] * (y_mla @ p_mla[l])
                  + gates[:, :, 2] * (y_mem @ p_mem[l]))
        x = _layer_norm(ALPHA * x + merged @ w_out[l], ln_g[l], ln_b[l])
    return x
```

```python
import contextlib
import math
import numpy as np
import ml_dtypes
import concourse.bass as bass
import concourse.mybir as mybir
from concourse.bass_utils import run_bass_kernel_spmd

F32 = mybir.dt.float32
BF16 = mybir.dt.bfloat16
I32 = mybir.dt.int32
AF = mybir.ActivationFunctionType
ALU = mybir.AluOpType
AX = mybir.AxisListType

D = 1024
SEQ = 4096
DEPTH = 4
MEM = 256
TT = 512
ALPHA = (2 * DEPTH) ** 0.25
EPS = 1e-5
MLA_SCALE = 96 ** -0.5
MEM_SCALE = 128 ** -0.5
TWO_PI = 2.0 * math.pi
MAGIC = 12582912.0
CH = 4096
(C_MEMK, C_MEMV, C_UQKV, C_IP0, C_IP1, C_IP2, C_IP3, C_GLU, C_IP4, C_IP5, C_IP6, C_PMEM) = range(12)
C_MG0 = 12
C_WO0 = 20
C_WO1 = 21
NCH = 22


class AV:
    __slots__ = ("ap", "key")

    def __init__(self, ap, key):
        self.ap = ap
        self.key = key


class Tile:
    def __init__(self, t, key):
        self.t = t
        self.key = key

    def __getitem__(self, idx):
        return AV(self.t[idx], self.key)

    def v(self, ap_fn):
        return AV(ap_fn(self.t), self.key)


class Op:
    __slots__ = ("id", "eng", "fn", "deps", "dma", "needs", "sem", "val", "waits")


class Prog:
    NSLOT = 8

    def __init__(self, nc):
        self.nc = nc
        self.ops = []
        self.res_w = {}
        self.res_r = {}
        self.stack = None
        self.ntile = 0

    def sb(self, shape, dt, name=None):
        self.ntile += 1
        name = name or f"t{self.ntile}"
        t = self.stack.enter_context(self.nc.sbuf_tensor("sb_" + name, list(shape), dt))
        return Tile(t, name)

    def ps(self, shape, dt, name):
        t = self.stack.enter_context(self.nc.psum_tensor(name, list(shape), dt))
        return Tile(t, name)

    def add(self, eng, fn, reads=(), writes=(), dma=False):
        op = Op()
        op.id = len(self.ops)
        op.eng = eng
        op.fn = fn
        op.dma = dma
        op.needs = False
        deps = set()
        for r in reads:
            if r in self.res_w:
                deps.add(self.res_w[r])
        for w in writes:
            if w in self.res_w:
                deps.add(self.res_w[w])
            for rr in self.res_r.get(w, ()):
                deps.add(rr)
        for r in reads:
            self.res_r.setdefault(r, []).append(op.id)
        for w in writes:
            self.res_w[w] = op.id
            self.res_r[w] = []
        deps.discard(op.id)
        if eng == "pe":
            deps = {d for d in deps if not (self.ops[d].eng == "pe" and not self.ops[d].dma)}
        op.deps = deps
        self.ops.append(op)
        return op.id

    def mm(self, out, lhsT, rhs, start=True, stop=True):
        return self.add("pe", lambda e: e.matmul(out.ap, lhsT.ap, rhs.ap, start=start, stop=stop),
                        reads=[lhsT.key, rhs.key], writes=[out.key])

    def tr(self, out, in_, ident):
        return self.add("pe", lambda e: e.transpose(out.ap, in_.ap, ident.ap),
                        reads=[in_.key, ident.key], writes=[out.key])

    def act(self, out, in_, func, bias=None, scale=None):
        kw = {}
        rd = [in_.key]
        if bias is not None:
            if isinstance(bias, AV):
                kw["bias"] = bias.ap
                rd.append(bias.key)
            else:
                kw["bias"] = float(bias)
        if scale is not None:
            if isinstance(scale, AV):
                kw["scale"] = scale.ap
                rd.append(scale.key)
            else:
                kw["scale"] = float(scale)
        return self.add("act", lambda e: e.activation(out.ap, in_.ap, func, **kw), reads=rd, writes=[out.key])

    def tt(self, eng, out, in0, in1, op):
        return self.add(eng, lambda e: e.tensor_tensor(out.ap, in0.ap, in1.ap, op),
                        reads=[in0.key, in1.key], writes=[out.key])

    def ts(self, eng, out, in0, s1, op0, s2=None, op1=None):
        rd = [in0.key]
        a1 = s1.ap if isinstance(s1, AV) else float(s1)
        if isinstance(s1, AV):
            rd.append(s1.key)
        a2 = None
        if s2 is not None:
            a2 = s2.ap if isinstance(s2, AV) else float(s2)
            if isinstance(s2, AV):
                rd.append(s2.key)
        if op1 is None:
            return self.add(eng, lambda e: e.tensor_scalar(out.ap, in0.ap, a1, None, op0), reads=rd, writes=[out.key])
        return self.add(eng, lambda e: e.tensor_scalar(out.ap, in0.ap, a1, a2, op0, op1), reads=rd, writes=[out.key])

    def stt(self, eng, out, in0, scalar, in1, op0, op1):
        rd = [in0.key, in1.key]
        a = scalar.ap if isinstance(scalar, AV) else float(scalar)
        if isinstance(scalar, AV):
            rd.append(scalar.key)
        return self.add(eng, lambda e: e.scalar_tensor_tensor(out.ap, in0.ap, a, in1.ap, op0, op1),
                        reads=rd, writes=[out.key])

    def copy(self, eng, out, in_):
        if eng == "act":
            return self.act(out, in_, AF.Copy)
        return self.add(eng, lambda e: e.tensor_copy(out.ap, in_.ap), reads=[in_.key], writes=[out.key])

    def memset(self, eng, out, val):
        return self.add(eng, lambda e: e.memset(out.ap, val), reads=[], writes=[out.key])

    def reduce(self, eng, out, in_, op):
        return self.add(eng, lambda e: e.tensor_reduce(out.ap, in_.ap, AX.X, op), reads=[in_.key], writes=[out.key])

    def recip(self, out, in_):
        return self.add("dve", lambda e: e.reciprocal(out.ap, in_.ap), reads=[in_.key], writes=[out.key])

    def dma(self, q, out, in_, extra_reads=(), extra_writes=()):
        return self.add(q, lambda e: e.dma_start(out=out.ap, in_=in_.ap),
                        reads=[in_.key] + list(extra_reads), writes=[out.key] + list(extra_writes), dma=True)

    def emit(self):
        nc = self.nc
        ops = self.ops
        for op in ops:
            for d in op.deps:
                ops[d].needs = True
        engs = []
        for op in ops:
            if op.eng not in engs:
                engs.append(op.eng)
        sems = {e: self.stack.enter_context(nc.semaphore(f"s_{e}")) for e in engs}
        dsem = {}
        for e in engs:
            if any(o.dma and o.eng == e for o in ops):
                dsem[e] = [self.stack.enter_context(nc.semaphore(f"d_{e}{i}")) for i in range(self.NSLOT)]
        cnt = {e: 0 for e in engs}
        dcnt = {e: [0] * self.NSLOT for e in engs}
        dn = {e: 0 for e in engs}
        for op in ops:
            op.waits = []
            if op.dma:
                slot = dn[op.eng] % self.NSLOT
                dn[op.eng] += 1
                if dcnt[op.eng][slot] > 0:
                    op.waits.append((dsem[op.eng][slot], dcnt[op.eng][slot] * 16))
                dcnt[op.eng][slot] += 1
                op.sem = dsem[op.eng][slot]
                op.val = dcnt[op.eng][slot] * 16
            elif op.needs:
                cnt[op.eng] += 1
                op.sem = sems[op.eng]
                op.val = cnt[op.eng]
            else:
                op.sem = None
                op.val = None
        known = {e: {} for e in engs}
        for op in ops:
            k = known[op.eng]
            cand = list(op.waits) + [(ops[d].sem, ops[d].val) for d in sorted(op.deps)]
            best = {}
            for s, v in cand:
                if v > best.get(id(s), (None, 0))[1]:
                    best[id(s)] = (s, v)
            ws = []
            for sid, (s, v) in best.items():
                if k.get(sid, 0) >= v:
                    continue
                k[sid] = v
                ws.append((s, v))
            op.waits = ws
        final_waits = {}
        for e in engs:
            if e in dsem:
                final_waits[e] = [(dsem[e][i], dcnt[e][i] * 16) for i in range(self.NSLOT) if dcnt[e][i] > 0]
        by_eng = {e: [o for o in ops if o.eng == e] for e in engs}
        self.stats = {e: len(v) for e, v in by_eng.items()}
        self.nwaits = sum(len(o.waits) for o in ops)
        attr = {"pe": "tensor", "act": "scalar", "dve": "vector", "pool": "gpsimd", "sp": "sync"}
        with nc.Block() as block:
            for e in engs:
                def body(eng, _ops=by_eng[e], _e=e):
                    for o in _ops:
                        for s, v in o.waits:
                            eng.wait_ge(s, v)
                        ins = o.fn(eng)
                        if o.sem is not None:
                            ins.then_inc(o.sem, 16 if o.dma else 1)
                    for s, v in final_waits.get(_e, ()):
                        eng.wait_ge(s, v)
                getattr(block, attr[e])(body)


def _img(w_rows_cols):
    K, N = w_rows_cols.shape
    return w_rows_cols.reshape(K // 128, 128, N).transpose(1, 0, 2).reshape(128, -1)


def host_weight_image(inp, l):
    w_in = inp["w_in"][l]
    o = np.cumsum([0, 512, 512, 256, 128, 32, 512, 512, 512, 3072])
    u, zs, cq, ckv, kr, zmla, qm, zm, gl = [w_in[:, o[i]:o[i + 1]] for i in range(9)]
    z128 = np.zeros((1024, 128), np.float32)
    img = np.zeros((NCH, 128, CH), np.float32)

    def put_tiles(c, tiles):
        buf = np.zeros((1024, 512), np.float32)
        for i, t in enumerate(tiles):
            buf[:, i * 128:(i + 1) * 128] = t
        img[c] = _img(buf)

    rope = z128.copy()
    rope[:, 64:96] = kr
    rope_sw = z128.copy()
    rope_sw[:, 64:80] = kr[:, 16:32]
    rope_sw[:, 80:96] = kr[:, 0:16]
    put_tiles(C_IP0, [ckv, rope, rope_sw])
    put_tiles(C_IP1, [cq[:, 0:128], cq[:, 128:256]])
    put_tiles(C_IP2, [zmla[:, i * 128:(i + 1) * 128] for i in range(4)])
    put_tiles(C_IP3, [u[:, i * 128:(i + 1) * 128] for i in range(4)])
    put_tiles(C_IP4, [zs[:, i * 128:(i + 1) * 128] for i in range(4)])
    put_tiles(C_IP5, [qm[:, i * 128:(i + 1) * 128] for i in range(4)])
    put_tiles(C_IP6, [zm[:, i * 128:(i + 1) * 128] for i in range(4)])
    wm = inp["w_mem_kv"][l]
    img[C_MEMK] = _img(wm[:, 0:512])
    img[C_MEMV] = _img(wm[:, 512:1024])
    wukv = inp["w_ukv"][l].reshape(128, 8, 128)
    wk = wukv[:, :, 0:64].reshape(128, 512)
    wv = wukv[:, :, 64:128].reshape(128, 512)
    wuq = inp["w_uq"][l].reshape(256, 8, 96)
    wuq_sw = np.zeros((256, 8, 96), np.float32)
    wuq_sw[:, :, 64:80] = wuq[:, :, 80:96]
    wuq_sw[:, :, 80:96] = wuq[:, :, 64:80]
    img[C_UQKV] = np.concatenate([wk, wv, _img(wuq.reshape(256, 768)), _img(wuq_sw.reshape(256, 768))], axis=1)
    img[C_GLU] = _img(inp["w_glu"][l])
    img[C_PMEM] = _img(inp["p_mem"][l])
    ps_, pm_ = inp["p_ssm"][l], inp["p_mla"][l]
    for j in range(8):
        parts = [_img(gl[:, k * 1024 + j * 128: k * 1024 + (j + 1) * 128]) for k in range(3)]
        parts.append(_img(ps_[:, j * 128:(j + 1) * 128]))
        parts.append(_img(pm_[:, j * 128:(j + 1) * 128]))
        img[C_MG0 + j] = np.concatenate(parts, axis=1)
    wo = inp["w_out"][l]
    img[C_WO0] = _img(wo[:, 0:512])
    img[C_WO1] = _img(wo[:, 512:1024])
    return img


def host_small(inp, l):
    d = {}
    d["bgate"] = inp["b_gate"][l].reshape(3, 8, 128).transpose(2, 0, 1).reshape(128, 24)
    d["bglu"] = inp["b_glu"][l].reshape(8, 128).T
    d["qn"] = inp["mla_q_norm"][l].reshape(2, 128).T
    d["kvn"] = inp["mla_kv_norm"][l].reshape(1, 128).T
    sm = np.concatenate([d["bgate"], d["bglu"], d["qn"], d["kvn"]], axis=1)

    def pp(a):
        sh = a.shape
        return a.reshape((16, 2, 64) + sh[2:]).transpose((1, 2, 0) + tuple(range(3, len(sh) + 1))).reshape((128, 16) + sh[2:])

    are = pp(inp["ssm_a_re"][l])
    aim = pp(inp["ssm_a_im"][l])
    ldt = pp(np.broadcast_to(inp["ssm_log_dt"][l][:, None], (32, 64)))
    bre = pp(inp["ssm_b_re"][l])
    bim = pp(inp["ssm_b_im"][l])
    cre = pp(inp["ssm_c_re"][l].transpose(0, 2, 1))
    cim = pp(inp["ssm_c_im"][l].transpose(0, 2, 1))
    ssm = np.concatenate([are, aim, ldt, bre.reshape(128, 256), bim.reshape(128, 256),
                          cre.reshape(128, 256), cim.reshape(128, 256)], axis=1)
    dtab = np.broadcast_to(inp["ssm_d"][l].reshape(32, 1, 16), (32, 8, 16)).transpose(1, 2, 0).reshape(128, 32)
    small = np.concatenate([sm, ssm, dtab], axis=1).astype(np.float32)
    ln = np.stack([inp["ln_g"][l], inp["ln_b"][l]], axis=0).astype(np.float32)
    return small, ln


NSMALL = 35 + 1072 + 32


def host_consts():
    c = {}
    c["ident"] = np.eye(128, dtype=np.float32)
    big = np.zeros((128, 8, 240), np.float32)
    for g in range(8):
        for cc in range(16):
            big[g * 16 + cc, g, cc + 112] = 1.0
    c["big"] = big.reshape(128, 8 * 240).astype(ml_dtypes.bfloat16)
    mask = np.zeros((128, 4, 512), np.float32)
    k = np.arange(128)[:, None]
    q = np.arange(512)[None, :]
    for v in range(4):
        mask[:, v, :] = (q >= 128 * v + k)
    c["cmask"] = mask.reshape(128, 2048).astype(ml_dtypes.bfloat16)
    r = np.arange(128)
    tm = ((r[None, :] // 16) >= (r[:, None] // 16)).astype(np.float32)
    c["tmask"] = tm
    invf = np.zeros((128, 4), np.float32)
    fr = (10000.0 ** (-np.arange(0, 32, 2, dtype=np.float32) / 32)).astype(np.float32)
    for rr in range(32):
        invf[64 + rr, 0] = fr[rr % 16]
        invf[64 + rr, 1] = -1.0 if rr < 16 else 1.0
    c["invf"] = invf
    return c


def build(S, NL, dbg=False):
    NT = S // TT
    nc = bass.Bass("TRN2", target_bir_lowering=False)
    P = Prog(nc)

    def din(name, shape, dt=F32):
        return nc.dram_tensor(name, list(shape), dt, kind="ExternalInput").ap()

    x_in = din("x", [S, D])
    mem_in = din("mem", [MEM, D])
    pos_in = din("pos", [1, S], I32)
    wimg = din("wimg", [NL * NCH, 128, CH])
    small_in = din("small", [NL, 128, NSMALL])
    ln_in = din("lnp", [NL, 2, D])
    ident_in = din("ident", [128, 128])
    big_in = din("big", [128, 8 * 240], BF16)
    cmask_in = din("cmask", [128, 2048], BF16)
    tmask_in = din("tmask", [128, 128])
    invf_in = din("invf", [128, 4])
    y_out = nc.dram_tensor("y", [S, D], F32, kind="ExternalOutput").ap()

    def dscr(name, shape, dt):
        return nc.dram_tensor(name, list(shape), dt, kind="Internal").ap()

    wbf = dscr("wbf", [NL * NCH, 128, CH], BF16)
    ssmw = dscr("ssmw", [NL * 4, 128, CH], BF16)
    Kc = dscr("Kc", [8, 96, S], BF16)
    Vc = dscr("Vc", [S // 128, 128, 512], BF16)
    xa = dscr("xa", [S, D], F32)
    xb = dscr("xb", [S, D], F32)
    csd = dscr("csd", [2, 32, S], F32)
    dbg_t = {}
    if dbg:
        for nm in ("yssm", "ymla", "ymem", "mT2"):
            dbg_t[nm] = nc.dram_tensor("d_" + nm, [128, 4 * TT], BF16, kind="ExternalOutput").ap()
        dbg_t["gel"] = nc.dram_tensor("d_gel", [128, 4 * TT], BF16, kind="ExternalOutput").ap()

    with contextlib.ExitStack() as st:
        P.stack = st
        bank = [P.ps([128, 512], F32, f"bank{i}") for i in range(8)]
        rot = {"ab": 0, "s": 0, "nd": 0}

        def bank_ab():
            rot["ab"] ^= 1
            return bank[rot["ab"]]

        def bank_s():
            rot["s"] ^= 1
            return bank[2 + rot["s"]]

        def bank_nd():
            rot["nd"] ^= 1
            return bank[4 + 2 * rot["nd"]], bank[5 + 2 * rot["nd"]]

        ident = P.sb([128, 128], F32, "ident")
        big = P.sb([128, 8, 240], BF16, "big")
        cmask = P.sb([128, 4, 512], BF16, "cmask")
        tmask = P.sb([128, 128], F32, "tmask")
        invf = P.sb([128, 4], F32, "invf")
        ones_bf = P.sb([128, 128], BF16, "ones_bf")
        onesk = P.sb([128, 128], F32, "onesk")
        onesq = P.sb([128, 128], F32, "onesq")
        P.dma("sp", ident[:], AV(ident_in, "c_ident"))
        P.dma("sp", big.v(lambda t: t[:].rearrange("p a b -> p (a b)")), AV(big_in, "c_big"))
        P.dma("sp", cmask.v(lambda t: t[:].rearrange("p a b -> p (a b)")), AV(cmask_in, "c_cmask"))
        P.dma("sp", tmask[:], AV(tmask_in, "c_tmask"))
        P.dma("sp", invf[:], AV(invf_in, "c_invf"))
        P.memset("dve", ones_bf[:], 1.0)
        P.memset("dve", onesk[:], 1.0 / 128)
        P.memset("dve", onesq[:], 1.0 / 256)

        for c in range(NL * NCH):
            P.dma("pool", AV(wbf[c], ("wbf", c)), AV(wimg[c], "wimg"))

        NB = 3
        wring = [P.sb([128, CH], BF16, f"wring{i}") for i in range(NB)]
        stream = []
        for l in range(NL):
            stream += [("w", l, C_MEMK), ("w", l, C_MEMV)]
            for i in range(NT):
                stream += [("w", l, C_IP0), ("w", l, C_IP1), ("w", l, C_IP2), ("w", l, C_IP3),
                           ("s", l, 0), ("s", l, 1), ("s", l, 2), ("s", l, 3),
                           ("w", l, C_GLU), ("w", l, C_IP4), ("w", l, C_IP5), ("w", l, C_IP6)]
                stream += [("w", l, C_MG0 + j) for j in range(8)]
                stream += [("w", l, C_WO0), ("w", l, C_WO1)]
        wst = {"issued": 0, "taken": 0}

        def w_issue():
            n = wst["issued"]
            kind, l, c = stream[n]
            if kind == "w":
                src = AV(wbf[l * NCH + c], ("wbf", l * NCH + c))
            else:
                src = AV(ssmw[l * 4 + c], ("ssmw", l * 4 + c))
            P.dma("sp", wring[n % NB][:], src)
            wst["issued"] += 1

        def w_next(kind, l, c, keep=0):
            n = wst["taken"]
            assert stream[n] == (kind, l, c), (stream[n], kind, l, c)
            while wst["issued"] < min(n + NB - keep, len(stream)):
                w_issue()
            wst["taken"] += 1
            return wring[n % NB]

        small = P.sb([128, NSMALL], F32, "small")
        lng = P.sb([128, D], F32, "lng")
        lnb = P.sb([128, D], F32, "lnb")
        uqkv = P.sb([128, CH], BF16, "uqkv")
        memT = P.sb([128, 8, MEM], BF16, "memT")
        memKT = P.sb([128, 4, MEM], BF16, "memKT")
        memV = P.sb([128, 2, 512], BF16, "memV")
        xs = [P.sb([128, D], F32, f"xs{i}") for i in range(2)]
        xT = P.sb([128, 8, TT], BF16, "xT")
        f32t = [P.sb([128, TT], F32, f"f32t{i}") for i in range(5)]
        fi = {"i": 0}

        def ftmp():
            fi["i"] = (fi["i"] + 1) % len(f32t)
            return f32t[fi["i"]]

        cosT = P.sb([128, TT], F32, "cosT")
        sinS = P.sb([128, TT], F32, "sinS")
        ckvn = P.sb([128, TT], BF16, "ckvn")
        cqn = P.sb([128, 2, TT], BF16, "cqn")
        krb = P.sb([128, TT], BF16, "krb")
        Qh = [P.sb([128, TT], BF16, f"Qh{i}") for i in range(2)]
        Kst = Qh
        Kbuf = [P.sb([128, 4096], BF16, f"Kbuf{i}") for i in range(2)]
        Vbuf = [P.sb([128, 32, 128], BF16, f"Vbuf{i}") for i in range(1)]
        PT = [P.sb([128, TT], BF16, f"PT{i}") for i in range(3)]
        pti = {"i": 0}

        def ptnext():
            pti["i"] = (pti["i"] + 1) % 3
            return PT[pti["i"]]

        ymla = P.sb([128, 4, TT], BF16, "ymla")
        yssm = P.sb([128, 4, TT], BF16, "yssm")
        ymem = P.sb([128, 4, TT], BF16, "ymem")
        UTb = P.sb([128, 4, TT], BF16, "UTb")
        gel = UTb
        Vg = P.sb([128, 32, 64], BF16, "Vg")
        Yg = Vg
        Vst = Vg.v(lambda t: t[:].rearrange("p a b -> p (a b)").rearrange("p (s c) -> p s c", s=4))
        Hre = P.sb([128, 16, 64], F32, "Hre")
        Him = P.sb([128, 16, 64], F32, "Him")
        hpre = P.sb([128, 16, 64], BF16, "hpre")
        hpim = P.sb([128, 16, 64], BF16, "hpim")
        carry = P.sb([128, 2, 16], F32, "carry")
        a8pw = P.sb([128, 6, 2, 16], F32, "a8pw")
        qmb = P.sb([128, TT], BF16, "qmb")
        szm = P.sb([128, TT], F32, "szm")
        mT = P.sb([128, 8, TT], BF16, "mT")
        lnbuf = P.sb([128, 4, D], F32, "lnbuf")
        vln = lnbuf.v(lambda t: t[:, 0, :])
        junk = lnbuf.v(lambda t: t[:, 1, :])
        oln = lnbuf.v(lambda t: t[:, 2, :])
        st1 = lnbuf.v(lambda t: t[:, 0, :].rearrange("p (a k) -> p a k", k=64))
        st2 = lnbuf.v(lambda t: t[:, 1, :].rearrange("p (a k) -> p a k", k=64))
        st3 = lnbuf.v(lambda t: t[:, 2, :].rearrange("p (a k) -> p a k", k=64))
        stat = P.sb([128, 8], F32, "stat")
        g_t = [P.sb([128, 16, 16], F32, f"g_t{i}") for i in range(10)]
        pw = P.sb([128, 2, 16, 16], F32, "pw")
        pwd = P.sb([128, 2, 8, 16], F32, "pwd")
        planeL = [lnbuf.v(lambda t, r=r: t[:, 2 * r:2 * r + 2, :].rearrange("p a (b c) -> p (a b) c", c=128)) for r in range(2)]
        planeK = [Kbuf[r].v(lambda t: t[:].bitcast(F32).rearrange("p (a c) -> p a c", c=128)) for r in range(2)]
        planeX = xT.v(lambda t: t[:].rearrange("p a b -> p (a b)").bitcast(F32).rearrange("p (a c) -> p a c", c=128))
        planeM = mT.v(lambda t: t[:].rearrange("p a b -> p (a b)").bitcast(F32).rearrange("p (a c) -> p a c", c=128))
        pmem = P.sb([128, CH], BF16, "pmem")
        simg = Tile(None, Vbuf[0].key)
        simg.t = Vbuf[0].t[:].rearrange("p a b -> p (a b)")

        Dx = [x_in, xa, xb]

        def sincos(o_sin, o_cos, ang, ta, tb, tc, td):
            C1 = 6.28125
            C2 = TWO_PI - 6.28125
            P.ts("dve", ta, ang, 1.0 / TWO_PI, ALU.mult, MAGIC, ALU.add)
            P.ts("dve", ta, ta, -MAGIC, ALU.add)
            P.stt("dve", tb, ta, -C1, ang, ALU.mult, ALU.add)
            P.stt("dve", tb, ta, -C2, tb, ALU.mult, ALU.add)
            P.ts("dve", tb, tb, 0.125, ALU.mult)
            P.tt("dve", tc, tb, tb, ALU.mult)
            a = [-1.0 / 6, 1.0 / 120, -1.0 / 5040, 1.0 / 362880]
            b = [-0.5, 1.0 / 24, -1.0 / 720, 1.0 / 40320]
            P.ts("dve", td, tc, a[3], ALU.mult)
            for cf in (a[2], a[1], a[0]):
                P.stt("dve", td, td, cf, tc, ALU.add, ALU.mult)
            P.stt("dve", o_sin, td, 1.0, tb, ALU.add, ALU.mult)
            P.ts("dve", td, tc, b[3], ALU.mult)
            for cf in (b[2], b[1], b[0]):
                P.stt("dve", td, td, cf, tc, ALU.add, ALU.mult)
            P.ts("dve", o_cos, td, 1.0, ALU.add)
            for _ in range(3):
                P.tt("dve", ta, o_sin, o_sin, ALU.mult)
                P.tt("dve", tb, o_cos, o_cos, ALU.mult)
                P.stt("dve", tc, o_sin, 2.0, o_cos, ALU.mult, ALU.mult)
                P.tt("dve", o_cos, tb, ta, ALU.subtract)
                P.copy("dve", o_sin, tc)

        def cmul(o_re, o_im, a_re, a_im, b_re, b_im, t1, t2):
            P.tt("dve", t1, a_re, b_re, ALU.mult)
            P.tt("dve", t2, a_im, b_im, ALU.mult)
            P.tt("dve", o_re, t1, t2, ALU.subtract)
            P.tt("dve", t1, a_re, b_im, ALU.mult)
            P.tt("dve", t2, a_im, b_re, ALU.mult)
            P.tt("dve", o_im, t1, t2, ALU.add)

        def rms_rstd(dst, ms_bank):
            P.ts("dve", dst, ms_bank, EPS, ALU.add)
            P.act(dst, dst, AF.Sqrt)
            P.recip(dst, dst)

        def silu_from_bank(bk):
            sg = ftmp()
            P.act(sg[:], bk[:], AF.Sigmoid)
            P.tt("dve", sg[:], sg[:], bk[:], ALU.mult)
            return sg

        def inproj_tile(wt, ti):
            bk = bank_ab()
            for kc in range(8):
                P.mm(bk[:], wt.v(lambda t, kc=kc, ti=ti: t[:, kc * 512 + ti * 128: kc * 512 + (ti + 1) * 128]),
                     xT.v(lambda t, kc=kc: t[:, kc, :]), start=(kc == 0), stop=(kc == 7))
            return bk

        for i in range(NT):
            pi32 = f32t[0].v(lambda t: t[64:96, :].bitcast(I32))
            src = AV(bass.AP(pos_in.tensor, i * TT, [[0, 32], [1, TT]]), "pos")
            P.dma("pool", pi32, src)
            angt = f32t[1]
            P.copy("dve", angt[64:96, :], pi32)
            P.ts("dve", angt[64:96, :], angt[64:96, :], invf[64:96, 0:1], ALU.mult)
            sincos(sinS[64:96, :], cosT[64:96, :], angt[64:96, :], f32t[2][64:96, :], f32t[3][64:96, :], f32t[4][64:96, :], f32t[0][64:96, :])
            P.ts("dve", sinS[64:96, :], sinS[64:96, :], invf[64:96, 1:2], ALU.mult)
            P.dma("pool", AV(csd[0, :, i * TT:(i + 1) * TT], ("csd", i)), cosT[64:96, :])
            P.dma("pool", AV(csd[1, :, i * TT:(i + 1) * TT], ("csd", i)), sinS[64:96, :])
        for mb in range(2):
            P.dma("pool", xs[mb][:], AV(mem_in[mb * 128:(mb + 1) * 128, :], "mem"))
            for half in range(2):
                bk = bank_ab()
                for q in range(4):
                    kc = half * 4 + q
                    P.tr(bk[:, q * 128:(q + 1) * 128], xs[mb][:, kc * 128:(kc + 1) * 128], ident[:])
                P.copy("dve", memT.v(lambda t, half=half, mb=mb: t[:, half * 4:half * 4 + 4, mb * 128:(mb + 1) * 128]),
                       bk.v(lambda t: t[:].rearrange("p (a b) -> p a b", a=4)))

        for l in range(NL):
            xin = Dx[0] if l == 0 else Dx[1 + ((l - 1) % 2)]
            xout = y_out if l == NL - 1 else Dx[1 + (l % 2)]
            kin = "x0" if l == 0 else ("xs", (l - 1) % 2)
            kout = "y" if l == NL - 1 else ("xs", l % 2)

            P.dma("pool", small[:], AV(small_in[l], "small_in"))
            P.dma("pool", lng[:], AV(bass.AP(ln_in.tensor, (l * 2) * D, [[0, 128], [1, D]]), "ln_in"))
            P.dma("pool", lnb[:], AV(bass.AP(ln_in.tensor, (l * 2 + 1) * D, [[0, 128], [1, D]]), "ln_in"))
            P.dma("pool", uqkv[:], AV(wbf[l * NCH + C_UQKV], ("wbf", l * NCH + C_UQKV)))
            P.dma("pool", pmem[:], AV(wbf[l * NCH + C_PMEM], ("wbf", l * NCH + C_PMEM)))
            bgate = lambda k, j: small[:, k * 8 + j: k * 8 + j + 1]
            bglu = lambda n: small[:, 24 + n: 24 + n + 1]
            qn = lambda kc: small[:, 32 + kc: 33 + kc]
            kvn = small[:, 34:35]
            o0 = 35
            s_are = small[:, o0:o0 + 16]
            s_aim = small[:, o0 + 16:o0 + 32]
            s_ldt = small[:, o0 + 32:o0 + 48]
            s_bre = small.v(lambda t: t[:, o0 + 48:o0 + 304].rearrange("p (a c) -> p a c", c=16))
            s_bim = small.v(lambda t: t[:, o0 + 304:o0 + 560].rearrange("p (a c) -> p a c", c=16))
            s_cre = small.v(lambda t: t[:, o0 + 560:o0 + 816].rearrange("p (a c) -> p a c", c=16))
            s_cim = small.v(lambda t: t[:, o0 + 816:o0 + 1072].rearrange("p (a c) -> p a c", c=16))
            dtab = lambda g: small[:, o0 + 1072 + g: o0 + 1073 + g]

            wk = w_next("w", l, C_MEMK)
            for h in range(4):
                bk = bank_ab()
                for kc in range(8):
                    P.mm(bk[:, 0:MEM], wk.v(lambda t, kc=kc, h=h: t[:, kc * 512 + h * 128: kc * 512 + (h + 1) * 128]),
                         memT.v(lambda t, kc=kc: t[:, kc, :]), start=(kc == 0), stop=(kc == 7))
                P.copy("act", memKT.v(lambda t, h=h: t[:, h, :]), bk[:, 0:MEM])
            wv = w_next("w", l, C_MEMV)
            for mb in range(2):
                bk = bank_ab()
                for kc in range(8):
                    P.mm(bk[:], memT.v(lambda t, kc=kc, mb=mb: t[:, kc, mb * 128:(mb + 1) * 128]),
                         wv.v(lambda t, kc=kc: t[:, kc * 512:(kc + 1) * 512]), start=(kc == 0), stop=(kc == 7))
                P.copy("act", memV.v(lambda t, mb=mb: t[:, mb, :]), bk[:])

            T_ = [g_t[i].v(lambda t: t[:, :, 0]) for i in range(10)]
            dt_, lrdt, ang, mag, lbre, lbim, tA, tB, tC, tD = T_
            P.act(dt_, s_ldt, AF.Exp)
            P.tt("dve", lrdt, s_are, dt_, ALU.mult)
            P.tt("dve", ang, s_aim, dt_, ALU.mult)
            P.act(mag, lrdt, AF.Exp)
            sincos(tA, tB, ang, tC, tD, g_t[8].v(lambda t: t[:, :, 2]), g_t[9].v(lambda t: t[:, :, 2]))
            P.tt("dve", lbre, mag, tB, ALU.mult)
            P.tt("dve", lbim, mag, tA, ALU.mult)
            T2 = [g_t[i].v(lambda t: t[:, :, 1]) for i in range(10)]
            nr, den, fre, fim, u1, u2, ivre, ivim, m2, u3 = T2
            P.ts("dve", nr, lbre, -1.0, ALU.add)
            P.tt("dve", u1, s_are, s_are, ALU.mult)
            P.tt("dve", u2, s_aim, s_aim, ALU.mult)
            P.tt("dve", den, u1, u2, ALU.add)
            P.recip(den, den)
            P.tt("dve", u1, nr, s_are, ALU.mult)
            P.tt("dve", u2, lbim, s_aim, ALU.mult)
            P.tt("dve", u1, u1, u2, ALU.add)
            P.tt("dve", fre, u1, den, ALU.mult)
            P.tt("dve", u1, lbim, s_are, ALU.mult)
            P.tt("dve", u2, nr, s_aim, ALU.mult)
            P.tt("dve", u1, u1, u2, ALU.subtract)
            P.tt("dve", fim, u1, den, ALU.mult)
            P.tt("dve", u1, lbre, lbre, ALU.mult)
            P.tt("dve", u2, lbim, lbim, ALU.mult)
            P.tt("dve", m2, u1, u2, ALU.add)
            P.recip(m2, m2)
            P.tt("dve", ivre, lbre, m2, ALU.mult)
            P.tt("dve", u3, lbim, m2, ALU.mult)
            P.ts("dve", ivim, u3, -1.0, ALU.mult)
            pwv = lambda ri, n: pw.v(lambda t, ri=ri, n=n: t[:, ri, n + 7, :])
            P.memset("dve", pwv(0, 0), 1.0)
            P.memset("dve", pwv(1, 0), 0.0)
            for n in range(1, 9):
                cmul(pwv(0, n), pwv(1, n), pwv(0, n - 1), pwv(1, n - 1), lbre, lbim, u1, u2)
            for n in range(-1, -8, -1):
                cmul(pwv(0, n), pwv(1, n), pwv(0, n + 1), pwv(1, n + 1), ivre, ivim, u1, u2)
            for j in range(8):
                for ri in range(2):
                    P.copy("dve", pwd.v(lambda t, ri=ri, j=j: t[:, ri, j, :]), pwv(ri, 7 - j))
            a8 = lambda j, ri: a8pw.v(lambda t, j=j, ri=ri: t[:, j, ri, :])
            P.copy("dve", a8(0, 0), pwv(0, 8))
            P.copy("dve", a8(0, 1), pwv(1, 8))
            for j in range(1, 6):
                cmul(a8(j, 0), a8(j, 1), a8(j - 1, 0), a8(j - 1, 1), a8(j - 1, 0), a8(j - 1, 1), u1, u2)
            bbre, bbim, w1, w2 = g_t[6], g_t[7], g_t[8], g_t[9]
            bc3 = lambda av: AV(av.ap.unsqueeze(2).to_broadcast([128, 16, 16]), av.key)
            P.tt("dve", w1[:], s_bre, bc3(fre), ALU.mult)
            P.tt("dve", w2[:], s_bim, bc3(fim), ALU.mult)
            P.tt("dve", bbre[:], w1[:], w2[:], ALU.subtract)
            P.tt("dve", w1[:], s_bim, bc3(fre), ALU.mult)
            P.tt("dve", w2[:], s_bre, bc3(fim), ALU.mult)
            P.tt("dve", bbim[:], w1[:], w2[:], ALU.add)

            def v4(plane):
                return AV(plane.ap.rearrange("p a (j c) -> p a j c", c=16), plane.key)

            def pwb(tl, ri, lo):
                return tl.v(lambda t, ri=ri, lo=lo: t[:, ri, lo:lo + 8, :].rearrange("p n a -> p a n").unsqueeze(3).to_broadcast([128, 16, 8, 16]))

            def cb(av):
                return AV(av.ap.unsqueeze(2).to_broadcast([128, 16, 8, 16]), av.key)

            def cgen(dst, pt, lo, cr, ci, neg_im):
                ta, tb = v4(planeX), v4(planeM)
                P.tt("dve", ta, pwb(pt, 0, lo), cb(cr), ALU.mult)
                P.tt("dve", tb, pwb(pt, 1, lo), cb(ci), ALU.mult)
                P.tt("dve", v4(dst[0]), ta, tb, ALU.subtract)
                P.tt("dve", ta, pwb(pt, 0, lo), cb(ci), ALU.mult)
                P.tt("dve", tb, pwb(pt, 1, lo), cb(cr), ALU.mult)
                P.tt("dve", v4(dst[1]), ta, tb, ALU.add)
                if neg_im:
                    P.ts("dve", v4(dst[1]), v4(dst[1]), -1.0, ALU.mult)

            cgen(planeK, pw, 8, s_cre, s_cim, True)
            for ri in range(2):
                P.copy("dve", simg.v(lambda t, ri=ri: t[:, ri * 2048:(ri + 1) * 2048].rearrange("p (a c) -> p a c", c=128)), planeK[ri])
            P.dma("pool", AV(ssmw[l * 4 + 3], ("ssmw", l * 4 + 3)), simg[:])
            cgen(planeL, pwd, 0, bbre[:], bbim[:], False)
            cgen(planeK, pw, 0, s_cre, s_cim, True)
            for g in range(32):
                a, par = g // 2, g % 2
                bk = bank_ab()
                rs = slice(par * 64, (par + 1) * 64)
                P.mm(bk[:, 0:128], AV(planeL[0].ap[rs, a, :], planeL[0].key), AV(planeK[0].ap[rs, a, :], planeK[0].key), start=True, stop=False)
                P.mm(bk[:, 0:128], AV(planeL[1].ap[rs, a, :], planeL[1].key), AV(planeK[1].ap[rs, a, :], planeK[1].key), start=False, stop=True)
                tt_ = ftmp()
                P.tt("dve", tt_[:, 0:128], bk[:, 0:128], tmask[:], ALU.mult)
                P.stt("dve", simg[:, g * 128:(g + 1) * 128], ident[:], dtab(g), tt_[:, 0:128], ALU.mult, ALU.add)
            P.dma("pool", AV(ssmw[l * 4 + 2], ("ssmw", l * 4 + 2)), simg[:])
            for half in range(2):
                P.memset("dve", simg[:], 0.0)
                for gl_ in range(16):
                    g = half * 16 + gl_
                    a, par = g // 2, g % 2
                    rs = slice(par * 64, (par + 1) * 64)
                    bk = bank_ab()
                    for ri in range(2):
                        P.tr(bk[:, ri * 64:(ri + 1) * 64], AV(planeL[ri].ap[rs, a, :], planeL[ri].key),
                             AV(ident.t[rs, rs], ident.key))
                    P.copy("dve", simg.v(lambda t, gl_=gl_, par=par: t[:, gl_ * 256:(gl_ + 1) * 256].rearrange("p (r c) -> p r c", r=2)[:, :, par * 64:(par + 1) * 64]),
                           bk.v(lambda t: t[:, 0:128].rearrange("p (r c) -> p r c", r=2)))
                P.dma("pool", AV(ssmw[l * 4 + half], ("ssmw", l * 4 + half)), simg[:])
            P.memset("dve", carry[:], 0.0)

            for i in range(NT):
                t0 = i * TT
                nblk = 4 * (i + 1)
                for sub in range(4):
                    xsub = xs[sub % 2]
                    P.dma("pool", xsub[:], AV(xin[t0 + sub * 128: t0 + (sub + 1) * 128, :], (kin, i)))
                    for half in range(2):
                        bk = bank_ab()
                        for q in range(4):
                            kc = half * 4 + q
                            P.tr(bk[:, q * 128:(q + 1) * 128], xsub[:, kc * 128:(kc + 1) * 128], ident[:])
                        P.copy("act" if half else "dve",
                               xT.v(lambda t, half=half, sub=sub: t[:, half * 4:half * 4 + 4, sub * 128:(sub + 1) * 128]),
                               bk.v(lambda t: t[:].rearrange("p (a b) -> p a b", a=4)))
                P.dma("pool", cosT[64:96, :], AV(csd[0, :, t0:t0 + TT], ("csd", i)))
                P.dma("pool", sinS[64:96, :], AV(csd[1, :, t0:t0 + TT], ("csd", i)))

                w0 = w_next("w", l, C_IP0)
                bk = inproj_tile(w0, 0)
                c32 = ftmp()
                P.copy("act", c32[:], bk[:])
                sq = ftmp()
                P.act(sq[:], bk[:], AF.Square)
                bk2 = bank_ab()
                P.mm(bk2[:], onesk[:], sq[:])
                rstd = ftmp()
                rms_rstd(rstd[:], bk2[:])
                P.stt("dve", ckvn[:], c32[:], kvn, rstd[:], ALU.mult, ALU.mult)
                bk = inproj_tile(w0, 1)
                bk2 = inproj_tile(w0, 2)
                ta = ftmp()
                tb = ftmp()
                P.tt("dve", ta[64:96, :], bk[64:96, :], cosT[64:96, :], ALU.mult)
                P.tt("dve", tb[64:96, :], bk2[64:96, :], sinS[64:96, :], ALU.mult)
                P.tt("dve", krb[64:96, :], ta[64:96, :], tb[64:96, :], ALU.add)
                for h in range(8):
                    bk = bank_ab()
                    P.mm(bk[0:64, :], uqkv[:, h * 64:(h + 1) * 64], ckvn[:])
                    ks = Kst[h % 2]
                    P.copy("act", ks[0:64, :], bk[0:64, :])
                    P.copy("dve", ks[64:96, :], krb[64:96, :])
                    P.dma("pool", AV(Kc[h, :, t0:t0 + TT], ("Kc", h, i)), ks[0:96, :])
                for sub in range(4):
                    bk = bank_ab()
                    P.mm(bk[:], ckvn[:, sub * 128:(sub + 1) * 128], uqkv[:, 512:1024])
                    P.copy("act" if sub % 2 else "dve", AV(Vst.ap[:, sub, :], Vst.key), bk[:])
                P.dma("pool", AV(Vc[4 * i:4 * i + 4].rearrange("b p c -> p b c"), ("Vc", i)), Vst)

                w1_ = w_next("w", l, C_IP1)
                c32s = []
                bk2 = bank_s()
                for kc in range(2):
                    bk = inproj_tile(w1_, kc)
                    c32 = ftmp()
                    P.copy("act", c32[:], bk[:])
                    sq = ftmp()
                    P.act(sq[:], bk[:], AF.Square)
                    P.mm(bk2[:], onesq[:], sq[:], start=(kc == 0), stop=(kc == 1))
                    c32s.append(c32)
                rstd = ftmp()
                rms_rstd(rstd[:], bk2[:])
                for kc in range(2):
                    P.stt("dve", cqn.v(lambda t, kc=kc: t[:, kc, :]), c32s[kc][:], qn(kc), rstd[:], ALU.mult, ALU.mult)
                w2_ = w_next("w", l, C_IP2)
                for a in range(4):
                    vb = Vbuf[0]
                    P.dma("sp", vb.v(lambda t: t[:, 0:nblk, :]),
                          AV(Vc[0:nblk, :, a * 128:(a + 1) * 128].rearrange("b p c -> p b c"), ("Vc", i)),
                          extra_reads=[("Vc", ii) for ii in range(i)])
                    zb = inproj_tile(w2_, a)
                    sz = szm
                    P.act(sz[:], zb[:], AF.Sigmoid)
                    P.tt("dve", sz[:], sz[:], zb[:], ALU.mult)
                    for hh in range(2):
                        h = 2 * a + hh
                        rows = slice(hh * 64, (hh + 1) * 64)
                        kb_ = Kbuf[h % 2]
                        P.dma("sp", kb_[0:96, 0:nblk * 128], AV(Kc[h, :, 0:nblk * 128], ("Kc", h, i)),
                              extra_reads=[("Kc", h, ii) for ii in range(i)])
                        bq = bank_ab()
                        bq2 = bank_ab()
                        for kc in range(2):
                            P.mm(bq[0:96, :], uqkv[:, 1024 + kc * 768 + h * 96: 1024 + kc * 768 + (h + 1) * 96],
                                 cqn.v(lambda t, kc=kc: t[:, kc, :]), start=(kc == 0), stop=(kc == 1))
                        for kc in range(2):
                            P.mm(bq2[0:96, :], uqkv[:, 2560 + kc * 768 + h * 96: 2560 + kc * 768 + (h + 1) * 96],
                                 cqn.v(lambda t, kc=kc: t[:, kc, :]), start=(kc == 0), stop=(kc == 1))
                        qh = Qh[h % 2]
                        P.copy("act", qh[0:64, :], bq[0:64, :])
                        ta = ftmp()
                        tb = ftmp()
                        P.tt("dve", ta[64:96, :], bq[64:96, :], cosT[64:96, :], ALU.mult)
                        P.tt("dve", tb[64:96, :], bq2[64:96, :], sinS[64:96, :], ALU.mult)
                        P.tt("dve", qh[64:96, :], ta[64:96, :], tb[64:96, :], ALU.add)
                        num, den = bank_nd()
                        for kb in range(nblk):
                            v_ = kb - 4 * i
                            c0 = 128 * v_ if v_ > 0 else 0
                            sbk = bank_s()
                            P.mm(sbk[:, c0:TT], kb_[0:96, kb * 128:(kb + 1) * 128], qh[0:96, c0:TT])
                            pt = ptnext()
                            P.act(pt[:, c0:TT], sbk[:, c0:TT], AF.Exp, scale=MLA_SCALE)
                            if v_ >= 0:
                                P.tt("dve", pt[:, c0:TT], pt[:, c0:TT], cmask.v(lambda t, v_=v_, c0=c0: t[:, v_, c0:TT]), ALU.mult)
                            P.mm(num[:, c0:TT], vb.v(lambda t, kb=kb: t[:, kb, :]), pt[:, c0:TT], start=(kb == 0), stop=(kb == nblk - 1))
                            P.mm(den[:, c0:TT], ones_bf[:], pt[:, c0:TT], start=(kb == 0), stop=(kb == nblk - 1))
                        rd = ftmp()
                        P.recip(rd[rows, :], den[rows, :])
                        P.tt("dve", rd[rows, :], rd[rows, :], num[rows, :], ALU.mult)
                        P.tt("dve", ymla.v(lambda t, a=a, rows=rows: t[rows, a, :]), rd[rows, :], sz[rows, :], ALU.mult)

                w3_ = w_next("w", l, C_IP3)
                for t_ in range(4):
                    bk = inproj_tile(w3_, t_)
                    P.copy("act" if t_ % 2 else "dve", UTb.v(lambda t, t_=t_: t[:, t_, :]), bk[:])
                for t_ in range(4):
                    bk = bank_ab()
                    for gp in range(8):
                        for j in range(8):
                            P.mm(bk[:, gp * 64:(gp + 1) * 64],
                                 big.v(lambda t, gp=gp, j=j: t[:, gp, 112 - 16 * j: 240 - 16 * j]),
                                 UTb.v(lambda t, t_=t_, j=j: t[:, t_, :].rearrange("p (k j) -> p k j", j=8)[:, :, j]),
                                 start=(j == 0), stop=(j == 7))
                    P.copy("act" if t_ % 2 else "dve", Vg.v(lambda t, t_=t_: t[:, 8 * t_:8 * t_ + 8, :]),
                           bk.v(lambda t: t[:].rearrange("p (a b) -> p a b", a=8)))
                for half in range(2):
                    wg = w_next("s", l, half)
                    bre_, bim_ = bank_nd()
                    for gl_ in range(16):
                        g = half * 16 + gl_
                        par = g % 2
                        al = gl_ // 2
                        for ri, bb in ((0, bre_), (1, bim_)):
                            P.mm(bb[:, al * 64:(al + 1) * 64],
                                 wg[:, gl_ * 256 + ri * 128: gl_ * 256 + (ri + 1) * 128],
                                 Vg.v(lambda t, g=g: t[:, g, :]), start=(par == 0), stop=(par == 1))
                    P.copy("act", Hre.v(lambda t, half=half: t[:, 8 * half:8 * half + 8, :]),
                           bre_.v(lambda t: t[:].rearrange("p (a b) -> p a b", a=8)))
                    P.copy("dve", Him.v(lambda t, half=half: t[:, 8 * half:8 * half + 8, :]),
                           bim_.v(lambda t: t[:].rearrange("p (a b) -> p a b", a=8)))
                cre_ = carry.v(lambda t: t[:, 0, :])
                cim_ = carry.v(lambda t: t[:, 1, :])
                h0r = Hre.v(lambda t: t[:, :, 0])
                h0i = Him.v(lambda t: t[:, :, 0])
                q1 = stat[:, 0:1]
                sA = g_t[0].v(lambda t: t[:, :, 2])
                sB = g_t[1].v(lambda t: t[:, :, 2])
                sC = g_t[2].v(lambda t: t[:, :, 2])
                sD = g_t[3].v(lambda t: t[:, :, 2])
                cmul(sC, sD, a8(0, 0), a8(0, 1), cre_, cim_, sA, sB)
                P.tt("dve", h0r, h0r, sC, ALU.add)
                P.tt("dve", h0i, h0i, sD, ALU.add)
                for j in range(6):
                    s_ = 1 << j
                    n_ = 64 - s_
                    arb = AV(a8(j, 0).ap.unsqueeze(2).to_broadcast([128, 16, n_]), a8pw.key)
                    aib = AV(a8(j, 1).ap.unsqueeze(2).to_broadcast([128, 16, n_]), a8pw.key)
                    lo = lambda tl: tl.v(lambda t: t[:, :, 0:n_]) if isinstance(tl, Tile) else AV(tl.ap[:, :, 0:n_], tl.key)
                    hi = lambda tl: tl.v(lambda t: t[:, :, s_:64])
                    P.tt("dve", lo(st1), lo(Hre), arb, ALU.mult)
                    P.tt("dve", lo(st2), lo(Him), aib, ALU.mult)
                    P.tt("dve", lo(st1), lo(st1), lo(st2), ALU.subtract)
                    P.tt("dve", lo(st2), lo(Him), arb, ALU.mult)
                    P.tt("dve", lo(st3), lo(Hre), aib, ALU.mult)
                    P.tt("dve", lo(st2), lo(st2), lo(st3), ALU.add)
                    P.tt("dve", hi(Hre), hi(Hre), lo(st1), ALU.add)
                    P.tt("dve", hi(Him), hi(Him), lo(st2), ALU.add)
                P.copy("dve", hpre.v(lambda t: t[:, :, 0]), cre_)
                P.copy("dve", hpim.v(lambda t: t[:, :, 0]), cim_)
                P.copy("act", hpre.v(lambda t: t[:, :, 1:64]), Hre.v(lambda t: t[:, :, 0:63]))
                P.copy("act", hpim.v(lambda t: t[:, :, 1:64]), Him.v(lambda t: t[:, :, 0:63]))
                P.copy("dve", cre_, Hre.v(lambda t: t[:, :, 63]))
                P.copy("dve", cim_, Him.v(lambda t: t[:, :, 63]))
                wT = w_next("s", l, 2)
                wE = w_next("s", l, 3, keep=1)
                for t_ in range(4):
                    bk = bank_ab()
                    for gp in range(8):
                        g = 8 * t_ + gp
                        a, par = g // 2, g % 2
                        rs = slice(par * 64, (par + 1) * 64)
                        oo = bk[:, gp * 64:(gp + 1) * 64]
                        P.mm(oo, wT[:, g * 128:(g + 1) * 128], Vg.v(lambda t, g=g: t[:, g, :]), start=True, stop=False)
                        P.mm(oo, wE[rs, a * 128:(a + 1) * 128], hpre.v(lambda t, rs=rs, a=a: t[rs, a, :]), start=False, stop=False)
                        P.mm(oo, wE[rs, 2048 + a * 128: 2048 + (a + 1) * 128], hpim.v(lambda t, rs=rs, a=a: t[rs, a, :]), start=False, stop=True)
                    P.copy("act" if t_ % 2 else "dve", Yg.v(lambda t, t_=t_: t[:, 8 * t_:8 * t_ + 8, :]),
                           bk.v(lambda t: t[:].rearrange("p (a b) -> p a b", a=8)))
                for t_ in range(4):
                    bk = bank_ab()
                    for ii in range(8):
                        for gp in range(8):
                            P.mm(bk.v(lambda t, ii=ii: t[:].rearrange("p (k i) -> p k i", i=8)[:, :, ii]),
                                 big.v(lambda t, ii=ii, gp=gp: t[:, ii, 112 - 16 * gp: 240 - 16 * gp]),
                                 Yg.v(lambda t, t_=t_, gp=gp: t[:, 8 * t_ + gp, :]), start=(gp == 0), stop=(gp == 7))
                    sq = ftmp()
                    P.act(sq[:], bk[:], AF.Square)
                    P.ts("dve", sq[:], sq[:], 0.044715, ALU.mult, 1.0, ALU.add)
                    P.tt("dve", sq[:], sq[:], bk[:], ALU.mult)
                    P.act(sq[:], sq[:], AF.Sigmoid, scale=1.5957691216057308)
                    P.tt("dve", gel.v(lambda t, t_=t_: t[:, t_, :]), sq[:], bk[:], ALU.mult)
                if dbg and l == 0 and i == 0:
                    P.dma("pool", AV(dbg_t["gel"], "d_gel"), gel.v(lambda t: t[:].rearrange("p a b -> p (a b)")))
                wgl = w_next("w", l, C_GLU)
                w4_ = w_next("w", l, C_IP4, keep=1)
                for n in range(4):
                    ba = bank_s()
                    bb = bank_s()
                    for kc in range(4):
                        P.mm(ba[:], wgl[:, kc * 1024 + n * 128: kc * 1024 + (n + 1) * 128], gel.v(lambda t, kc=kc: t[:, kc, :]),
                             start=(kc == 0), stop=(kc == 3))
                    for kc in range(4):
                        P.mm(bb[:], wgl[:, kc * 1024 + 512 + n * 128: kc * 1024 + 512 + (n + 1) * 128], gel.v(lambda t, kc=kc: t[:, kc, :]),
                             start=(kc == 0), stop=(kc == 3))
                    sg = ftmp()
                    P.act(sg[:], bb[:], AF.Sigmoid, bias=bglu(4 + n))
                    P.stt("dve", sg[:], ba[:], bglu(n), sg[:], ALU.add, ALU.mult)
                    zb = inproj_tile(w4_, n)
                    sz = silu_from_bank(zb)
                    P.tt("dve", yssm.v(lambda t, n=n: t[:, n, :]), sg[:], sz[:], ALU.mult)

                w5_ = w_next("w", l, C_IP5)
                w6_ = w_next("w", l, C_IP6, keep=1)
                for h in range(4):
                    bk = inproj_tile(w5_, h)
                    P.copy("act", qmb[:], bk[:])
                    num, den = bank_nd()
                    for mb in range(2):
                        sbk = bank_s()
                        P.mm(sbk[:], memKT.v(lambda t, h=h, mb=mb: t[:, h, mb * 128:(mb + 1) * 128]), qmb[:])
                        pt = ptnext()
                        P.act(pt[:], sbk[:], AF.Exp, scale=MEM_SCALE)
                        P.mm(num[:], memV.v(lambda t, h=h, mb=mb: t[:, mb, h * 128:(h + 1) * 128]), pt[:], start=(mb == 0), stop=(mb == 1))
                        P.mm(den[:], ones_bf[:], pt[:], start=(mb == 0), stop=(mb == 1))
                    rd = ftmp()
                    P.recip(rd[:], den[:])
                    P.tt("dve", rd[:], rd[:], num[:], ALU.mult)
                    zb = inproj_tile(w6_, h)
                    sz = silu_from_bank(zb)
                    P.tt("dve", ymem.v(lambda t, h=h: t[:, h, :]), rd[:], sz[:], ALU.mult)

                wpm = pmem
                for j in range(8):
                    wm_ = w_next("w", l, C_MG0 + j)
                    macc = ftmp()
                    for k in range(3):
                        ysrc = (yssm, ymla, ymem)[k]
                        ba = bank_ab()
                        for kc in range(4):
                            if k < 2:
                                lw = wm_[:, 3072 + k * 512 + kc * 128: 3072 + k * 512 + (kc + 1) * 128]
                            else:
                                lw = wpm[:, kc * 1024 + j * 128: kc * 1024 + (j + 1) * 128]
                            P.mm(ba[:], lw, ysrc.v(lambda t, kc=kc: t[:, kc, :]), start=(kc == 0), stop=(kc == 3))
                        bg = bank_s()
                        for kc in range(8):
                            P.mm(bg[:], wm_[:, k * 1024 + kc * 128: k * 1024 + (kc + 1) * 128], xT.v(lambda t, kc=kc: t[:, kc, :]),
                                 start=(kc == 0), stop=(kc == 7))
                        gt = ftmp()
                        P.act(gt[:], bg[:], AF.Sigmoid, bias=bgate(k, j))
                        if k == 0:
                            P.tt("dve", macc[:], gt[:], ba[:], ALU.mult)
                        elif k == 1:
                            P.tt("dve", gt[:], gt[:], ba[:], ALU.mult)
                            P.tt("dve", macc[:], macc[:], gt[:], ALU.add)
                        else:
                            P.tt("dve", gt[:], gt[:], ba[:], ALU.mult)
                            P.tt("dve", mT.v(lambda t, j=j: t[:, j, :]), macc[:], gt[:], ALU.add)
                if dbg and l == 0 and i == 0:
                    for nm, tl in (("yssm", yssm), ("ymla", ymla), ("ymem", ymem)):
                        P.dma("pool", AV(dbg_t[nm], "d_" + nm), tl.v(lambda t: t[:].rearrange("p a b -> p (a b)")))
                    P.dma("pool", AV(dbg_t["mT2"], "d_mT2"), mT.v(lambda t: t[:, 0:4, :].rearrange("p a b -> p (a b)")))
                wo0 = w_next("w", l, C_WO0)
                wo1 = w_next("w", l, C_WO1, keep=1)
                for sub in range(4):
                    xsub = xs[sub % 2]
                    P.dma("pool", xsub[:], AV(xin[t0 + sub * 128: t0 + (sub + 1) * 128, :], (kin, i)))
                    for half, wo in ((0, wo0), (1, wo1)):
                        bo = bank_ab() if half == 0 else bank_s()
                        for kc in range(8):
                            P.mm(bo[:], mT.v(lambda t, kc=kc, sub=sub: t[:, kc, sub * 128:(sub + 1) * 128]),
                                 wo[:, kc * 512:(kc + 1) * 512], start=(kc == 0), stop=(kc == 7))
                        P.stt("dve", AV(vln.ap[:, half * 512:(half + 1) * 512], vln.key), xsub[:, half * 512:(half + 1) * 512],
                              ALPHA, bo[:], ALU.mult, ALU.add)
                    P.reduce("dve", stat[:, 0:1], vln, ALU.add)
                    P.act(junk, vln, AF.Square)
                    P.reduce("dve", stat[:, 1:2], junk, ALU.add)
                    P.ts("dve", stat[:, 2:3], stat[:, 0:1], 1.0 / D, ALU.mult)
                    P.tt("dve", stat[:, 3:4], stat[:, 2:3], stat[:, 2:3], ALU.mult)
                    P.stt("dve", stat[:, 4:5], stat[:, 1:2], 1.0 / D, stat[:, 3:4], ALU.mult, ALU.subtract)
                    P.ts("dve", stat[:, 4:5], stat[:, 4:5], EPS, ALU.add)
                    P.act(stat[:, 5:6], stat[:, 4:5], AF.Sqrt)
                    P.recip(stat[:, 5:6], stat[:, 5:6])
                    P.ts("dve", oln, vln, stat[:, 2:3], ALU.subtract, stat[:, 5:6], ALU.mult)
                    P.tt("dve", oln, oln, lng[:], ALU.mult)
                    P.tt("dve", oln, oln, lnb[:], ALU.add)
                    P.dma("pool", AV(xout[t0 + sub * 128: t0 + (sub + 1) * 128, :], (kout, i)), oln)
        P.emit()
    return nc, P


_CACHE = {}


def _prep(inputs, S, NL):
    wimg = np.concatenate([host_weight_image(inputs, l) for l in range(NL)], axis=0)
    sm, ln = zip(*[host_small(inputs, l) for l in range(NL)])
    com = dict(host_consts())
    com["wimg"] = np.ascontiguousarray(wimg)
    com["small"] = np.ascontiguousarray(np.stack(sm))
    com["lnp"] = np.ascontiguousarray(np.stack(ln))
    return com


def run(inputs, S=SEQ, NL=DEPTH, cores=8, dbg=False, ret=None):
    inputs = {k: np.asarray(v) for k, v in inputs.items()}
    key = (S, NL, dbg)
    if key not in _CACHE:
        _CACHE[key] = build(S, NL, dbg)
    nc, P = _CACHE[key]
    com = _prep(inputs, S, NL)
    in_maps = []
    for b in range(cores):
        m = dict(com)
        m["x"] = np.ascontiguousarray(inputs["x"][b, :S]).astype(np.float32)
        m["mem"] = np.ascontiguousarray(inputs["mem"][b]).astype(np.float32)
        m["pos"] = np.ascontiguousarray(inputs["positions"][b, :S]).reshape(1, S).astype(np.int32)
        in_maps.append(m)
    import time as _t
    _t0 = _t.time()
    res = run_bass_kernel_spmd(nc, in_maps, core_ids=list(range(cores)))
    print("KERNEL spmd run seconds", _t.time() - _t0, "stats", P.stats, "waits", P.nwaits, flush=True)
    if ret is not None:
        ret.update({k: np.asarray(v) for k, v in res.results[0].items()})
    return np.stack([np.asarray(r["y"]) for r in res.results], axis=0).astype(np.float32)


def kernel(**inputs):
    return run(inputs)
```

```python
import contextlib
import math
import numpy as np
import ml_dtypes
import concourse.bass as bass
import concourse.mybir as mybir
from concourse.bass_utils import run_bass_kernel_spmd

F32 = mybir.dt.float32
BF16 = mybir.dt.bfloat16
I32 = mybir.dt.int32
AF = mybir.ActivationFunctionType
ALU = mybir.AluOpType
AX = mybir.AxisListType

D = 1024
SEQ = 4096
DEPTH = 4
MEM = 256
TT = 512
ALPHA = (2 * DEPTH) ** 0.25
EPS = 1e-5
MLA_SCALE = 96 ** -0.5
MEM_SCALE = 128 ** -0.5
TWO_PI = 2.0 * math.pi
MAGIC = 12582912.0
CH = 4096
(C_MEMK, C_MEMV, C_UQKV, C_IP0, C_IP1, C_IP2, C_IP3, C_GLU, C_IP4, C_IP5, C_IP6, C_PMEM) = range(12)
C_MG0 = 12
C_WO0 = 20
C_WO1 = 21
NCH = 22


class AV:
    __slots__ = ("ap", "key")

    def __init__(self, ap, key):
        self.ap = ap
        self.key = key


class Tile:
    def __init__(self, t, key):
        self.t = t
        self.key = key

    def __getitem__(self, idx):
        return AV(self.t[idx], self.key)

    def v(self, ap_fn):
        return AV(ap_fn(self.t), self.key)


class Op:
    __slots__ = ("id", "eng", "fn", "deps", "dma", "needs", "sem", "val", "waits")


class Prog:
    NSLOT = 8

    def __init__(self, nc):
        self.nc = nc
        self.ops = []
        self.res_w = {}
        self.res_r = {}
        self.stack = None
        self.ntile = 0

    def sb(self, shape, dt, name=None):
        self.ntile += 1
        name = name or f"t{self.ntile}"
        t = self.stack.enter_context(self.nc.sbuf_tensor("sb_" + name, list(shape), dt))
        return Tile(t, name)

    def ps(self, shape, dt, name):
        t = self.stack.enter_context(self.nc.psum_tensor(name, list(shape), dt))
        return Tile(t, name)

    def add(self, eng, fn, reads=(), writes=(), dma=False):
        op = Op()
        op.id = len(self.ops)
        op.eng = eng
        op.fn = fn
        op.dma = dma
        op.needs = False
        deps = set()
        for r in reads:
            if r in self.res_w:
                deps.add(self.res_w[r])
        for w in writes:
            if w in self.res_w:
                deps.add(self.res_w[w])
            for rr in self.res_r.get(w, ()):
                deps.add(rr)
        for r in reads:
            self.res_r.setdefault(r, []).append(op.id)
        for w in writes:
            self.res_w[w] = op.id
            self.res_r[w] = []
        deps.discard(op.id)
        if eng == "pe":
            deps = {d for d in deps if not (self.ops[d].eng == "pe" and not self.ops[d].dma)}
        op.deps = deps
        self.ops.append(op)
        return op.id

    def mm(self, out, lhsT, rhs, start=True, stop=True):
        return self.add("pe", lambda e: e.matmul(out.ap, lhsT.ap, rhs.ap, start=start, stop=stop),
                        reads=[lhsT.key, rhs.key], writes=[out.key])

    def tr(self, out, in_, ident):
        return self.add("pe", lambda e: e.transpose(out.ap, in_.ap, ident.ap),
                        reads=[in_.key, ident.key], writes=[out.key])

    def act(self, out, in_, func, bias=None, scale=None):
        kw = {}
        rd = [in_.key]
        if bias is not None:
            if isinstance(bias, AV):
                kw["bias"] = bias.ap
                rd.append(bias.key)
            else:
                kw["bias"] = float(bias)
        if scale is not None:
            if isinstance(scale, AV):
                kw["scale"] = scale.ap
                rd.append(scale.key)
            else:
                kw["scale"] = float(scale)
        return self.add("act", lambda e: e.activation(out.ap, in_.ap, func, **kw), reads=rd, writes=[out.key])

    def tt(self, eng, out, in0, in1, op):
        return self.add(eng, lambda e: e.tensor_tensor(out.ap, in0.ap, in1.ap, op),
                        reads=[in0.key, in1.key], writes=[out.key])

    def ts(self, eng, out, in0, s1, op0, s2=None, op1=None):
        rd = [in0.key]
        a1 = s1.ap if isinstance(s1, AV) else float(s1)
        if isinstance(s1, AV):
            rd.append(s1.key)
        a2 = None
        if s2 is not None:
            a2 = s2.ap if isinstance(s2, AV) else float(s2)
            if isinstance(s2, AV):
                rd.append(s2.key)
        if op1 is None:
            return self.add(eng, lambda e: e.tensor_scalar(out.ap, in0.ap, a1, None, op0), reads=rd, writes=[out.key])
        return self.add(eng, lambda e: e.tensor_scalar(out.ap, in0.ap, a1, a2, op0, op1), reads=rd, writes=[out.key])

    def stt(self, eng, out, in0, scalar, in1, op0, op1):
        rd = [in0.key, in1.key]
        a = scalar.ap if isinstance(scalar, AV) else float(scalar)
        if isinstance(scalar, AV):
            rd.append(scalar.key)
        return self.add(eng, lambda e: e.scalar_tensor_tensor(out.ap, in0.ap, a, in1.ap, op0, op1),
                        reads=rd, writes=[out.key])

    def copy(self, eng, out, in_):
        if eng == "act":
            return self.act(out, in_, AF.Copy)
        return self.add(eng, lambda e: e.tensor_copy(out.ap, in_.ap), reads=[in_.key], writes=[out.key])

    def memset(self, eng, out, val):
        return self.add(eng, lambda e: e.memset(out.ap, val), reads=[], writes=[out.key])

    def reduce(self, eng, out, in_, op):
        return self.add(eng, lambda e: e.tensor_reduce(out.ap, in_.ap, AX.X, op), reads=[in_.key], writes=[out.key])

    def recip(self, out, in_):
        return self.add("dve", lambda e: e.reciprocal(out.ap, in_.ap), reads=[in_.key], writes=[out.key])

    def dma(self, q, out, in_, extra_reads=(), extra_writes=()):
        return self.add(q, lambda e: e.dma_start(out=out.ap, in_=in_.ap),
                        reads=[in_.key] + list(extra_reads), writes=[out.key] + list(extra_writes), dma=True)

    def emit(self):
        nc = self.nc
        ops = self.ops
        for op in ops:
            for d in op.deps:
                ops[d].needs = True
        engs = []
        for op in ops:
            if op.eng not in engs:
                engs.append(op.eng)
        sems = {e: self.stack.enter_context(nc.semaphore(f"s_{e}")) for e in engs}
        dsem = {}
        for e in engs:
            if any(o.dma and o.eng == e for o in ops):
                dsem[e] = [self.stack.enter_context(nc.semaphore(f"d_{e}{i}")) for i in range(self.NSLOT)]
        cnt = {e: 0 for e in engs}
        dcnt = {e: [0] * self.NSLOT for e in engs}
        dn = {e: 0 for e in engs}
        for op in ops:
            op.waits = []
            if op.dma:
                slot = dn[op.eng] % self.NSLOT
                dn[op.eng] += 1
                if dcnt[op.eng][slot] > 0:
                    op.waits.append((dsem[op.eng][slot], dcnt[op.eng][slot] * 16))
                dcnt[op.eng][slot] += 1
                op.sem = dsem[op.eng][slot]
                op.val = dcnt[op.eng][slot] * 16
            elif op.needs:
                cnt[op.eng] += 1
                op.sem = sems[op.eng]
                op.val = cnt[op.eng]
            else:
                op.sem = None
                op.val = None
        known = {e: {} for e in engs}
        for op in ops:
            k = known[op.eng]
            cand = list(op.waits) + [(ops[d].sem, ops[d].val) for d in sorted(op.deps)]
            best = {}
            for s, v in cand:
                if v > best.get(id(s), (None, 0))[1]:
                    best[id(s)] = (s, v)
            ws = []
            for sid, (s, v) in best.items():
                if k.get(sid, 0) >= v:
                    continue
                k[sid] = v
                ws.append((s, v))
            op.waits = ws
        final_waits = {}
        for e in engs:
            if e in dsem:
                final_waits[e] = [(dsem[e][i], dcnt[e][i] * 16) for i in range(self.NSLOT) if dcnt[e][i] > 0]
        by_eng = {e: [o for o in ops if o.eng == e] for e in engs}
        self.stats = {e: len(v) for e, v in by_eng.items()}
        self.nwaits = sum(len(o.waits) for o in ops)
        attr = {"pe": "tensor", "act": "scalar", "dve": "vector", "pool": "gpsimd", "sp": "sync"}
        with nc.Block() as block:
            for e in engs:
                def body(eng, _ops=by_eng[e], _e=e):
                    for o in _ops:
                        for s, v in o.waits:
                            eng.wait_ge(s, v)
                        ins = o.fn(eng)
                        if o.sem is not None:
                            ins.then_inc(o.sem, 16 if o.dma else 1)
                    for s, v in final_waits.get(_e, ()):
                        eng.wait_ge(s, v)
                getattr(block, attr[e])(body)


def _img(w_rows_cols):
    K, N = w_rows_cols.shape
    return w_rows_cols.reshape(K // 128, 128, N).transpose(1, 0, 2).reshape(128, -1)


def host_weight_image(inp, l):
    w_in = inp["w_in"][l]
    o = np.cumsum([0, 512, 512, 256, 128, 32, 512, 512, 512, 3072])
    u, zs, cq, ckv, kr, zmla, qm, zm, gl = [w_in[:, o[i]:o[i + 1]] for i in range(9)]
    z128 = np.zeros((1024, 128), np.float32)
    img = np.zeros((NCH, 128, CH), np.float32)

    def put_tiles(c, tiles):
        buf = np.zeros((1024, 512), np.float32)
        for i, t in enumerate(tiles):
            buf[:, i * 128:(i + 1) * 128] = t
        img[c] = _img(buf)

    rope = z128.copy()
    rope[:, 64:96] = kr
    rope_sw = z128.copy()
    rope_sw[:, 64:80] = kr[:, 16:32]
    rope_sw[:, 80:96] = kr[:, 0:16]
    put_tiles(C_IP0, [ckv, rope, rope_sw])
    put_tiles(C_IP1, [cq[:, 0:128], cq[:, 128:256]])
    put_tiles(C_IP2, [zmla[:, i * 128:(i + 1) * 128] for i in range(4)])
    put_tiles(C_IP3, [u[:, i * 128:(i + 1) * 128] for i in range(4)])
    put_tiles(C_IP4, [zs[:, i * 128:(i + 1) * 128] for i in range(4)])
    put_tiles(C_IP5, [qm[:, i * 128:(i + 1) * 128] for i in range(4)])
    put_tiles(C_IP6, [zm[:, i * 128:(i + 1) * 128] for i in range(4)])
    wm = inp["w_mem_kv"][l]
    img[C_MEMK] = _img(wm[:, 0:512])
    img[C_MEMV] = _img(wm[:, 512:1024])
    wukv = inp["w_ukv"][l].reshape(128, 8, 128)
    wk = wukv[:, :, 0:64].reshape(128, 512)
    wv = wukv[:, :, 64:128].reshape(128, 512)
    wuq = inp["w_uq"][l].reshape(256, 8, 96)
    wuq_sw = np.zeros((256, 8, 96), np.float32)
    wuq_sw[:, :, 64:80] = wuq[:, :, 80:96]
    wuq_sw[:, :, 80:96] = wuq[:, :, 64:80]
    img[C_UQKV] = np.concatenate([wk, wv, _img(wuq.reshape(256, 768)), _img(wuq_sw.reshape(256, 768))], axis=1)
    img[C_GLU] = _img(inp["w_glu"][l])
    img[C_PMEM] = _img(inp["p_mem"][l])
    ps_, pm_ = inp["p_ssm"][l], inp["p_mla"][l]
    for j in range(8):
        parts = [_img(gl[:, k * 1024 + j * 128: k * 1024 + (j + 1) * 128]) for k in range(3)]
        parts.append(_img(ps_[:, j * 128:(j + 1) * 128]))
        parts.append(_img(pm_[:, j * 128:(j + 1) * 128]))
        img[C_MG0 + j] = np.concatenate(parts, axis=1)
    wo = inp["w_out"][l]
    img[C_WO0] = _img(wo[:, 0:512])
    img[C_WO1] = _img(wo[:, 512:1024])
    return img


def host_small(inp, l):
    d = {}
    d["bgate"] = inp["b_gate"][l].reshape(3, 8, 128).transpose(2, 0, 1).reshape(128, 24)
    d["bglu"] = inp["b_glu"][l].reshape(8, 128).T
    d["qn"] = inp["mla_q_norm"][l].reshape(2, 128).T
    d["kvn"] = inp["mla_kv_norm"][l].reshape(1, 128).T
    sm = np.concatenate([d["bgate"], d["bglu"], d["qn"], d["kvn"]], axis=1)

    def pp(a):
        sh = a.shape
        return a.reshape((16, 2, 64) + sh[2:]).transpose((1, 2, 0) + tuple(range(3, len(sh) + 1))).reshape((128, 16) + sh[2:])

    are = pp(inp["ssm_a_re"][l])
    aim = pp(inp["ssm_a_im"][l])
    ldt = pp(np.broadcast_to(inp["ssm_log_dt"][l][:, None], (32, 64)))
    bre = pp(inp["ssm_b_re"][l])
    bim = pp(inp["ssm_b_im"][l])
    cre = pp(inp["ssm_c_re"][l].transpose(0, 2, 1))
    cim = pp(inp["ssm_c_im"][l].transpose(0, 2, 1))
    ssm = np.concatenate([are, aim, ldt, bre.reshape(128, 256), bim.reshape(128, 256),
                          cre.reshape(128, 256), cim.reshape(128, 256)], axis=1)
    dtab = np.broadcast_to(inp["ssm_d"][l].reshape(32, 1, 16), (32, 8, 16)).transpose(1, 2, 0).reshape(128, 32)
    small = np.concatenate([sm, ssm, dtab], axis=1).astype(np.float32)
    ln = np.stack([inp["ln_g"][l], inp["ln_b"][l]], axis=0).astype(np.float32)
    return small, ln


NSMALL = 35 + 1072 + 32


def host_consts():
    c = {}
    c["ident"] = np.eye(128, dtype=np.float32)
    big = np.zeros((128, 8, 240), np.float32)
    for g in range(8):
        for cc in range(16):
            big[g * 16 + cc, g, cc + 112] = 1.0
    c["big"] = big.reshape(128, 8 * 240).astype(ml_dtypes.bfloat16)
    mask = np.zeros((128, 4, 512), np.float32)
    k = np.arange(128)[:, None]
    q = np.arange(512)[None, :]
    for v in range(4):
        mask[:, v, :] = (q >= 128 * v + k)
    c["cmask"] = mask.reshape(128, 2048).astype(ml_dtypes.bfloat16)
    r = np.arange(128)
    tm = ((r[None, :] // 16) >= (r[:, None] // 16)).astype(np.float32)
    c["tmask"] = tm
    invf = np.zeros((128, 4), np.float32)
    fr = (10000.0 ** (-np.arange(0, 32, 2, dtype=np.float32) / 32)).astype(np.float32)
    for rr in range(32):
        invf[64 + rr, 0] = fr[rr % 16]
        invf[64 + rr, 1] = -1.0 if rr < 16 else 1.0
    c["invf"] = invf
    return c


def build(S, NL, dbg=False):
    NT = S // TT
    nc = bass.Bass("TRN2", target_bir_lowering=False)
    P = Prog(nc)

    def din(name, shape, dt=F32):
        return nc.dram_tensor(name, list(shape), dt, kind="ExternalInput").ap()

    x_in = din("x", [S, D])
    mem_in = din("mem", [MEM, D])
    pos_in = din("pos", [1, S], I32)
    wimg = din("wimg", [NL * NCH, 128, CH])
    small_in = din("small", [NL, 128, NSMALL])
    ln_in = din("lnp", [NL, 2, D])
    ident_in = din("ident", [128, 128])
    big_in = din("big", [128, 8 * 240], BF16)
    cmask_in = din("cmask", [128, 2048], BF16)
    tmask_in = din("tmask", [128, 128])
    invf_in = din("invf", [128, 4])
    y_out = nc.dram_tensor("y", [S, D], F32, kind="ExternalOutput").ap()

    def dscr(name, shape, dt):
        return nc.dram_tensor(name, list(shape), dt, kind="Internal").ap()

    wbf = dscr("wbf", [NL * NCH, 128, CH], BF16)
    ssmw = dscr("ssmw", [NL * 4, 128, CH], BF16)
    Kc = dscr("Kc", [8, 96, S], BF16)
    Vc = dscr("Vc", [S // 128, 128, 512], BF16)
    xa = dscr("xa", [S, D], F32)
    xb = dscr("xb", [S, D], F32)
    csd = dscr("csd", [2, 32, S], F32)
    dbg_t = {}
    if dbg:
        for nm in ("yssm", "ymla", "ymem", "mT2"):
            dbg_t[nm] = nc.dram_tensor("d_" + nm, [128, 4 * TT], BF16, kind="ExternalOutput").ap()
        dbg_t["gel"] = nc.dram_tensor("d_gel", [128, 4 * TT], BF16, kind="ExternalOutput").ap()

    with contextlib.ExitStack() as st:
        P.stack = st
        bank = [P.ps([128, 512], F32, f"bank{i}") for i in range(8)]
        rot = {"ab": 0, "s": 0, "nd": 0}

        def bank_ab():
            rot["ab"] ^= 1
            return bank[rot["ab"]]

        def bank_s():
            rot["s"] ^= 1
            return bank[2 + rot["s"]]

        def bank_nd():
            rot["nd"] ^= 1
            return bank[4 + 2 * rot["nd"]], bank[5 + 2 * rot["nd"]]

        ident = P.sb([128, 128], F32, "ident")
        big = P.sb([128, 8, 240], BF16, "big")
        cmask = P.sb([128, 4, 512], BF16, "cmask")
        tmask = P.sb([128, 128], F32, "tmask")
        invf = P.sb([128, 4], F32, "invf")
        ones_bf = P.sb([128, 128], BF16, "ones_bf")
        onesk = P.sb([128, 128], F32, "onesk")
        onesq = P.sb([128, 128], F32, "onesq")
        P.dma("sp", ident[:], AV(ident_in, "c_ident"))
        P.dma("sp", big.v(lambda t: t[:].rearrange("p a b -> p (a b)")), AV(big_in, "c_big"))
        P.dma("sp", cmask.v(lambda t: t[:].rearrange("p a b -> p (a b)")), AV(cmask_in, "c_cmask"))
        P.dma("sp", tmask[:], AV(tmask_in, "c_tmask"))
        P.dma("sp", invf[:], AV(invf_in, "c_invf"))
        P.memset("dve", ones_bf[:], 1.0)
        P.memset("dve", onesk[:], 1.0 / 128)
        P.memset("dve", onesq[:], 1.0 / 256)

        for c in range(NL * NCH):
            P.dma("pool", AV(wbf[c], ("wbf", c)), AV(wimg[c], "wimg"))

        NB = 3
        wring = [P.sb([128, CH], BF16, f"wring{i}") for i in range(NB)]
        stream = []
        for l in range(NL):
            stream += [("w", l, C_MEMK), ("w", l, C_MEMV)]
            for i in range(NT):
                stream += [("w", l, C_IP0), ("w", l, C_IP1), ("w", l, C_IP2), ("w", l, C_IP3),
                           ("s", l, 0), ("s", l, 1), ("s", l, 2), ("s", l, 3),
                           ("w", l, C_GLU), ("w", l, C_IP4), ("w", l, C_IP5), ("w", l, C_IP6)]
                stream += [("w", l, C_MG0 + j) for j in range(8)]
                stream += [("w", l, C_WO0), ("w", l, C_WO1)]
        wst = {"issued": 0, "taken": 0}

        def w_issue():
            n = wst["issued"]
            kind, l, c = stream[n]
            if kind == "w":
                src = AV(wbf[l * NCH + c], ("wbf", l * NCH + c))
            else:
                src = AV(ssmw[l * 4 + c], ("ssmw", l * 4 + c))
            P.dma("sp", wring[n % NB][:], src)
            wst["issued"] += 1

        def w_next(kind, l, c, keep=0):
            n = wst["taken"]
            assert stream[n] == (kind, l, c), (stream[n], kind, l, c)
            while wst["issued"] < min(n + NB - keep, len(stream)):
                w_issue()
            wst["taken"] += 1
            return wring[n % NB]

        small = P.sb([128, NSMALL], F32, "small")
        lng = P.sb([128, D], F32, "lng")
        lnb = P.sb([128, D], F32, "lnb")
        uqkv = P.sb([128, CH], BF16, "uqkv")
        memT = P.sb([128, 8, MEM], BF16, "memT")
        memKT = P.sb([128, 4, MEM], BF16, "memKT")
        memV = P.sb([128, 2, 512], BF16, "memV")
        xs = [P.sb([128, D], F32, f"xs{i}") for i in range(2)]
        xT = P.sb([128, 8, TT], BF16, "xT")
        f32t = [P.sb([128, TT], F32, f"f32t{i}") for i in range(5)]
        fi = {"i": 0}

        def ftmp():
            fi["i"] = (fi["i"] + 1) % len(f32t)
            return f32t[fi["i"]]

        cosT = P.sb([128, TT], F32, "cosT")
        sinS = P.sb([128, TT], F32, "sinS")
        ckvn = P.sb([128, TT], BF16, "ckvn")
        cqn = P.sb([128, 2, TT], BF16, "cqn")
        krb = P.sb([128, TT], BF16, "krb")
        Qh = [P.sb([128, TT], BF16, f"Qh{i}") for i in range(2)]
        Kst = Qh
        Kbuf = [P.sb([128, 4096], BF16, f"Kbuf{i}") for i in range(2)]
        Vbuf = [P.sb([128, 32, 128], BF16, f"Vbuf{i}") for i in range(1)]
        PT = [P.sb([128, TT], BF16, f"PT{i}") for i in range(3)]
        pti = {"i": 0}

        def ptnext():
            pti["i"] = (pti["i"] + 1) % 3
            return PT[pti["i"]]

        ymla = P.sb([128, 4, TT], BF16, "ymla")
        yssm = P.sb([128, 4, TT], BF16, "yssm")
        ymem = P.sb([128, 4, TT], BF16, "ymem")
        UTb = P.sb([128, 4, TT], BF16, "UTb")
        gel = UTb
        Vg = P.sb([128, 32, 64], BF16, "Vg")
        Yg = Vg
        Vst = Vg.v(lambda t: t[:].rearrange("p a b -> p (a b)").rearrange("p (s c) -> p s c", s=4))
        Hre = P.sb([128, 16, 64], F32, "Hre")
        Him = P.sb([128, 16, 64], F32, "Him")
        hpre = P.sb([128, 16, 64], BF16, "hpre")
        hpim = P.sb([128, 16, 64], BF16, "hpim")
        carry = P.sb([128, 2, 16], F32, "carry")
        a8pw = P.sb([128, 6, 2, 16], F32, "a8pw")
        qmb = P.sb([128, TT], BF16, "qmb")
        szm = P.sb([128, TT], F32, "szm")
        mT = P.sb([128, 8, TT], BF16, "mT")
        lnbuf = P.sb([128, 4, D], F32, "lnbuf")
        vln = lnbuf.v(lambda t: t[:, 0, :])
        junk = lnbuf.v(lambda t: t[:, 1, :])
        oln = lnbuf.v(lambda t: t[:, 2, :])
        st1 = lnbuf.v(lambda t: t[:, 0, :].rearrange("p (a k) -> p a k", k=64))
        st2 = lnbuf.v(lambda t: t[:, 1, :].rearrange("p (a k) -> p a k", k=64))
        st3 = lnbuf.v(lambda t: t[:, 2, :].rearrange("p (a k) -> p a k", k=64))
        stat = P.sb([128, 8], F32, "stat")
        g_t = [P.sb([128, 16, 16], F32, f"g_t{i}") for i in range(10)]
        pw = P.sb([128, 2, 16, 16], F32, "pw")
        pwd = P.sb([128, 2, 8, 16], F32, "pwd")
        planeL = [lnbuf.v(lambda t, r=r: t[:, 2 * r:2 * r + 2, :].rearrange("p a (b c) -> p (a b) c", c=128)) for r in range(2)]
        planeK = [Kbuf[r].v(lambda t: t[:].bitcast(F32).rearrange("p (a c) -> p a c", c=128)) for r in range(2)]
        planeX = xT.v(lambda t: t[:].rearrange("p a b -> p (a b)").bitcast(F32).rearrange("p (a c) -> p a c", c=128))
        planeM = mT.v(lambda t: t[:].rearrange("p a b -> p (a b)").bitcast(F32).rearrange("p (a c) -> p a c", c=128))
        pmem = P.sb([128, CH], BF16, "pmem")
        simg = Tile(None, Vbuf[0].key)
        simg.t = Vbuf[0].t[:].rearrange("p a b -> p (a b)")

        Dx = [x_in, xa, xb]
        xsA = [AV(Kbuf[s_ // 2].t[:].bitcast(F32)[:, (s_ % 2) * D:(s_ % 2 + 1) * D], Kbuf[s_ // 2].key) for s_ in range(4)]

        def sincos(o_sin, o_cos, ang, ta, tb, tc, td):
            C1 = 6.28125
            C2 = TWO_PI - 6.28125
            P.ts("dve", ta, ang, 1.0 / TWO_PI, ALU.mult, MAGIC, ALU.add)
            P.ts("dve", ta, ta, -MAGIC, ALU.add)
            P.stt("dve", tb, ta, -C1, ang, ALU.mult, ALU.add)
            P.stt("dve", tb, ta, -C2, tb, ALU.mult, ALU.add)
            P.ts("dve", tb, tb, 0.125, ALU.mult)
            P.tt("dve", tc, tb, tb, ALU.mult)
            a = [-1.0 / 6, 1.0 / 120, -1.0 / 5040, 1.0 / 362880]
            b = [-0.5, 1.0 / 24, -1.0 / 720, 1.0 / 40320]
            P.ts("dve", td, tc, a[3], ALU.mult)
            for cf in (a[2], a[1], a[0]):
                P.stt("dve", td, td, cf, tc, ALU.add, ALU.mult)
            P.stt("dve", o_sin, td, 1.0, tb, ALU.add, ALU.mult)
            P.ts("dve", td, tc, b[3], ALU.mult)
            for cf in (b[2], b[1], b[0]):
                P.stt("dve", td, td, cf, tc, ALU.add, ALU.mult)
            P.ts("dve", o_cos, td, 1.0, ALU.add)
            for _ in range(3):
                P.tt("dve", ta, o_sin, o_sin, ALU.mult)
                P.tt("dve", tb, o_cos, o_cos, ALU.mult)
                P.stt("dve", tc, o_sin, 2.0, o_cos, ALU.mult, ALU.mult)
                P.tt("dve", o_cos, tb, ta, ALU.subtract)
                P.copy("dve", o_sin, tc)

        def cmul(o_re, o_im, a_re, a_im, b_re, b_im, t1, t2):
            P.tt("dve", t1, a_re, b_re, ALU.mult)
            P.tt("dve", t2, a_im, b_im, ALU.mult)
            P.tt("dve", o_re, t1, t2, ALU.subtract)
            P.tt("dve", t1, a_re, b_im, ALU.mult)
            P.tt("dve", t2, a_im, b_re, ALU.mult)
            P.tt("dve", o_im, t1, t2, ALU.add)

        def rms_rstd(dst, ms_bank):
            P.ts("dve", dst, ms_bank, EPS, ALU.add)
            P.act(dst, dst, AF.Sqrt)
            P.recip(dst, dst)

        def silu_from_bank(bk):
            sg = ftmp()
            P.act(sg[:], bk[:], AF.Sigmoid)
            P.tt("dve", sg[:], sg[:], bk[:], ALU.mult)
            return sg

        rotc = {}

        def bank_rot(name, ids):
            k = rotc.get(name, -1) + 1
            rotc[name] = k
            return bank[ids[k % len(ids)]]

        def inproj_tile(wt, ti, banks=None):
            bk = bank_ab() if banks is None else bank_rot("ip" + str(banks), banks)
            for kc in range(8):
                P.mm(bk[:], wt.v(lambda t, kc=kc, ti=ti: t[:, kc * 512 + ti * 128: kc * 512 + (ti + 1) * 128]),
                     xT.v(lambda t, kc=kc: t[:, kc, :]), start=(kc == 0), stop=(kc == 7))
            return bk

        for i in range(NT):
            pi32 = f32t[0].v(lambda t: t[64:96, :].bitcast(I32))
            src = AV(bass.AP(pos_in.tensor, i * TT, [[0, 32], [1, TT]]), "pos")
            P.dma("pool", pi32, src)
            angt = f32t[1]
            P.copy("dve", angt[64:96, :], pi32)
            P.ts("dve", angt[64:96, :], angt[64:96, :], invf[64:96, 0:1], ALU.mult)
            sincos(sinS[64:96, :], cosT[64:96, :], angt[64:96, :], f32t[2][64:96, :], f32t[3][64:96, :], f32t[4][64:96, :], f32t[0][64:96, :])
            P.ts("dve", sinS[64:96, :], sinS[64:96, :], invf[64:96, 1:2], ALU.mult)
            P.dma("pool", AV(csd[0, :, i * TT:(i + 1) * TT], ("csd", i)), cosT[64:96, :])
            P.dma("pool", AV(csd[1, :, i * TT:(i + 1) * TT], ("csd", i)), sinS[64:96, :])
        for mb in range(2):
            P.dma("pool", xs[mb][:], AV(mem_in[mb * 128:(mb + 1) * 128, :], "mem"))
            for half in range(2):
                bk = bank_ab()
                for q in range(4):
                    kc = half * 4 + q
                    P.tr(bk[:, q * 128:(q + 1) * 128], xs[mb][:, kc * 128:(kc + 1) * 128], ident[:])
                P.copy("dve", memT.v(lambda t, half=half, mb=mb: t[:, half * 4:half * 4 + 4, mb * 128:(mb + 1) * 128]),
                       bk.v(lambda t: t[:].rearrange("p (a b) -> p a b", a=4)))

        for l in range(NL):
            xin = Dx[0] if l == 0 else Dx[1 + ((l - 1) % 2)]
            xout = y_out if l == NL - 1 else Dx[1 + (l % 2)]
            kin = "x0" if l == 0 else ("xs", (l - 1) % 2)
            kout = "y" if l == NL - 1 else ("xs", l % 2)

            P.dma("pool", small[:], AV(small_in[l], "small_in"))
            P.dma("pool", lng[:], AV(bass.AP(ln_in.tensor, (l * 2) * D, [[0, 128], [1, D]]), "ln_in"))
            P.dma("pool", lnb[:], AV(bass.AP(ln_in.tensor, (l * 2 + 1) * D, [[0, 128], [1, D]]), "ln_in"))
            P.dma("pool", uqkv[:], AV(wbf[l * NCH + C_UQKV], ("wbf", l * NCH + C_UQKV)))
            P.dma("pool", pmem[:], AV(wbf[l * NCH + C_PMEM], ("wbf", l * NCH + C_PMEM)))
            bgate = lambda k, j: small[:, k * 8 + j: k * 8 + j + 1]
            bglu = lambda n: small[:, 24 + n: 24 + n + 1]
            qn = lambda kc: small[:, 32 + kc: 33 + kc]
            kvn = small[:, 34:35]
            o0 = 35
            s_are = small[:, o0:o0 + 16]
            s_aim = small[:, o0 + 16:o0 + 32]
            s_ldt = small[:, o0 + 32:o0 + 48]
            s_bre = small.v(lambda t: t[:, o0 + 48:o0 + 304].rearrange("p (a c) -> p a c", c=16))
            s_bim = small.v(lambda t: t[:, o0 + 304:o0 + 560].rearrange("p (a c) -> p a c", c=16))
            s_cre = small.v(lambda t: t[:, o0 + 560:o0 + 816].rearrange("p (a c) -> p a c", c=16))
            s_cim = small.v(lambda t: t[:, o0 + 816:o0 + 1072].rearrange("p (a c) -> p a c", c=16))
            dtab = lambda g: small[:, o0 + 1072 + g: o0 + 1073 + g]

            wk = w_next("w", l, C_MEMK)
            for h in range(4):
                bk = bank_ab()
                for kc in range(8):
                    P.mm(bk[:, 0:MEM], wk.v(lambda t, kc=kc, h=h: t[:, kc * 512 + h * 128: kc * 512 + (h + 1) * 128]),
                         memT.v(lambda t, kc=kc: t[:, kc, :]), start=(kc == 0), stop=(kc == 7))
                P.copy("act", memKT.v(lambda t, h=h: t[:, h, :]), bk[:, 0:MEM])
            wv = w_next("w", l, C_MEMV)
            for mb in range(2):
                bk = bank_ab()
                for kc in range(8):
                    P.mm(bk[:], memT.v(lambda t, kc=kc, mb=mb: t[:, kc, mb * 128:(mb + 1) * 128]),
                         wv.v(lambda t, kc=kc: t[:, kc * 512:(kc + 1) * 512]), start=(kc == 0), stop=(kc == 7))
                P.copy("act", memV.v(lambda t, mb=mb: t[:, mb, :]), bk[:])

            T_ = [g_t[i].v(lambda t: t[:, :, 0]) for i in range(10)]
            dt_, lrdt, ang, mag, lbre, lbim, tA, tB, tC, tD = T_
            P.act(dt_, s_ldt, AF.Exp)
            P.tt("dve", lrdt, s_are, dt_, ALU.mult)
            P.tt("dve", ang, s_aim, dt_, ALU.mult)
            P.act(mag, lrdt, AF.Exp)
            sincos(tA, tB, ang, tC, tD, g_t[8].v(lambda t: t[:, :, 2]), g_t[9].v(lambda t: t[:, :, 2]))
            P.tt("dve", lbre, mag, tB, ALU.mult)
            P.tt("dve", lbim, mag, tA, ALU.mult)
            T2 = [g_t[i].v(lambda t: t[:, :, 1]) for i in range(10)]
            nr, den, fre, fim, u1, u2, ivre, ivim, m2, u3 = T2
            P.ts("dve", nr, lbre, -1.0, ALU.add)
            P.tt("dve", u1, s_are, s_are, ALU.mult)
            P.tt("dve", u2, s_aim, s_aim, ALU.mult)
            P.tt("dve", den, u1, u2, ALU.add)
            P.recip(den, den)
            P.tt("dve", u1, nr, s_are, ALU.mult)
            P.tt("dve", u2, lbim, s_aim, ALU.mult)
            P.tt("dve", u1, u1, u2, ALU.add)
            P.tt("dve", fre, u1, den, ALU.mult)
            P.tt("dve", u1, lbim, s_are, ALU.mult)
            P.tt("dve", u2, nr, s_aim, ALU.mult)
            P.tt("dve", u1, u1, u2, ALU.subtract)
            P.tt("dve", fim, u1, den, ALU.mult)
            P.tt("dve", u1, lbre, lbre, ALU.mult)
            P.tt("dve", u2, lbim, lbim, ALU.mult)
            P.tt("dve", m2, u1, u2, ALU.add)
            P.recip(m2, m2)
            P.tt("dve", ivre, lbre, m2, ALU.mult)
            P.tt("dve", u3, lbim, m2, ALU.mult)
            P.ts("dve", ivim, u3, -1.0, ALU.mult)
            pwv = lambda ri, n: pw.v(lambda t, ri=ri, n=n: t[:, ri, n + 7, :])
            P.memset("dve", pwv(0, 0), 1.0)
            P.memset("dve", pwv(1, 0), 0.0)
            for n in range(1, 9):
                cmul(pwv(0, n), pwv(1, n), pwv(0, n - 1), pwv(1, n - 1), lbre, lbim, u1, u2)
            for n in range(-1, -8, -1):
                cmul(pwv(0, n), pwv(1, n), pwv(0, n + 1), pwv(1, n + 1), ivre, ivim, u1, u2)
            for j in range(8):
                for ri in range(2):
                    P.copy("dve", pwd.v(lambda t, ri=ri, j=j: t[:, ri, j, :]), pwv(ri, 7 - j))
            a8 = lambda j, ri: a8pw.v(lambda t, j=j, ri=ri: t[:, j, ri, :])
            P.copy("dve", a8(0, 0), pwv(0, 8))
            P.copy("dve", a8(0, 1), pwv(1, 8))
            for j in range(1, 6):
                cmul(a8(j, 0), a8(j, 1), a8(j - 1, 0), a8(j - 1, 1), a8(j - 1, 0), a8(j - 1, 1), u1, u2)
            bbre, bbim, w1, w2 = g_t[6], g_t[7], g_t[8], g_t[9]
            bc3 = lambda av: AV(av.ap.unsqueeze(2).to_broadcast([128, 16, 16]), av.key)
            P.tt("dve", w1[:], s_bre, bc3(fre), ALU.mult)
            P.tt("dve", w2[:], s_bim, bc3(fim), ALU.mult)
            P.tt("dve", bbre[:], w1[:], w2[:], ALU.subtract)
            P.tt("dve", w1[:], s_bim, bc3(fre), ALU.mult)
            P.tt("dve", w2[:], s_bre, bc3(fim), ALU.mult)
            P.tt("dve", bbim[:], w1[:], w2[:], ALU.add)

            def v4(plane):
                return AV(plane.ap.rearrange("p a (j c) -> p a j c", c=16), plane.key)

            def pwb(tl, ri, lo):
                return tl.v(lambda t, ri=ri, lo=lo: t[:, ri, lo:lo + 8, :].rearrange("p n a -> p a n").unsqueeze(3).to_broadcast([128, 16, 8, 16]))

            def cb(av):
                return AV(av.ap.unsqueeze(2).to_broadcast([128, 16, 8, 16]), av.key)

            def cgen(dst, pt, lo, cr, ci, neg_im):
                ta, tb = v4(planeX), v4(planeM)
                P.tt("dve", ta, pwb(pt, 0, lo), cb(cr), ALU.mult)
                P.tt("dve", tb, pwb(pt, 1, lo), cb(ci), ALU.mult)
                P.tt("dve", v4(dst[0]), ta, tb, ALU.subtract)
                P.tt("dve", ta, pwb(pt, 0, lo), cb(ci), ALU.mult)
                P.tt("dve", tb, pwb(pt, 1, lo), cb(cr), ALU.mult)
                P.tt("dve", v4(dst[1]), ta, tb, ALU.add)
                if neg_im:
                    P.ts("dve", v4(dst[1]), v4(dst[1]), -1.0, ALU.mult)

            cgen(planeK, pw, 8, s_cre, s_cim, True)
            for ri in range(2):
                P.copy("dve", simg.v(lambda t, ri=ri: t[:, ri * 2048:(ri + 1) * 2048].rearrange("p (a c) -> p a c", c=128)), planeK[ri])
            P.dma("pool", AV(ssmw[l * 4 + 3], ("ssmw", l * 4 + 3)), simg[:])
            cgen(planeL, pwd, 0, bbre[:], bbim[:], False)
            cgen(planeK, pw, 0, s_cre, s_cim, True)
            for g in range(32):
                a, par = g // 2, g % 2
                bk = bank_ab()
                rs = slice(par * 64, (par + 1) * 64)
                P.mm(bk[:, 0:128], AV(planeL[0].ap[rs, a, :], planeL[0].key), AV(planeK[0].ap[rs, a, :], planeK[0].key), start=True, stop=False)
                P.mm(bk[:, 0:128], AV(planeL[1].ap[rs, a, :], planeL[1].key), AV(planeK[1].ap[rs, a, :], planeK[1].key), start=False, stop=True)
                tt_ = ftmp()
                P.tt("dve", tt_[:, 0:128], bk[:, 0:128], tmask[:], ALU.mult)
                P.stt("dve", simg[:, g * 128:(g + 1) * 128], ident[:], dtab(g), tt_[:, 0:128], ALU.mult, ALU.add)
            P.dma("pool", AV(ssmw[l * 4 + 2], ("ssmw", l * 4 + 2)), simg[:])
            for half in range(2):
                P.memset("dve", simg[:], 0.0)
                for gl_ in range(16):
                    g = half * 16 + gl_
                    a, par = g // 2, g % 2
                    rs = slice(par * 64, (par + 1) * 64)
                    bk = bank_ab()
                    for ri in range(2):
                        P.tr(bk[:, ri * 64:(ri + 1) * 64], AV(planeL[ri].ap[rs, a, :], planeL[ri].key),
                             AV(ident.t[rs, rs], ident.key))
                    P.copy("dve", simg.v(lambda t, gl_=gl_, par=par: t[:, gl_ * 256:(gl_ + 1) * 256].rearrange("p (r c) -> p r c", r=2)[:, :, par * 64:(par + 1) * 64]),
                           bk.v(lambda t: t[:, 0:128].rearrange("p (r c) -> p r c", r=2)))
                P.dma("pool", AV(ssmw[l * 4 + half], ("ssmw", l * 4 + half)), simg[:])
            P.memset("dve", carry[:], 0.0)

            for i in range(NT):
                t0 = i * TT
                nblk = 4 * (i + 1)
                for sub in range(4):
                    xsub = xs[sub % 2]
                    P.dma("pool", xsub[:], AV(xin[t0 + sub * 128: t0 + (sub + 1) * 128, :], (kin, i)))
                    for half in range(2):
                        bk = bank_rot("t0", [0, 1, 2, 3])
                        for q in range(4):
                            kc = half * 4 + q
                            P.tr(bk[:, q * 128:(q + 1) * 128], xsub[:, kc * 128:(kc + 1) * 128], ident[:])
                        P.copy("act" if half else "dve",
                               xT.v(lambda t, half=half, sub=sub: t[:, half * 4:half * 4 + 4, sub * 128:(sub + 1) * 128]),
                               bk.v(lambda t: t[:].rearrange("p (a b) -> p a b", a=4)))
                P.dma("pool", cosT[64:96, :], AV(csd[0, :, t0:t0 + TT], ("csd", i)))
                P.dma("pool", sinS[64:96, :], AV(csd[1, :, t0:t0 + TT], ("csd", i)))

                w0 = w_next("w", l, C_IP0)
                bk = inproj_tile(w0, 0)
                c32 = ftmp()
                P.copy("act", c32[:], bk[:])
                sq = ftmp()
                P.act(sq[:], bk[:], AF.Square)
                bk2 = bank_ab()
                P.mm(bk2[:], onesk[:], sq[:])
                rstd = ftmp()
                rms_rstd(rstd[:], bk2[:])
                P.stt("dve", ckvn[:], c32[:], kvn, rstd[:], ALU.mult, ALU.mult)
                bk = inproj_tile(w0, 1)
                bk2 = inproj_tile(w0, 2)
                ta = ftmp()
                tb = ftmp()
                P.tt("dve", ta[64:96, :], bk[64:96, :], cosT[64:96, :], ALU.mult)
                P.tt("dve", tb[64:96, :], bk2[64:96, :], sinS[64:96, :], ALU.mult)
                P.tt("dve", krb[64:96, :], ta[64:96, :], tb[64:96, :], ALU.add)
                for h in range(8):
                    bk = bank_ab()
                    P.mm(bk[0:64, :], uqkv[:, h * 64:(h + 1) * 64], ckvn[:])
                    ks = Kst[h % 2]
                    P.copy("act", ks[0:64, :], bk[0:64, :])
                    P.copy("dve", ks[64:96, :], krb[64:96, :])
                    P.dma("pool", AV(Kc[h, :, t0:t0 + TT], ("Kc", h, i)), ks[0:96, :])
                for sub in range(4):
                    bk = bank_ab()
                    P.mm(bk[:], ckvn[:, sub * 128:(sub + 1) * 128], uqkv[:, 512:1024])
                    P.copy("act" if sub % 2 else "dve", AV(Vst.ap[:, sub, :], Vst.key), bk[:])
                P.dma("pool", AV(Vc[4 * i:4 * i + 4].rearrange("b p c -> p b c"), ("Vc", i)), Vst)

                w1_ = w_next("w", l, C_IP1)
                c32s = []
                bk2 = bank_s()
                for kc in range(2):
                    bk = inproj_tile(w1_, kc)
                    c32 = ftmp()
                    P.copy("act", c32[:], bk[:])
                    sq = ftmp()
                    P.act(sq[:], bk[:], AF.Square)
                    P.mm(bk2[:], onesq[:], sq[:], start=(kc == 0), stop=(kc == 1))
                    c32s.append(c32)
                rstd = ftmp()
                rms_rstd(rstd[:], bk2[:])
                for kc in range(2):
                    P.stt("dve", cqn.v(lambda t, kc=kc: t[:, kc, :]), c32s[kc][:], qn(kc), rstd[:], ALU.mult, ALU.mult)
                w2_ = w_next("w", l, C_IP2)
                def emit_q(h):
                    kb_ = Kbuf[h % 2]
                    P.dma("sp", kb_[0:96, 0:nblk * 128], AV(Kc[h, :, 0:nblk * 128], ("Kc", h, i)),
                          extra_reads=[("Kc", h, ii) for ii in range(i)])
                    bq = bank_ab()
                    bq2 = bank_ab()
                    for kc in range(2):
                        P.mm(bq[0:96, :], uqkv[:, 1024 + kc * 768 + h * 96: 1024 + kc * 768 + (h + 1) * 96],
                             cqn.v(lambda t, kc=kc: t[:, kc, :]), start=(kc == 0), stop=(kc == 1))
                    for kc in range(2):
                        P.mm(bq2[0:96, :], uqkv[:, 2560 + kc * 768 + h * 96: 2560 + kc * 768 + (h + 1) * 96],
                             cqn.v(lambda t, kc=kc: t[:, kc, :]), start=(kc == 0), stop=(kc == 1))
                    qh = Qh[h % 2]
                    P.copy("act", qh[0:64, :], bq[0:64, :])
                    ta = ftmp()
                    tb = ftmp()
                    P.tt("dve", ta[64:96, :], bq[64:96, :], cosT[64:96, :], ALU.mult)
                    P.tt("dve", tb[64:96, :], bq2[64:96, :], sinS[64:96, :], ALU.mult)
                    P.tt("dve", qh[64:96, :], ta[64:96, :], tb[64:96, :], ALU.add)

                emit_q(0)
                for a in range(4):
                    vb = Vbuf[0]
                    P.dma("sp", vb.v(lambda t: t[:, 0:nblk, :]),
                          AV(Vc[0:nblk, :, a * 128:(a + 1) * 128].rearrange("b p c -> p b c"), ("Vc", i)),
                          extra_reads=[("Vc", ii) for ii in range(i)])
                    zb = inproj_tile(w2_, a)
                    sz = szm
                    P.act(sz[:], zb[:], AF.Sigmoid)
                    P.tt("dve", sz[:], sz[:], zb[:], ALU.mult)
                    for hh in range(2):
                        h = 2 * a + hh
                        rows = slice(hh * 64, (hh + 1) * 64)
                        kb_ = Kbuf[h % 2]
                        qh = Qh[h % 2]
                        if h + 1 < 8:
                            emit_q(h + 1)
                        num, den = bank_nd()
                        pend = None

                        def pv(pn):
                            kb, pt, c0 = pn
                            P.mm(num[:, c0:TT], vb.v(lambda t, kb=kb: t[:, kb, :]), pt[:, c0:TT], start=(kb == 0), stop=(kb == nblk - 1))
                            P.mm(den[:, c0:TT], ones_bf[:], pt[:, c0:TT], start=(kb == 0), stop=(kb == nblk - 1))

                        for kb in range(nblk):
                            v_ = kb - 4 * i
                            c0 = 128 * v_ if v_ > 0 else 0
                            sbk = bank_s()
                            P.mm(sbk[:, c0:TT], kb_[0:96, kb * 128:(kb + 1) * 128], qh[0:96, c0:TT])
                            pt = ptnext()
                            P.act(pt[:, c0:TT], sbk[:, c0:TT], AF.Exp, scale=MLA_SCALE)
                            if v_ >= 0:
                                P.tt("dve", pt[:, c0:TT], pt[:, c0:TT], cmask.v(lambda t, v_=v_, c0=c0: t[:, v_, c0:TT]), ALU.mult)
                            if pend is not None:
                                pv(pend)
                            pend = (kb, pt, c0)
                        pv(pend)
                        rd = ftmp()
                        P.recip(rd[rows, :], den[rows, :])
                        P.tt("dve", rd[rows, :], rd[rows, :], num[rows, :], ALU.mult)
                        P.tt("dve", ymla.v(lambda t, a=a, rows=rows: t[rows, a, :]), rd[rows, :], sz[rows, :], ALU.mult)


                w3_ = w_next("w", l, C_IP3)
                for t_ in range(4):
                    bk = inproj_tile(w3_, t_, banks=[0, 1, 2, 3])
                    P.copy("act" if t_ % 2 else "dve", UTb.v(lambda t, t_=t_: t[:, t_, :]), bk[:])
                for t_ in range(4):
                    bk = bank_rot("sel", [4, 5, 6, 7])
                    for gp in range(8):
                        for j in range(8):
                            P.mm(bk[:, gp * 64:(gp + 1) * 64],
                                 big.v(lambda t, gp=gp, j=j: t[:, gp, 112 - 16 * j: 240 - 16 * j]),
                                 UTb.v(lambda t, t_=t_, j=j: t[:, t_, :].rearrange("p (k j) -> p k j", j=8)[:, :, j]),
                                 start=(j == 0), stop=(j == 7))
                    P.copy("act" if t_ % 2 else "dve", Vg.v(lambda t, t_=t_: t[:, 8 * t_:8 * t_ + 8, :]),
                           bk.v(lambda t: t[:].rearrange("p (a b) -> p a b", a=8)))
                for half in range(2):
                    wg = w_next("s", l, half)
                    bre_, bim_ = bank[2 * half], bank[2 * half + 1]
                    for gl_ in range(16):
                        g = half * 16 + gl_
                        par = g % 2
                        al = gl_ // 2
                        for ri, bb in ((0, bre_), (1, bim_)):
                            P.mm(bb[:, al * 64:(al + 1) * 64],
                                 wg[:, gl_ * 256 + ri * 128: gl_ * 256 + (ri + 1) * 128],
                                 Vg.v(lambda t, g=g: t[:, g, :]), start=(par == 0), stop=(par == 1))
                    P.copy("act", Hre.v(lambda t, half=half: t[:, 8 * half:8 * half + 8, :]),
                           bre_.v(lambda t: t[:].rearrange("p (a b) -> p a b", a=8)))
                    P.copy("dve", Him.v(lambda t, half=half: t[:, 8 * half:8 * half + 8, :]),
                           bim_.v(lambda t: t[:].rearrange("p (a b) -> p a b", a=8)))
                cre_ = carry.v(lambda t: t[:, 0, :])
                cim_ = carry.v(lambda t: t[:, 1, :])
                h0r = Hre.v(lambda t: t[:, :, 0])
                h0i = Him.v(lambda t: t[:, :, 0])
                q1 = stat[:, 0:1]
                sA = g_t[0].v(lambda t: t[:, :, 2])
                sB = g_t[1].v(lambda t: t[:, :, 2])
                sC = g_t[2].v(lambda t: t[:, :, 2])
                sD = g_t[3].v(lambda t: t[:, :, 2])
                cmul(sC, sD, a8(0, 0), a8(0, 1), cre_, cim_, sA, sB)
                P.tt("dve", h0r, h0r, sC, ALU.add)
                P.tt("dve", h0i, h0i, sD, ALU.add)
                for j in range(6):
                    s_ = 1 << j
                    n_ = 64 - s_
                    arb = AV(a8(j, 0).ap.unsqueeze(2).to_broadcast([128, 16, n_]), a8pw.key)
                    aib = AV(a8(j, 1).ap.unsqueeze(2).to_broadcast([128, 16, n_]), a8pw.key)
                    lo = lambda tl: tl.v(lambda t: t[:, :, 0:n_]) if isinstance(tl, Tile) else AV(tl.ap[:, :, 0:n_], tl.key)
                    hi = lambda tl: tl.v(lambda t: t[:, :, s_:64])
                    P.tt("dve", lo(st1), lo(Hre), arb, ALU.mult)
                    P.tt("dve", lo(st2), lo(Him), aib, ALU.mult)
                    P.tt("dve", lo(st1), lo(st1), lo(st2), ALU.subtract)
                    P.tt("dve", lo(st2), lo(Him), arb, ALU.mult)
                    P.tt("dve", lo(st3), lo(Hre), aib, ALU.mult)
                    P.tt("dve", lo(st2), lo(st2), lo(st3), ALU.add)
                    P.tt("dve", hi(Hre), hi(Hre), lo(st1), ALU.add)
                    P.tt("dve", hi(Him), hi(Him), lo(st2), ALU.add)
                P.copy("dve", hpre.v(lambda t: t[:, :, 0]), cre_)
                P.copy("dve", hpim.v(lambda t: t[:, :, 0]), cim_)
                P.copy("act", hpre.v(lambda t: t[:, :, 1:64]), Hre.v(lambda t: t[:, :, 0:63]))
                P.copy("act", hpim.v(lambda t: t[:, :, 1:64]), Him.v(lambda t: t[:, :, 0:63]))
                P.copy("dve", cre_, Hre.v(lambda t: t[:, :, 63]))
                P.copy("dve", cim_, Him.v(lambda t: t[:, :, 63]))
                wT = w_next("s", l, 2)
                wE = w_next("s", l, 3, keep=1)
                for t_ in range(4):
                    bk = bank_rot("y", [4, 5, 6, 7])
                    for gp in range(8):
                        g = 8 * t_ + gp
                        a, par = g // 2, g % 2
                        rs = slice(par * 64, (par + 1) * 64)
                        oo = bk[:, gp * 64:(gp + 1) * 64]
                        P.mm(oo, wT[:, g * 128:(g + 1) * 128], Vg.v(lambda t, g=g: t[:, g, :]), start=True, stop=False)
                        P.mm(oo, wE[rs, a * 128:(a + 1) * 128], hpre.v(lambda t, rs=rs, a=a: t[rs, a, :]), start=False, stop=False)
                        P.mm(oo, wE[rs, 2048 + a * 128: 2048 + (a + 1) * 128], hpim.v(lambda t, rs=rs, a=a: t[rs, a, :]), start=False, stop=True)
                    P.copy("act" if t_ % 2 else "dve", Yg.v(lambda t, t_=t_: t[:, 8 * t_:8 * t_ + 8, :]),
                           bk.v(lambda t: t[:].rearrange("p (a b) -> p a b", a=8)))
                for t_ in range(4):
                    bk = bank_rot("inv", [0, 1, 2, 3])
                    for ii in range(8):
                        for gp in range(8):
                            P.mm(bk.v(lambda t, ii=ii: t[:].rearrange("p (k i) -> p k i", i=8)[:, :, ii]),
                                 big.v(lambda t, ii=ii, gp=gp: t[:, ii, 112 - 16 * gp: 240 - 16 * gp]),
                                 Yg.v(lambda t, t_=t_, gp=gp: t[:, 8 * t_ + gp, :]), start=(gp == 0), stop=(gp == 7))
                    sq = ftmp()
                    P.act(sq[:], bk[:], AF.Square)
                    P.ts("dve", sq[:], sq[:], 0.044715, ALU.mult, 1.0, ALU.add)
                    P.tt("dve", sq[:], sq[:], bk[:], ALU.mult)
                    P.act(sq[:], sq[:], AF.Sigmoid, scale=1.5957691216057308)
                    P.tt("dve", gel.v(lambda t, t_=t_: t[:, t_, :]), sq[:], bk[:], ALU.mult)
                if dbg and l == 0 and i == 0:
                    P.dma("pool", AV(dbg_t["gel"], "d_gel"), gel.v(lambda t: t[:].rearrange("p a b -> p (a b)")))
                wgl = w_next("w", l, C_GLU)
                w4_ = w_next("w", l, C_IP4, keep=1)
                for n in range(4):
                    ba = bank_rot("glu", [2, 3, 4, 5])
                    bb = bank_rot("glu", [2, 3, 4, 5])
                    for kc in range(4):
                        P.mm(ba[:], wgl[:, kc * 1024 + n * 128: kc * 1024 + (n + 1) * 128], gel.v(lambda t, kc=kc: t[:, kc, :]),
                             start=(kc == 0), stop=(kc == 3))
                    for kc in range(4):
                        P.mm(bb[:], wgl[:, kc * 1024 + 512 + n * 128: kc * 1024 + 512 + (n + 1) * 128], gel.v(lambda t, kc=kc: t[:, kc, :]),
                             start=(kc == 0), stop=(kc == 3))
                    sg = ftmp()
                    P.act(sg[:], bb[:], AF.Sigmoid, bias=bglu(4 + n))
                    P.stt("dve", sg[:], ba[:], bglu(n), sg[:], ALU.add, ALU.mult)
                    zb = inproj_tile(w4_, n, banks=[0, 1, 6, 7])
                    sz = silu_from_bank(zb)
                    P.tt("dve", yssm.v(lambda t, n=n: t[:, n, :]), sg[:], sz[:], ALU.mult)

                w5_ = w_next("w", l, C_IP5)
                w6_ = w_next("w", l, C_IP6, keep=1)
                for h in range(4):
                    bk = inproj_tile(w5_, h)
                    P.copy("act", qmb[:], bk[:])
                    num, den = bank_nd()
                    for mb in range(2):
                        sbk = bank_s()
                        P.mm(sbk[:], memKT.v(lambda t, h=h, mb=mb: t[:, h, mb * 128:(mb + 1) * 128]), qmb[:])
                        pt = ptnext()
                        P.act(pt[:], sbk[:], AF.Exp, scale=MEM_SCALE)
                        P.mm(num[:], memV.v(lambda t, h=h, mb=mb: t[:, mb, h * 128:(h + 1) * 128]), pt[:], start=(mb == 0), stop=(mb == 1))
                        P.mm(den[:], ones_bf[:], pt[:], start=(mb == 0), stop=(mb == 1))
                    rd = ftmp()
                    P.recip(rd[:], den[:])
                    P.tt("dve", rd[:], rd[:], num[:], ALU.mult)
                    zb = inproj_tile(w6_, h)
                    sz = silu_from_bank(zb)
                    P.tt("dve", ymem.v(lambda t, h=h: t[:, h, :]), rd[:], sz[:], ALU.mult)

                wpm = pmem
                for j in range(8):
                    wm_ = w_next("w", l, C_MG0 + j)
                    macc = ftmp()
                    for k in range(3):
                        ysrc = (yssm, ymla, ymem)[k]
                        ba = bank_rot("mga", [4, 5, 6, 7])
                        for kc in range(4):
                            if k < 2:
                                lw = wm_[:, 3072 + k * 512 + kc * 128: 3072 + k * 512 + (kc + 1) * 128]
                            else:
                                lw = wpm[:, kc * 1024 + j * 128: kc * 1024 + (j + 1) * 128]
                            P.mm(ba[:], lw, ysrc.v(lambda t, kc=kc: t[:, kc, :]), start=(kc == 0), stop=(kc == 3))
                        bg = bank_rot("mgg", [0, 1, 2, 3])
                        for kc in range(8):
                            P.mm(bg[:], wm_[:, k * 1024 + kc * 128: k * 1024 + (kc + 1) * 128], xT.v(lambda t, kc=kc: t[:, kc, :]),
                                 start=(kc == 0), stop=(kc == 7))
                        gt = ftmp()
                        P.act(gt[:], bg[:], AF.Sigmoid, bias=bgate(k, j))
                        if k == 0:
                            P.tt("dve", macc[:], gt[:], ba[:], ALU.mult)
                        elif k == 1:
                            P.tt("dve", gt[:], gt[:], ba[:], ALU.mult)
                            P.tt("dve", macc[:], macc[:], gt[:], ALU.add)
                        else:
                            P.tt("dve", gt[:], gt[:], ba[:], ALU.mult)
                            P.tt("dve", mT.v(lambda t, j=j: t[:, j, :]), macc[:], gt[:], ALU.add)
                if dbg and l == 0 and i == 0:
                    for nm, tl in (("yssm", yssm), ("ymla", ymla), ("ymem", ymem)):
                        P.dma("pool", AV(dbg_t[nm], "d_" + nm), tl.v(lambda t: t[:].rearrange("p a b -> p (a b)")))
                    P.dma("pool", AV(dbg_t["mT2"], "d_mT2"), mT.v(lambda t: t[:, 0:4, :].rearrange("p a b -> p (a b)")))
                wo0 = w_next("w", l, C_WO0)
                wo1 = w_next("w", l, C_WO1, keep=1)
                for sub in range(4):
                    xsub = xs[sub % 2]
                    P.dma("pool", xsub[:], AV(xin[t0 + sub * 128: t0 + (sub + 1) * 128, :], (kin, i)))
                    for half, wo in ((0, wo0), (1, wo1)):
                        bo = bank_rot("wo", [0, 1, 2, 3])
                        for kc in range(8):
                            P.mm(bo[:], mT.v(lambda t, kc=kc, sub=sub: t[:, kc, sub * 128:(sub + 1) * 128]),
                                 wo[:, kc * 512:(kc + 1) * 512], start=(kc == 0), stop=(kc == 7))
                        P.stt("dve", AV(vln.ap[:, half * 512:(half + 1) * 512], vln.key), xsub[:, half * 512:(half + 1) * 512],
                              ALPHA, bo[:], ALU.mult, ALU.add)
                    P.reduce("dve", stat[:, 0:1], vln, ALU.add)
                    P.act(junk, vln, AF.Square)
                    P.reduce("dve", stat[:, 1:2], junk, ALU.add)
                    P.ts("dve", stat[:, 2:3], stat[:, 0:1], 1.0 / D, ALU.mult)
                    P.tt("dve", stat[:, 3:4], stat[:, 2:3], stat[:, 2:3], ALU.mult)
                    P.stt("dve", stat[:, 4:5], stat[:, 1:2], 1.0 / D, stat[:, 3:4], ALU.mult, ALU.subtract)
                    P.ts("dve", stat[:, 4:5], stat[:, 4:5], EPS, ALU.add)
                    P.act(stat[:, 5:6], stat[:, 4:5], AF.Sqrt)
                    P.recip(stat[:, 5:6], stat[:, 5:6])
                    P.ts("dve", oln, vln, stat[:, 2:3], ALU.subtract, stat[:, 5:6], ALU.mult)
                    P.tt("dve", oln, oln, lng[:], ALU.mult)
                    P.tt("dve", oln, oln, lnb[:], ALU.add)
                    P.dma("pool", AV(xout[t0 + sub * 128: t0 + (sub + 1) * 128, :], (kout, i)), oln)
        P.emit()
    return nc, P


_CACHE = {}


def _prep(inputs, S, NL):
    wimg = np.concatenate([host_weight_image(inputs, l) for l in range(NL)], axis=0)
    sm, ln = zip(*[host_small(inputs, l) for l in range(NL)])
    com = dict(host_consts())
    com["wimg"] = np.ascontiguousarray(wimg)
    com["small"] = np.ascontiguousarray(np.stack(sm))
    com["lnp"] = np.ascontiguousarray(np.stack(ln))
    return com


def run(inputs, S=SEQ, NL=DEPTH, cores=8, dbg=False, ret=None):
    inputs = {k: np.asarray(v) for k, v in inputs.items()}
    key = (S, NL, dbg)
    if key not in _CACHE:
        _CACHE[key] = build(S, NL, dbg)
    nc, P = _CACHE[key]
    com = _prep(inputs, S, NL)
    in_maps = []
    for b in range(cores):
        m = dict(com)
        m["x"] = np.ascontiguousarray(inputs["x"][b, :S]).astype(np.float32)
        m["mem"] = np.ascontiguousarray(inputs["mem"][b]).astype(np.float32)
        m["pos"] = np.ascontiguousarray(inputs["positions"][b, :S]).reshape(1, S).astype(np.int32)
        in_maps.append(m)
    import time as _t
    _t0 = _t.time()
    res = run_bass_kernel_spmd(nc, in_maps, core_ids=list(range(cores)))
    print("KERNEL spmd run seconds", _t.time() - _t0, "stats", P.stats, "waits", P.nwaits, flush=True)
    if ret is not None:
        ret.update({k: np.asarray(v) for k, v in res.results[0].items()})
    return np.stack([np.asarray(r["y"]) for r in res.results], axis=0).astype(np.float32)


def kernel(**inputs):
    return run(inputs)
```

```python
import contextlib
import math
import numpy as np
import ml_dtypes
import concourse.bass as bass
import concourse.mybir as mybir
from concourse.bass_utils import run_bass_kernel_spmd

F32 = mybir.dt.float32
BF16 = mybir.dt.bfloat16
I32 = mybir.dt.int32
AF = mybir.ActivationFunctionType
ALU = mybir.AluOpType
AX = mybir.AxisListType

D = 1024
SEQ = 4096
DEPTH = 4
MEM = 256
TT = 512
ALPHA = (2 * DEPTH) ** 0.25
EPS = 1e-5
MLA_SCALE = 96 ** -0.5
MEM_SCALE = 128 ** -0.5
TWO_PI = 2.0 * math.pi
MAGIC = 12582912.0
CH = 4096
(C_MEMK, C_MEMV, C_UQKV, C_IP0, C_IP1, C_IP2, C_IP3, C_GLU, C_IP4, C_IP5, C_IP6, C_PMEM) = range(12)
C_MG0 = 12
C_WO0 = 20
C_WO1 = 21
NCH = 22


class AV:
    __slots__ = ("ap", "key")

    def __init__(self, ap, key):
        self.ap = ap
        self.key = key


class Tile:
    def __init__(self, t, key):
        self.t = t
        self.key = key

    def __getitem__(self, idx):
        return AV(self.t[idx], self.key)

    def v(self, ap_fn):
        return AV(ap_fn(self.t), self.key)


class Op:
    __slots__ = ("id", "eng", "fn", "deps", "dma", "needs", "sem", "val", "waits")


class Prog:
    NSLOT = 8

    def __init__(self, nc):
        self.nc = nc
        self.ops = []
        self.res_w = {}
        self.res_r = {}
        self.stack = None
        self.ntile = 0

    def sb(self, shape, dt, name=None):
        self.ntile += 1
        name = name or f"t{self.ntile}"
        t = self.stack.enter_context(self.nc.sbuf_tensor("sb_" + name, list(shape), dt))
        return Tile(t, name)

    def ps(self, shape, dt, name):
        t = self.stack.enter_context(self.nc.psum_tensor(name, list(shape), dt))
        return Tile(t, name)

    def add(self, eng, fn, reads=(), writes=(), dma=False):
        op = Op()
        op.id = len(self.ops)
        op.eng = eng
        op.fn = fn
        op.dma = dma
        op.needs = False
        deps = set()
        for r in reads:
            if r in self.res_w:
                deps.add(self.res_w[r])
        for w in writes:
            if w in self.res_w:
                deps.add(self.res_w[w])
            for rr in self.res_r.get(w, ()):
                deps.add(rr)
        for r in reads:
            self.res_r.setdefault(r, []).append(op.id)
        for w in writes:
            self.res_w[w] = op.id
            self.res_r[w] = []
        deps.discard(op.id)
        if eng == "pe":
            deps = {d for d in deps if not (self.ops[d].eng == "pe" and not self.ops[d].dma)}
        op.deps = deps
        self.ops.append(op)
        return op.id

    def mm(self, out, lhsT, rhs, start=True, stop=True):
        return self.add("pe", lambda e: e.matmul(out.ap, lhsT.ap, rhs.ap, start=start, stop=stop),
                        reads=[lhsT.key, rhs.key], writes=[out.key])

    def tr(self, out, in_, ident):
        return self.add("pe", lambda e: e.transpose(out.ap, in_.ap, ident.ap),
                        reads=[in_.key, ident.key], writes=[out.key])

    def act(self, out, in_, func, bias=None, scale=None):
        kw = {}
        rd = [in_.key]
        if bias is not None:
            if isinstance(bias, AV):
                kw["bias"] = bias.ap
                rd.append(bias.key)
            else:
                kw["bias"] = float(bias)
        if scale is not None:
            if isinstance(scale, AV):
                kw["scale"] = scale.ap
                rd.append(scale.key)
            else:
                kw["scale"] = float(scale)
        return self.add("act", lambda e: e.activation(out.ap, in_.ap, func, **kw), reads=rd, writes=[out.key])

    def tt(self, eng, out, in0, in1, op):
        return self.add(eng, lambda e: e.tensor_tensor(out.ap, in0.ap, in1.ap, op),
                        reads=[in0.key, in1.key], writes=[out.key])

    def ts(self, eng, out, in0, s1, op0, s2=None, op1=None):
        rd = [in0.key]
        a1 = s1.ap if isinstance(s1, AV) else float(s1)
        if isinstance(s1, AV):
            rd.append(s1.key)
        a2 = None
        if s2 is not None:
            a2 = s2.ap if isinstance(s2, AV) else float(s2)
            if isinstance(s2, AV):
                rd.append(s2.key)
        if op1 is None:
            return self.add(eng, lambda e: e.tensor_scalar(out.ap, in0.ap, a1, None, op0), reads=rd, writes=[out.key])
        return self.add(eng, lambda e: e.tensor_scalar(out.ap, in0.ap, a1, a2, op0, op1), reads=rd, writes=[out.key])

    def stt(self, eng, out, in0, scalar, in1, op0, op1):
        rd = [in0.key, in1.key]
        a = scalar.ap if isinstance(scalar, AV) else float(scalar)
        if isinstance(scalar, AV):
            rd.append(scalar.key)
        return self.add(eng, lambda e: e.scalar_tensor_tensor(out.ap, in0.ap, a, in1.ap, op0, op1),
                        reads=rd, writes=[out.key])

    def copy(self, eng, out, in_):
        if eng == "act":
            return self.act(out, in_, AF.Copy)
        return self.add(eng, lambda e: e.tensor_copy(out.ap, in_.ap), reads=[in_.key], writes=[out.key])

    def memset(self, eng, out, val):
        return self.add(eng, lambda e: e.memset(out.ap, val), reads=[], writes=[out.key])

    def reduce(self, eng, out, in_, op):
        return self.add(eng, lambda e: e.tensor_reduce(out.ap, in_.ap, AX.X, op), reads=[in_.key], writes=[out.key])

    def recip(self, out, in_):
        return self.add("dve", lambda e: e.reciprocal(out.ap, in_.ap), reads=[in_.key], writes=[out.key])

    def dma(self, q, out, in_, extra_reads=(), extra_writes=()):
        return self.add(q, lambda e: e.dma_start(out=out.ap, in_=in_.ap),
                        reads=[in_.key] + list(extra_reads), writes=[out.key] + list(extra_writes), dma=True)

    def emit(self):
        nc = self.nc
        ops = self.ops
        for op in ops:
            for d in op.deps:
                ops[d].needs = True
        engs = []
        for op in ops:
            if op.eng not in engs:
                engs.append(op.eng)
        sems = {e: self.stack.enter_context(nc.semaphore(f"s_{e}")) for e in engs}
        dsem = {}
        for e in engs:
            if any(o.dma and o.eng == e for o in ops):
                dsem[e] = [self.stack.enter_context(nc.semaphore(f"d_{e}{i}")) for i in range(self.NSLOT)]
        cnt = {e: 0 for e in engs}
        dcnt = {e: [0] * self.NSLOT for e in engs}
        dn = {e: 0 for e in engs}
        for op in ops:
            op.waits = []
            if op.dma:
                slot = dn[op.eng] % self.NSLOT
                dn[op.eng] += 1
                if dcnt[op.eng][slot] > 0:
                    op.waits.append((dsem[op.eng][slot], dcnt[op.eng][slot] * 16))
                dcnt[op.eng][slot] += 1
                op.sem = dsem[op.eng][slot]
                op.val = dcnt[op.eng][slot] * 16
            elif op.needs:
                cnt[op.eng] += 1
                op.sem = sems[op.eng]
                op.val = cnt[op.eng]
            else:
                op.sem = None
                op.val = None
        known = {e: {} for e in engs}
        for op in ops:
            k = known[op.eng]
            cand = list(op.waits) + [(ops[d].sem, ops[d].val) for d in sorted(op.deps)]
            best = {}
            for s, v in cand:
                if v > best.get(id(s), (None, 0))[1]:
                    best[id(s)] = (s, v)
            ws = []
            for sid, (s, v) in best.items():
                if k.get(sid, 0) >= v:
                    continue
                k[sid] = v
                ws.append((s, v))
            op.waits = ws
        final_waits = {}
        for e in engs:
            if e in dsem:
                final_waits[e] = [(dsem[e][i], dcnt[e][i] * 16) for i in range(self.NSLOT) if dcnt[e][i] > 0]
        by_eng = {e: [o for o in ops if o.eng == e] for e in engs}
        self.stats = {e: len(v) for e, v in by_eng.items()}
        self.nwaits = sum(len(o.waits) for o in ops)
        attr = {"pe": "tensor", "act": "scalar", "dve": "vector", "pool": "gpsimd", "sp": "sync"}
        with nc.Block() as block:
            for e in engs:
                def body(eng, _ops=by_eng[e], _e=e):
                    for o in _ops:
                        for s, v in o.waits:
                            eng.wait_ge(s, v)
                        ins = o.fn(eng)
                        if o.sem is not None:
                            ins.then_inc(o.sem, 16 if o.dma else 1)
                    for s, v in final_waits.get(_e, ()):
                        eng.wait_ge(s, v)
                getattr(block, attr[e])(body)


def _img(w_rows_cols):
    K, N = w_rows_cols.shape
    return w_rows_cols.reshape(K // 128, 128, N).transpose(1, 0, 2).reshape(128, -1)


def host_weight_image(inp, l):
    w_in = inp["w_in"][l]
    o = np.cumsum([0, 512, 512, 256, 128, 32, 512, 512, 512, 3072])
    u, zs, cq, ckv, kr, zmla, qm, zm, gl = [w_in[:, o[i]:o[i + 1]] for i in range(9)]
    z128 = np.zeros((1024, 128), np.float32)
    img = np.zeros((NCH, 128, CH), np.float32)

    def put_tiles(c, tiles):
        buf = np.zeros((1024, 512), np.float32)
        for i, t in enumerate(tiles):
            buf[:, i * 128:(i + 1) * 128] = t
        img[c] = _img(buf)

    rope = z128.copy()
    rope[:, 64:96] = kr
    rope_sw = z128.copy()
    rope_sw[:, 64:80] = kr[:, 16:32]
    rope_sw[:, 80:96] = kr[:, 0:16]
    put_tiles(C_IP0, [ckv, rope, rope_sw])
    put_tiles(C_IP1, [cq[:, 0:128], cq[:, 128:256]])
    put_tiles(C_IP2, [zmla[:, i * 128:(i + 1) * 128] for i in range(4)])
    put_tiles(C_IP3, [u[:, i * 128:(i + 1) * 128] for i in range(4)])
    put_tiles(C_IP4, [zs[:, i * 128:(i + 1) * 128] for i in range(4)])
    put_tiles(C_IP5, [qm[:, i * 128:(i + 1) * 128] for i in range(4)])
    put_tiles(C_IP6, [zm[:, i * 128:(i + 1) * 128] for i in range(4)])
    wm = inp["w_mem_kv"][l]
    img[C_MEMK] = _img(wm[:, 0:512])
    img[C_MEMV] = _img(wm[:, 512:1024])
    wukv = inp["w_ukv"][l].reshape(128, 8, 128)
    wk = wukv[:, :, 0:64].reshape(128, 512)
    wv = wukv[:, :, 64:128].reshape(128, 512)
    wuq = inp["w_uq"][l].reshape(256, 8, 96)
    wuq_sw = np.zeros((256, 8, 96), np.float32)
    wuq_sw[:, :, 64:80] = wuq[:, :, 80:96]
    wuq_sw[:, :, 80:96] = wuq[:, :, 64:80]
    img[C_UQKV] = np.concatenate([wk, wv, _img(wuq.reshape(256, 768)), _img(wuq_sw.reshape(256, 768))], axis=1)
    img[C_GLU] = _img(inp["w_glu"][l])
    img[C_PMEM] = _img(inp["p_mem"][l])
    ps_, pm_ = inp["p_ssm"][l], inp["p_mla"][l]
    for j in range(8):
        parts = [_img(gl[:, k * 1024 + j * 128: k * 1024 + (j + 1) * 128]) for k in range(3)]
        parts.append(_img(ps_[:, j * 128:(j + 1) * 128]))
        parts.append(_img(pm_[:, j * 128:(j + 1) * 128]))
        img[C_MG0 + j] = np.concatenate(parts, axis=1)
    wo = inp["w_out"][l]
    img[C_WO0] = _img(wo[:, 0:512])
    img[C_WO1] = _img(wo[:, 512:1024])
    return img


def host_small(inp, l):
    d = {}
    d["bgate"] = inp["b_gate"][l].reshape(3, 8, 128).transpose(2, 0, 1).reshape(128, 24)
    d["bglu"] = inp["b_glu"][l].reshape(8, 128).T
    d["qn"] = inp["mla_q_norm"][l].reshape(2, 128).T
    d["kvn"] = inp["mla_kv_norm"][l].reshape(1, 128).T
    sm = np.concatenate([d["bgate"], d["bglu"], d["qn"], d["kvn"]], axis=1)

    def pp(a):
        sh = a.shape
        return a.reshape((16, 2, 64) + sh[2:]).transpose((1, 2, 0) + tuple(range(3, len(sh) + 1))).reshape((128, 16) + sh[2:])

    are = pp(inp["ssm_a_re"][l])
    aim = pp(inp["ssm_a_im"][l])
    ldt = pp(np.broadcast_to(inp["ssm_log_dt"][l][:, None], (32, 64)))
    bre = pp(inp["ssm_b_re"][l])
    bim = pp(inp["ssm_b_im"][l])
    cre = pp(inp["ssm_c_re"][l].transpose(0, 2, 1))
    cim = pp(inp["ssm_c_im"][l].transpose(0, 2, 1))
    ssm = np.concatenate([are, aim, ldt, bre.reshape(128, 256), bim.reshape(128, 256),
                          cre.reshape(128, 256), cim.reshape(128, 256)], axis=1)
    dtab = np.broadcast_to(inp["ssm_d"][l].reshape(32, 1, 16), (32, 8, 16)).transpose(1, 2, 0).reshape(128, 32)
    small = np.concatenate([sm, ssm, dtab], axis=1).astype(np.float32)
    ln = np.stack([inp["ln_g"][l], inp["ln_b"][l]], axis=0).astype(np.float32)
    return small, ln


NSMALL = 35 + 1072 + 32


def host_consts():
    c = {}
    c["ident"] = np.eye(128, dtype=np.float32)
    big = np.zeros((128, 8, 240), np.float32)
    for g in range(8):
        for cc in range(16):
            big[g * 16 + cc, g, cc + 112] = 1.0
    c["big"] = big.reshape(128, 8 * 240).astype(ml_dtypes.bfloat16)
    mask = np.zeros((128, 4, 512), np.float32)
    k = np.arange(128)[:, None]
    q = np.arange(512)[None, :]
    for v in range(4):
        mask[:, v, :] = (q >= 128 * v + k)
    c["cmask"] = mask.reshape(128, 2048).astype(ml_dtypes.bfloat16)
    r = np.arange(128)
    tm = ((r[None, :] // 16) >= (r[:, None] // 16)).astype(np.float32)
    c["tmask"] = tm
    invf = np.zeros((128, 4), np.float32)
    fr = (10000.0 ** (-np.arange(0, 32, 2, dtype=np.float32) / 32)).astype(np.float32)
    for rr in range(32):
        invf[64 + rr, 0] = fr[rr % 16]
        invf[64 + rr, 1] = -1.0 if rr < 16 else 1.0
    c["invf"] = invf
    return c


def build(S, NL, dbg=False):
    NT = S // TT
    nc = bass.Bass("TRN2", target_bir_lowering=False)
    P = Prog(nc)

    def din(name, shape, dt=F32):
        return nc.dram_tensor(name, list(shape), dt, kind="ExternalInput").ap()

    x_in = din("x", [S, D])
    mem_in = din("mem", [MEM, D])
    pos_in = din("pos", [1, S], I32)
    wimg = din("wimg", [NL * NCH, 128, CH])
    small_in = din("small", [NL, 128, NSMALL])
    ln_in = din("lnp", [NL, 2, D])
    ident_in = din("ident", [128, 128])
    big_in = din("big", [128, 8 * 240], BF16)
    cmask_in = din("cmask", [128, 2048], BF16)
    tmask_in = din("tmask", [128, 128])
    invf_in = din("invf", [128, 4])
    y_out = nc.dram_tensor("y", [S, D], F32, kind="ExternalOutput").ap()

    def dscr(name, shape, dt):
        return nc.dram_tensor(name, list(shape), dt, kind="Internal").ap()

    wbf = dscr("wbf", [NL * NCH, 128, CH], BF16)
    ssmw = dscr("ssmw", [NL * 4, 128, CH], BF16)
    Kc = dscr("Kc", [8, 96, S], BF16)
    Vc = dscr("Vc", [S // 128, 128, 512], BF16)
    xa = dscr("xa", [S, D], F32)
    xb = dscr("xb", [S, D], F32)
    csd = dscr("csd", [2, 32, S], F32)
    dbg_t = {}
    if dbg:
        for nm in ("yssm", "ymla", "ymem", "mT2"):
            dbg_t[nm] = nc.dram_tensor("d_" + nm, [128, 4 * TT], BF16, kind="ExternalOutput").ap()
        dbg_t["gel"] = nc.dram_tensor("d_gel", [128, 4 * TT], BF16, kind="ExternalOutput").ap()

    with contextlib.ExitStack() as st:
        P.stack = st
        bank = [P.ps([128, 512], F32, f"bank{i}") for i in range(8)]
        rot = {"ab": 0, "s": 0, "nd": 0}

        def bank_ab():
            rot["ab"] ^= 1
            return bank[rot["ab"]]

        def bank_s():
            rot["s"] ^= 1
            return bank[2 + rot["s"]]

        def bank_nd():
            rot["nd"] ^= 1
            return bank[4 + 2 * rot["nd"]], bank[5 + 2 * rot["nd"]]

        ident = P.sb([128, 128], F32, "ident")
        big = P.sb([128, 8, 240], BF16, "big")
        cmask = P.sb([128, 4, 512], BF16, "cmask")
        tmask = P.sb([128, 128], F32, "tmask")
        invf = P.sb([128, 4], F32, "invf")
        ones_bf = P.sb([128, 128], BF16, "ones_bf")
        onesk = P.sb([128, 128], F32, "onesk")
        onesq = P.sb([128, 128], F32, "onesq")
        P.dma("sp", ident[:], AV(ident_in, "c_ident"))
        P.dma("sp", big.v(lambda t: t[:].rearrange("p a b -> p (a b)")), AV(big_in, "c_big"))
        P.dma("sp", cmask.v(lambda t: t[:].rearrange("p a b -> p (a b)")), AV(cmask_in, "c_cmask"))
        P.dma("sp", tmask[:], AV(tmask_in, "c_tmask"))
        P.dma("sp", invf[:], AV(invf_in, "c_invf"))
        P.memset("dve", ones_bf[:], 1.0)
        ident_bf = P.sb([128, 128], BF16, "ident_bf")
        P.copy("dve", ident_bf[:], ident[:])
        P.ts("dve", cmask.v(lambda t: t[:].rearrange("p a b -> p (a b)")), cmask.v(lambda t: t[:].rearrange("p a b -> p (a b)")),
             30000.0, ALU.mult, -30000.0, ALU.add)
        P.memset("dve", onesk[:], 1.0 / 128)
        P.memset("dve", onesq[:], 1.0 / 256)

        for c in range(NL * NCH):
            P.dma("pool", AV(wbf[c], ("wbf", c)), AV(wimg[c], "wimg"))

        NB = 3
        wring = [P.sb([128, CH], BF16, f"wring{i}") for i in range(NB)]
        stream = []
        for l in range(NL):
            stream += [("w", l, C_MEMK), ("w", l, C_MEMV)]
            for i in range(NT):
                stream += [("w", l, C_IP0), ("w", l, C_IP1), ("w", l, C_IP3), ("s", l, 0), ("s", l, 1),
                           ("w", l, C_IP2), ("s", l, 2), ("s", l, 3),
                           ("w", l, C_GLU), ("w", l, C_IP4), ("w", l, C_IP5), ("w", l, C_IP6)]
                stream += [("w", l, C_MG0 + j) for j in range(8)]
                stream += [("w", l, C_WO0), ("w", l, C_WO1)]
        wst = {"issued": 0, "taken": 0}

        def w_issue():
            n = wst["issued"]
            kind, l, c = stream[n]
            if kind == "w":
                src = AV(wbf[l * NCH + c], ("wbf", l * NCH + c))
            else:
                src = AV(ssmw[l * 4 + c], ("ssmw", l * 4 + c))
            P.dma("sp", wring[n % NB][:], src)
            wst["issued"] += 1

        def w_next(kind, l, c, keep=0):
            n = wst["taken"]
            assert stream[n] == (kind, l, c), (stream[n], kind, l, c)
            while wst["issued"] < min(n + NB - keep, len(stream)):
                w_issue()
            wst["taken"] += 1
            return wring[n % NB]

        small = P.sb([128, NSMALL], F32, "small")
        lng = P.sb([128, D], F32, "lng")
        lnb = P.sb([128, D], F32, "lnb")
        uqkv = P.sb([128, CH], BF16, "uqkv")
        memT = P.sb([128, 8, MEM], BF16, "memT")
        memKT = P.sb([128, 4, MEM], BF16, "memKT")
        memV = P.sb([128, 2, 512], BF16, "memV")
        xs = [P.sb([128, D], F32, f"xs{i}") for i in range(2)]
        xT = P.sb([128, 8, TT], BF16, "xT")
        f32t = [P.sb([128, TT], F32, f"f32t{i}") for i in range(5)]
        fi = {"i": 0}

        def ftmp():
            fi["i"] = (fi["i"] + 1) % len(f32t)
            return f32t[fi["i"]]

        cosT = P.sb([128, TT], F32, "cosT")
        sinS = P.sb([128, TT], F32, "sinS")
        ckvn = P.sb([128, TT], BF16, "ckvn")
        cqn = P.sb([128, 2, TT], BF16, "cqn")
        krb = P.sb([128, TT], BF16, "krb")
        Qh = [P.sb([128, TT], BF16, f"Qh{i}") for i in range(2)]
        Kst = Qh
        Kbuf = [P.sb([128, 4096], BF16, f"Kbuf{i}") for i in range(2)]
        Vbuf = [P.sb([128, 32, 128], BF16, f"Vbuf{i}") for i in range(1)]
        PT = [P.sb([128, TT], BF16, f"PT{i}") for i in range(3)]
        pti = {"i": 0}

        def ptnext():
            pti["i"] = (pti["i"] + 1) % 3
            return PT[pti["i"]]

        ymla = P.sb([128, 4, TT], BF16, "ymla")
        yssm = P.sb([128, 4, TT], BF16, "yssm")
        ymem = P.sb([128, 4, TT], BF16, "ymem")
        UTb = P.sb([128, 4, TT], BF16, "UTb")
        gel = UTb
        Vg = P.sb([128, 32, 64], BF16, "Vg")
        Yg = Vg
        Vst = Vg.v(lambda t: t[:].rearrange("p a b -> p (a b)").rearrange("p (s c) -> p s c", s=4))
        Hre = P.sb([128, 16, 64], F32, "Hre")
        Him = P.sb([128, 16, 64], F32, "Him")
        hpre = P.sb([128, 16, 64], BF16, "hpre")
        hpim = P.sb([128, 16, 64], BF16, "hpim")
        carry = P.sb([128, 2, 16], F32, "carry")
        a8pw = P.sb([128, 6, 2, 16], F32, "a8pw")
        qmb = P.sb([128, TT], BF16, "qmb")
        szm = P.sb([128, TT], F32, "szm")
        mT = P.sb([128, 8, TT], BF16, "mT")
        lnbuf = P.sb([128, 4, D], F32, "lnbuf")
        vln = lnbuf.v(lambda t: t[:, 0, :])
        junk = lnbuf.v(lambda t: t[:, 1, :])
        oln = lnbuf.v(lambda t: t[:, 2, :])
        st1 = lnbuf.v(lambda t: t[:, 0, :].rearrange("p (a k) -> p a k", k=64))
        st2 = lnbuf.v(lambda t: t[:, 1, :].rearrange("p (a k) -> p a k", k=64))
        st3 = lnbuf.v(lambda t: t[:, 2, :].rearrange("p (a k) -> p a k", k=64))
        stat = P.sb([128, 8], F32, "stat")
        g_t = [P.sb([128, 16, 16], F32, f"g_t{i}") for i in range(10)]
        pw = P.sb([128, 2, 16, 16], F32, "pw")
        pwd = P.sb([128, 2, 8, 16], F32, "pwd")
        planeL = [lnbuf.v(lambda t, r=r: t[:, 2 * r:2 * r + 2, :].rearrange("p a (b c) -> p (a b) c", c=128)) for r in range(2)]
        planeK = [Kbuf[r].v(lambda t: t[:].bitcast(F32).rearrange("p (a c) -> p a c", c=128)) for r in range(2)]
        planeX = xT.v(lambda t: t[:].rearrange("p a b -> p (a b)").bitcast(F32).rearrange("p (a c) -> p a c", c=128))
        planeM = mT.v(lambda t: t[:].rearrange("p a b -> p (a b)").bitcast(F32).rearrange("p (a c) -> p a c", c=128))
        pmem = P.sb([128, CH], BF16, "pmem")
        simg = Tile(None, Vbuf[0].key)
        simg.t = Vbuf[0].t[:].rearrange("p a b -> p (a b)")

        Dx = [x_in, xa, xb]
        xsA = [AV(Kbuf[s_ // 2].t[:].bitcast(F32)[:, (s_ % 2) * D:(s_ % 2 + 1) * D], Kbuf[s_ // 2].key) for s_ in range(4)]

        def sincos(o_sin, o_cos, ang, ta, tb, tc, td):
            C1 = 6.28125
            C2 = TWO_PI - 6.28125
            P.ts("dve", ta, ang, 1.0 / TWO_PI, ALU.mult, MAGIC, ALU.add)
            P.ts("dve", ta, ta, -MAGIC, ALU.add)
            P.stt("dve", tb, ta, -C1, ang, ALU.mult, ALU.add)
            P.stt("dve", tb, ta, -C2, tb, ALU.mult, ALU.add)
            P.ts("dve", tb, tb, 0.125, ALU.mult)
            P.tt("dve", tc, tb, tb, ALU.mult)
            a = [-1.0 / 6, 1.0 / 120, -1.0 / 5040, 1.0 / 362880]
            b = [-0.5, 1.0 / 24, -1.0 / 720, 1.0 / 40320]
            P.ts("dve", td, tc, a[3], ALU.mult)
            for cf in (a[2], a[1], a[0]):
                P.stt("dve", td, td, cf, tc, ALU.add, ALU.mult)
            P.stt("dve", o_sin, td, 1.0, tb, ALU.add, ALU.mult)
            P.ts("dve", td, tc, b[3], ALU.mult)
            for cf in (b[2], b[1], b[0]):
                P.stt("dve", td, td, cf, tc, ALU.add, ALU.mult)
            P.ts("dve", o_cos, td, 1.0, ALU.add)
            for _ in range(3):
                P.tt("dve", ta, o_sin, o_sin, ALU.mult)
                P.tt("dve", tb, o_cos, o_cos, ALU.mult)
                P.stt("dve", tc, o_sin, 2.0, o_cos, ALU.mult, ALU.mult)
                P.tt("dve", o_cos, tb, ta, ALU.subtract)
                P.copy("dve", o_sin, tc)

        def cmul(o_re, o_im, a_re, a_im, b_re, b_im, t1, t2):
            P.tt("dve", t1, a_re, b_re, ALU.mult)
            P.tt("dve", t2, a_im, b_im, ALU.mult)
            P.tt("dve", o_re, t1, t2, ALU.subtract)
            P.tt("dve", t1, a_re, b_im, ALU.mult)
            P.tt("dve", t2, a_im, b_re, ALU.mult)
            P.tt("dve", o_im, t1, t2, ALU.add)

        def rms_rstd(dst, ms_bank):
            P.ts("dve", dst, ms_bank, EPS, ALU.add)
            P.act(dst, dst, AF.Sqrt)
            P.recip(dst, dst)

        def silu_from_bank(bk):
            sg = ftmp()
            P.act(sg[:], bk[:], AF.Sigmoid)
            P.tt("dve", sg[:], sg[:], bk[:], ALU.mult)
            return sg

        rotc = {}

        def bank_rot(name, ids):
            k = rotc.get(name, -1) + 1
            rotc[name] = k
            return bank[ids[k % len(ids)]]

        def inproj_tile(wt, ti, banks=None):
            bk = bank_ab() if banks is None else bank_rot("ip" + str(banks), banks)
            for kc in range(8):
                P.mm(bk[:], wt.v(lambda t, kc=kc, ti=ti: t[:, kc * 512 + ti * 128: kc * 512 + (ti + 1) * 128]),
                     xT.v(lambda t, kc=kc: t[:, kc, :]), start=(kc == 0), stop=(kc == 7))
            return bk

        for i in range(NT):
            pi32 = f32t[0].v(lambda t: t[64:96, :].bitcast(I32))
            src = AV(bass.AP(pos_in.tensor, i * TT, [[0, 32], [1, TT]]), "pos")
            P.dma("pool", pi32, src)
            angt = f32t[1]
            P.copy("dve", angt[64:96, :], pi32)
            P.ts("dve", angt[64:96, :], angt[64:96, :], invf[64:96, 0:1], ALU.mult)
            sincos(sinS[64:96, :], cosT[64:96, :], angt[64:96, :], f32t[2][64:96, :], f32t[3][64:96, :], f32t[4][64:96, :], f32t[0][64:96, :])
            P.ts("dve", sinS[64:96, :], sinS[64:96, :], invf[64:96, 1:2], ALU.mult)
            P.dma("pool", AV(csd[0, :, i * TT:(i + 1) * TT], ("csd", i)), cosT[64:96, :])
            P.dma("pool", AV(csd[1, :, i * TT:(i + 1) * TT], ("csd", i)), sinS[64:96, :])
        for mb in range(2):
            P.dma("pool", xs[mb][:], AV(mem_in[mb * 128:(mb + 1) * 128, :], "mem"))
            for half in range(2):
                bk = bank_ab()
                for q in range(4):
                    kc = half * 4 + q
                    P.tr(bk[:, q * 128:(q + 1) * 128], xs[mb][:, kc * 128:(kc + 1) * 128], ident[:])
                P.copy("dve", memT.v(lambda t, half=half, mb=mb: t[:, half * 4:half * 4 + 4, mb * 128:(mb + 1) * 128]),
                       bk.v(lambda t: t[:].rearrange("p (a b) -> p a b", a=4)))

        for l in range(NL):
            xin = Dx[0] if l == 0 else Dx[1 + ((l - 1) % 2)]
            xout = y_out if l == NL - 1 else Dx[1 + (l % 2)]
            kin = "x0" if l == 0 else ("xs", (l - 1) % 2)
            kout = "y" if l == NL - 1 else ("xs", l % 2)

            P.dma("pool", small[:], AV(small_in[l], "small_in"))
            P.dma("pool", lng[:], AV(bass.AP(ln_in.tensor, (l * 2) * D, [[0, 128], [1, D]]), "ln_in"))
            P.dma("pool", lnb[:], AV(bass.AP(ln_in.tensor, (l * 2 + 1) * D, [[0, 128], [1, D]]), "ln_in"))
            P.dma("pool", uqkv[:], AV(wbf[l * NCH + C_UQKV], ("wbf", l * NCH + C_UQKV)))
            P.dma("pool", pmem[:], AV(wbf[l * NCH + C_PMEM], ("wbf", l * NCH + C_PMEM)))
            bgate = lambda k, j: small[:, k * 8 + j: k * 8 + j + 1]
            bglu = lambda n: small[:, 24 + n: 24 + n + 1]
            qn = lambda kc: small[:, 32 + kc: 33 + kc]
            kvn = small[:, 34:35]
            o0 = 35
            s_are = small[:, o0:o0 + 16]
            s_aim = small[:, o0 + 16:o0 + 32]
            s_ldt = small[:, o0 + 32:o0 + 48]
            s_bre = small.v(lambda t: t[:, o0 + 48:o0 + 304].rearrange("p (a c) -> p a c", c=16))
            s_bim = small.v(lambda t: t[:, o0 + 304:o0 + 560].rearrange("p (a c) -> p a c", c=16))
            s_cre = small.v(lambda t: t[:, o0 + 560:o0 + 816].rearrange("p (a c) -> p a c", c=16))
            s_cim = small.v(lambda t: t[:, o0 + 816:o0 + 1072].rearrange("p (a c) -> p a c", c=16))
            dtab = lambda g: small[:, o0 + 1072 + g: o0 + 1073 + g]

            wk = w_next("w", l, C_MEMK)
            for h in range(4):
                bk = bank_ab()
                for kc in range(8):
                    P.mm(bk[:, 0:MEM], wk.v(lambda t, kc=kc, h=h: t[:, kc * 512 + h * 128: kc * 512 + (h + 1) * 128]),
                         memT.v(lambda t, kc=kc: t[:, kc, :]), start=(kc == 0), stop=(kc == 7))
                P.copy("act", memKT.v(lambda t, h=h: t[:, h, :]), bk[:, 0:MEM])
            wv = w_next("w", l, C_MEMV)
            for mb in range(2):
                bk = bank_ab()
                for kc in range(8):
                    P.mm(bk[:], memT.v(lambda t, kc=kc, mb=mb: t[:, kc, mb * 128:(mb + 1) * 128]),
                         wv.v(lambda t, kc=kc: t[:, kc * 512:(kc + 1) * 512]), start=(kc == 0), stop=(kc == 7))
                P.copy("act", memV.v(lambda t, mb=mb: t[:, mb, :]), bk[:])

            T_ = [g_t[i].v(lambda t: t[:, :, 0]) for i in range(10)]
            dt_, lrdt, ang, mag, lbre, lbim, tA, tB, tC, tD = T_
            P.act(dt_, s_ldt, AF.Exp)
            P.tt("dve", lrdt, s_are, dt_, ALU.mult)
            P.tt("dve", ang, s_aim, dt_, ALU.mult)
            P.act(mag, lrdt, AF.Exp)
            sincos(tA, tB, ang, tC, tD, g_t[8].v(lambda t: t[:, :, 2]), g_t[9].v(lambda t: t[:, :, 2]))
            P.tt("dve", lbre, mag, tB, ALU.mult)
            P.tt("dve", lbim, mag, tA, ALU.mult)
            T2 = [g_t[i].v(lambda t: t[:, :, 1]) for i in range(10)]
            nr, den, fre, fim, u1, u2, ivre, ivim, m2, u3 = T2
            P.ts("dve", nr, lbre, -1.0, ALU.add)
            P.tt("dve", u1, s_are, s_are, ALU.mult)
            P.tt("dve", u2, s_aim, s_aim, ALU.mult)
            P.tt("dve", den, u1, u2, ALU.add)
            P.recip(den, den)
            P.tt("dve", u1, nr, s_are, ALU.mult)
            P.tt("dve", u2, lbim, s_aim, ALU.mult)
            P.tt("dve", u1, u1, u2, ALU.add)
            P.tt("dve", fre, u1, den, ALU.mult)
            P.tt("dve", u1, lbim, s_are, ALU.mult)
            P.tt("dve", u2, nr, s_aim, ALU.mult)
            P.tt("dve", u1, u1, u2, ALU.subtract)
            P.tt("dve", fim, u1, den, ALU.mult)
            P.tt("dve", u1, lbre, lbre, ALU.mult)
            P.tt("dve", u2, lbim, lbim, ALU.mult)
            P.tt("dve", m2, u1, u2, ALU.add)
            P.recip(m2, m2)
            P.tt("dve", ivre, lbre, m2, ALU.mult)
            P.tt("dve", u3, lbim, m2, ALU.mult)
            P.ts("dve", ivim, u3, -1.0, ALU.mult)
            pwv = lambda ri, n: pw.v(lambda t, ri=ri, n=n: t[:, ri, n + 7, :])
            P.memset("dve", pwv(0, 0), 1.0)
            P.memset("dve", pwv(1, 0), 0.0)
            for n in range(1, 9):
                cmul(pwv(0, n), pwv(1, n), pwv(0, n - 1), pwv(1, n - 1), lbre, lbim, u1, u2)
            for n in range(-1, -8, -1):
                cmul(pwv(0, n), pwv(1, n), pwv(0, n + 1), pwv(1, n + 1), ivre, ivim, u1, u2)
            for j in range(8):
                for ri in range(2):
                    P.copy("dve", pwd.v(lambda t, ri=ri, j=j: t[:, ri, j, :]), pwv(ri, 7 - j))
            a8 = lambda j, ri: a8pw.v(lambda t, j=j, ri=ri: t[:, j, ri, :])
            P.copy("dve", a8(0, 0), pwv(0, 8))
            P.copy("dve", a8(0, 1), pwv(1, 8))
            for j in range(1, 6):
                cmul(a8(j, 0), a8(j, 1), a8(j - 1, 0), a8(j - 1, 1), a8(j - 1, 0), a8(j - 1, 1), u1, u2)
            bbre, bbim, w1, w2 = g_t[6], g_t[7], g_t[8], g_t[9]
            bc3 = lambda av: AV(av.ap.unsqueeze(2).to_broadcast([128, 16, 16]), av.key)
            P.tt("dve", w1[:], s_bre, bc3(fre), ALU.mult)
            P.tt("dve", w2[:], s_bim, bc3(fim), ALU.mult)
            P.tt("dve", bbre[:], w1[:], w2[:], ALU.subtract)
            P.tt("dve", w1[:], s_bim, bc3(fre), ALU.mult)
            P.tt("dve", w2[:], s_bre, bc3(fim), ALU.mult)
            P.tt("dve", bbim[:], w1[:], w2[:], ALU.add)

            def v4(plane):
                return AV(plane.ap.rearrange("p a (j c) -> p a j c", c=16), plane.key)

            def pwb(tl, ri, lo):
                return tl.v(lambda t, ri=ri, lo=lo: t[:, ri, lo:lo + 8, :].rearrange("p n a -> p a n").unsqueeze(3).to_broadcast([128, 16, 8, 16]))

            def cb(av):
                return AV(av.ap.unsqueeze(2).to_broadcast([128, 16, 8, 16]), av.key)

            def cgen(dst, pt, lo, cr, ci, neg_im):
                ta, tb = v4(planeX), v4(planeM)
                P.tt("dve", ta, pwb(pt, 0, lo), cb(cr), ALU.mult)
                P.tt("dve", tb, pwb(pt, 1, lo), cb(ci), ALU.mult)
                P.tt("dve", v4(dst[0]), ta, tb, ALU.subtract)
                P.tt("dve", ta, pwb(pt, 0, lo), cb(ci), ALU.mult)
                P.tt("dve", tb, pwb(pt, 1, lo), cb(cr), ALU.mult)
                P.tt("dve", v4(dst[1]), ta, tb, ALU.add)
                if neg_im:
                    P.ts("dve", v4(dst[1]), v4(dst[1]), -1.0, ALU.mult)

            cgen(planeK, pw, 8, s_cre, s_cim, True)
            for ri in range(2):
                P.copy("dve", simg.v(lambda t, ri=ri: t[:, ri * 2048:(ri + 1) * 2048].rearrange("p (a c) -> p a c", c=128)), planeK[ri])
            P.dma("pool", AV(ssmw[l * 4 + 3], ("ssmw", l * 4 + 3)), simg[:])
            cgen(planeL, pwd, 0, bbre[:], bbim[:], False)
            cgen(planeK, pw, 0, s_cre, s_cim, True)
            for g in range(32):
                a, par = g // 2, g % 2
                bk = bank_ab()
                rs = slice(par * 64, (par + 1) * 64)
                P.mm(bk[:, 0:128], AV(planeL[0].ap[rs, a, :], planeL[0].key), AV(planeK[0].ap[rs, a, :], planeK[0].key), start=True, stop=False)
                P.mm(bk[:, 0:128], AV(planeL[1].ap[rs, a, :], planeL[1].key), AV(planeK[1].ap[rs, a, :], planeK[1].key), start=False, stop=True)
                tt_ = ftmp()
                P.tt("dve", tt_[:, 0:128], bk[:, 0:128], tmask[:], ALU.mult)
                P.stt("dve", simg[:, g * 128:(g + 1) * 128], ident[:], dtab(g), tt_[:, 0:128], ALU.mult, ALU.add)
            P.dma("pool", AV(ssmw[l * 4 + 2], ("ssmw", l * 4 + 2)), simg[:])
            for half in range(2):
                P.memset("dve", simg[:], 0.0)
                for gl_ in range(16):
                    g = half * 16 + gl_
                    a, par = g // 2, g % 2
                    rs = slice(par * 64, (par + 1) * 64)
                    bk = bank_ab()
                    for ri in range(2):
                        P.tr(bk[:, ri * 64:(ri + 1) * 64], AV(planeL[ri].ap[rs, a, :], planeL[ri].key),
                             AV(ident.t[rs, rs], ident.key))
                    P.copy("dve", simg.v(lambda t, gl_=gl_, par=par: t[:, gl_ * 256:(gl_ + 1) * 256].rearrange("p (r c) -> p r c", r=2)[:, :, par * 64:(par + 1) * 64]),
                           bk.v(lambda t: t[:, 0:128].rearrange("p (r c) -> p r c", r=2)))
                P.dma("pool", AV(ssmw[l * 4 + half], ("ssmw", l * 4 + half)), simg[:])
            P.memset("dve", carry[:], 0.0)

            for i in range(NT):
                t0 = i * TT
                nblk = 4 * (i + 1)
                for sub in range(4):
                    xsub = xs[sub % 2]
                    P.dma("pool", xsub[:], AV(xin[t0 + sub * 128: t0 + (sub + 1) * 128, :], (kin, i)))
                    for half in range(2):
                        bk = bank_rot("t0", [0, 1, 2, 3])
                        for q in range(4):
                            kc = half * 4 + q
                            P.tr(bk[:, q * 128:(q + 1) * 128], xsub[:, kc * 128:(kc + 1) * 128], ident[:])
                        P.copy("act" if half else "dve",
                               xT.v(lambda t, half=half, sub=sub: t[:, half * 4:half * 4 + 4, sub * 128:(sub + 1) * 128]),
                               bk.v(lambda t: t[:].rearrange("p (a b) -> p a b", a=4)))
                P.dma("pool", cosT[64:96, :], AV(csd[0, :, t0:t0 + TT], ("csd", i)))
                P.dma("pool", sinS[64:96, :], AV(csd[1, :, t0:t0 + TT], ("csd", i)))

                w0 = w_next("w", l, C_IP0)
                bk = inproj_tile(w0, 0)
                c32 = ftmp()
                P.copy("act", c32[:], bk[:])
                sq = ftmp()
                P.act(sq[:], bk[:], AF.Square)
                bk2 = bank_ab()
                P.mm(bk2[:], onesk[:], sq[:])
                rstd = ftmp()
                rms_rstd(rstd[:], bk2[:])
                P.stt("dve", ckvn[:], c32[:], kvn, rstd[:], ALU.mult, ALU.mult)
                bk = inproj_tile(w0, 1)
                bk2 = inproj_tile(w0, 2)
                ta = ftmp()
                tb = ftmp()
                P.tt("dve", ta[64:96, :], bk[64:96, :], cosT[64:96, :], ALU.mult)
                P.tt("dve", tb[64:96, :], bk2[64:96, :], sinS[64:96, :], ALU.mult)
                P.tt("dve", krb[64:96, :], ta[64:96, :], tb[64:96, :], ALU.add)
                for h in range(8):
                    bk = bank_ab()
                    P.mm(bk[0:64, :], uqkv[:, h * 64:(h + 1) * 64], ckvn[:])
                    ks = Kst[h % 2]
                    P.copy("act", ks[0:64, :], bk[0:64, :])
                    P.copy("dve", ks[64:96, :], krb[64:96, :])
                    P.dma("pool", AV(Kc[h, :, t0:t0 + TT], ("Kc", h, i)), ks[0:96, :])
                for sub in range(4):
                    bk = bank_ab()
                    P.mm(bk[:], ckvn[:, sub * 128:(sub + 1) * 128], uqkv[:, 512:1024])
                    P.copy("act" if sub % 2 else "dve", AV(Vst.ap[:, sub, :], Vst.key), bk[:])
                P.dma("pool", AV(Vc[4 * i:4 * i + 4].rearrange("b p c -> p b c"), ("Vc", i)), Vst)

                w1_ = w_next("w", l, C_IP1)
                c32s = []
                bk2 = bank_s()
                for kc in range(2):
                    bk = inproj_tile(w1_, kc)
                    c32 = ftmp()
                    P.copy("act", c32[:], bk[:])
                    sq = ftmp()
                    P.act(sq[:], bk[:], AF.Square)
                    P.mm(bk2[:], onesq[:], sq[:], start=(kc == 0), stop=(kc == 1))
                    c32s.append(c32)
                rstd = ftmp()
                rms_rstd(rstd[:], bk2[:])
                for kc in range(2):
                    P.stt("dve", cqn.v(lambda t, kc=kc: t[:, kc, :]), c32s[kc][:], qn(kc), rstd[:], ALU.mult, ALU.mult)
                def emit_q(h):
                    kb_ = Kbuf[h % 2]
                    P.dma("sp", kb_[0:96, 0:nblk * 128], AV(Kc[h, :, 0:nblk * 128], ("Kc", h, i)),
                          extra_reads=[("Kc", h, ii) for ii in range(i)])
                    bq = bank_ab()
                    bq2 = bank_ab()
                    for kc in range(2):
                        P.mm(bq[0:96, :], uqkv[:, 1024 + kc * 768 + h * 96: 1024 + kc * 768 + (h + 1) * 96],
                             cqn.v(lambda t, kc=kc: t[:, kc, :]), start=(kc == 0), stop=(kc == 1))
                    for kc in range(2):
                        P.mm(bq2[0:96, :], uqkv[:, 2560 + kc * 768 + h * 96: 2560 + kc * 768 + (h + 1) * 96],
                             cqn.v(lambda t, kc=kc: t[:, kc, :]), start=(kc == 0), stop=(kc == 1))
                    qh = Qh[h % 2]
                    P.copy("act", qh[0:64, :], bq[0:64, :])
                    ta = ftmp()
                    tb = ftmp()
                    P.tt("dve", ta[64:96, :], bq[64:96, :], cosT[64:96, :], ALU.mult)
                    P.tt("dve", tb[64:96, :], bq2[64:96, :], sinS[64:96, :], ALU.mult)
                    P.tt("dve", qh[64:96, :], ta[64:96, :], tb[64:96, :], ALU.add)

                emit_q(0)
                emit_q(1)

                w3_ = w_next("w", l, C_IP3)
                for t_ in range(4):
                    bk = inproj_tile(w3_, t_, banks=[0, 1, 2, 3])
                    P.copy("act" if t_ % 2 else "dve", UTb.v(lambda t, t_=t_: t[:, t_, :]), bk[:])
                for t_ in range(4):
                    bk = bank_rot("sel", [4, 5, 6, 7])
                    for gp in range(8):
                        for j in range(8):
                            P.mm(bk[:, gp * 64:(gp + 1) * 64],
                                 big.v(lambda t, gp=gp, j=j: t[:, gp, 112 - 16 * j: 240 - 16 * j]),
                                 UTb.v(lambda t, t_=t_, j=j: t[:, t_, :].rearrange("p (k j) -> p k j", j=8)[:, :, j]),
                                 start=(j == 0), stop=(j == 7))
                    P.copy("act" if t_ % 2 else "dve", Vg.v(lambda t, t_=t_: t[:, 8 * t_:8 * t_ + 8, :]),
                           bk.v(lambda t: t[:].rearrange("p (a b) -> p a b", a=8)))
                for half in range(2):
                    wg = w_next("s", l, half)
                    bre_, bim_ = bank[2 * half], bank[2 * half + 1]
                    for gl_ in range(16):
                        g = half * 16 + gl_
                        par = g % 2
                        al = gl_ // 2
                        for ri, bb in ((0, bre_), (1, bim_)):
                            P.mm(bb[:, al * 64:(al + 1) * 64],
                                 wg[:, gl_ * 256 + ri * 128: gl_ * 256 + (ri + 1) * 128],
                                 Vg.v(lambda t, g=g: t[:, g, :]), start=(par == 0), stop=(par == 1))
                    P.copy("act", Hre.v(lambda t, half=half: t[:, 8 * half:8 * half + 8, :]),
                           bre_.v(lambda t: t[:].rearrange("p (a b) -> p a b", a=8)))
                    P.copy("dve", Him.v(lambda t, half=half: t[:, 8 * half:8 * half + 8, :]),
                           bim_.v(lambda t: t[:].rearrange("p (a b) -> p a b", a=8)))
                cre_ = carry.v(lambda t: t[:, 0, :])
                cim_ = carry.v(lambda t: t[:, 1, :])
                h0r = Hre.v(lambda t: t[:, :, 0])
                h0i = Him.v(lambda t: t[:, :, 0])
                q1 = stat[:, 0:1]
                sA = g_t[0].v(lambda t: t[:, :, 2])
                sB = g_t[1].v(lambda t: t[:, :, 2])
                sC = g_t[2].v(lambda t: t[:, :, 2])
                sD = g_t[3].v(lambda t: t[:, :, 2])
                cmul(sC, sD, a8(0, 0), a8(0, 1), cre_, cim_, sA, sB)
                P.tt("dve", h0r, h0r, sC, ALU.add)
                P.tt("dve", h0i, h0i, sD, ALU.add)
                for j in range(6):
                    s_ = 1 << j
                    n_ = 64 - s_
                    arb = AV(a8(j, 0).ap.unsqueeze(2).to_broadcast([128, 16, n_]), a8pw.key)
                    aib = AV(a8(j, 1).ap.unsqueeze(2).to_broadcast([128, 16, n_]), a8pw.key)
                    lo = lambda tl: tl.v(lambda t: t[:, :, 0:n_]) if isinstance(tl, Tile) else AV(tl.ap[:, :, 0:n_], tl.key)
                    hi = lambda tl: tl.v(lambda t: t[:, :, s_:64])
                    P.tt("dve", lo(st1), lo(Hre), arb, ALU.mult)
                    P.tt("dve", lo(st2), lo(Him), aib, ALU.mult)
                    P.tt("dve", lo(st1), lo(st1), lo(st2), ALU.subtract)
                    P.tt("dve", lo(st2), lo(Him), arb, ALU.mult)
                    P.tt("dve", lo(st3), lo(Hre), aib, ALU.mult)
                    P.tt("dve", lo(st2), lo(st2), lo(st3), ALU.add)
                    P.tt("dve", hi(Hre), hi(Hre), lo(st1), ALU.add)
                    P.tt("dve", hi(Him), hi(Him), lo(st2), ALU.add)
                w2_ = w_next("w", l, C_IP2)
                for a in range(4):
                    vb = Vbuf[0]
                    P.dma("sp", vb.v(lambda t: t[:, 0:nblk, :]),
                          AV(Vc[0:nblk, :, a * 128:(a + 1) * 128].rearrange("b p c -> p b c"), ("Vc", i)),
                          extra_reads=[("Vc", ii) for ii in range(i)])
                    sz = szm
                    for hh in range(2):
                        h = 2 * a + hh
                        rows = slice(hh * 64, (hh + 1) * 64)
                        kb_ = Kbuf[h % 2]
                        qh = Qh[h % 2]
                        num, den = bank_nd()
                        pend = None

                        def pv(pn):
                            kb, pt, c0 = pn
                            P.mm(num[:, c0:TT], vb.v(lambda t, kb=kb: t[:, kb, :]), pt[:, c0:TT], start=(kb == 0), stop=(kb == nblk - 1))
                            P.mm(den[:, c0:TT], ones_bf[:], pt[:, c0:TT], start=(kb == 0), stop=(kb == nblk - 1))

                        for kb in range(nblk):
                            v_ = kb - 4 * i
                            c0 = 128 * v_ if v_ > 0 else 0
                            sbk = bank_s()
                            P.mm(sbk[:, c0:TT], kb_[0:96, kb * 128:(kb + 1) * 128], qh[0:96, c0:TT], start=True, stop=(v_ < 0))
                            if v_ >= 0:
                                P.mm(sbk[:, c0:TT], ident_bf[:], cmask.v(lambda t, v_=v_, c0=c0: t[:, v_, c0:TT]), start=False, stop=True)
                            pt = ptnext()
                            P.act(pt[:, c0:TT], sbk[:, c0:TT], AF.Exp, scale=MLA_SCALE)
                            if pend is not None:
                                pv(pend)
                            pend = (kb, pt, c0)
                        pv(pend)
                        if h + 2 < 8:
                            emit_q(h + 2)
                        if hh == 0:
                            zb = inproj_tile(w2_, a)
                            P.act(sz[:], zb[:], AF.Sigmoid)
                            P.tt("dve", sz[:], sz[:], zb[:], ALU.mult)
                        rd = ftmp()
                        P.recip(rd[rows, :], den[rows, :])
                        P.tt("dve", rd[rows, :], rd[rows, :], num[rows, :], ALU.mult)
                        P.tt("dve", ymla.v(lambda t, a=a, rows=rows: t[rows, a, :]), rd[rows, :], sz[rows, :], ALU.mult)


                P.copy("dve", hpre.v(lambda t: t[:, :, 0]), cre_)
                P.copy("dve", hpim.v(lambda t: t[:, :, 0]), cim_)
                P.copy("act", hpre.v(lambda t: t[:, :, 1:64]), Hre.v(lambda t: t[:, :, 0:63]))
                P.copy("act", hpim.v(lambda t: t[:, :, 1:64]), Him.v(lambda t: t[:, :, 0:63]))
                P.copy("dve", cre_, Hre.v(lambda t: t[:, :, 63]))
                P.copy("dve", cim_, Him.v(lambda t: t[:, :, 63]))
                wT = w_next("s", l, 2)
                wE = w_next("s", l, 3, keep=1)
                for t_ in range(4):
                    bk = bank_rot("y", [4, 5, 6, 7])
                    for gp in range(8):
                        g = 8 * t_ + gp
                        a, par = g // 2, g % 2
                        rs = slice(par * 64, (par + 1) * 64)
                        oo = bk[:, gp * 64:(gp + 1) * 64]
                        P.mm(oo, wT[:, g * 128:(g + 1) * 128], Vg.v(lambda t, g=g: t[:, g, :]), start=True, stop=False)
                        P.mm(oo, wE[rs, a * 128:(a + 1) * 128], hpre.v(lambda t, rs=rs, a=a: t[rs, a, :]), start=False, stop=False)
                        P.mm(oo, wE[rs, 2048 + a * 128: 2048 + (a + 1) * 128], hpim.v(lambda t, rs=rs, a=a: t[rs, a, :]), start=False, stop=True)
                    P.copy("act" if t_ % 2 else "dve", Yg.v(lambda t, t_=t_: t[:, 8 * t_:8 * t_ + 8, :]),
                           bk.v(lambda t: t[:].rearrange("p (a b) -> p a b", a=8)))
                for t_ in range(4):
                    bk = bank_rot("inv", [0, 1, 2, 3])
                    for ii in range(8):
                        for gp in range(8):
                            P.mm(bk.v(lambda t, ii=ii: t[:].rearrange("p (k i) -> p k i", i=8)[:, :, ii]),
                                 big.v(lambda t, ii=ii, gp=gp: t[:, ii, 112 - 16 * gp: 240 - 16 * gp]),
                                 Yg.v(lambda t, t_=t_, gp=gp: t[:, 8 * t_ + gp, :]), start=(gp == 0), stop=(gp == 7))
                    sq = ftmp()
                    P.act(sq[:], bk[:], AF.Square)
                    P.ts("dve", sq[:], sq[:], 0.044715, ALU.mult, 1.0, ALU.add)
                    P.tt("dve", sq[:], sq[:], bk[:], ALU.mult)
                    P.act(sq[:], sq[:], AF.Sigmoid, scale=1.5957691216057308)
                    P.tt("dve", gel.v(lambda t, t_=t_: t[:, t_, :]), sq[:], bk[:], ALU.mult)
                if dbg and l == 0 and i == 0:
                    P.dma("pool", AV(dbg_t["gel"], "d_gel"), gel.v(lambda t: t[:].rearrange("p a b -> p (a b)")))
                wgl = w_next("w", l, C_GLU)
                w4_ = w_next("w", l, C_IP4, keep=1)
                for n in range(4):
                    ba = bank_rot("glu", [2, 3, 4, 5])
                    bb = bank_rot("glu", [2, 3, 4, 5])
                    for kc in range(4):
                        P.mm(ba[:], wgl[:, kc * 1024 + n * 128: kc * 1024 + (n + 1) * 128], gel.v(lambda t, kc=kc: t[:, kc, :]),
                             start=(kc == 0), stop=(kc == 3))
                    for kc in range(4):
                        P.mm(bb[:], wgl[:, kc * 1024 + 512 + n * 128: kc * 1024 + 512 + (n + 1) * 128], gel.v(lambda t, kc=kc: t[:, kc, :]),
                             start=(kc == 0), stop=(kc == 3))
                    sg = ftmp()
                    P.act(sg[:], bb[:], AF.Sigmoid, bias=bglu(4 + n))
                    P.stt("dve", sg[:], ba[:], bglu(n), sg[:], ALU.add, ALU.mult)
                    zb = inproj_tile(w4_, n, banks=[0, 1, 6, 7])
                    sz = silu_from_bank(zb)
                    P.tt("dve", yssm.v(lambda t, n=n: t[:, n, :]), sg[:], sz[:], ALU.mult)

                w5_ = w_next("w", l, C_IP5)
                w6_ = w_next("w", l, C_IP6, keep=1)
                for h in range(4):
                    bk = inproj_tile(w5_, h)
                    P.copy("act", qmb[:], bk[:])
                    num, den = bank_nd()
                    for mb in range(2):
                        sbk = bank_s()
                        P.mm(sbk[:], memKT.v(lambda t, h=h, mb=mb: t[:, h, mb * 128:(mb + 1) * 128]), qmb[:])
                        pt = ptnext()
                        P.act(pt[:], sbk[:], AF.Exp, scale=MEM_SCALE)
                        P.mm(num[:], memV.v(lambda t, h=h, mb=mb: t[:, mb, h * 128:(h + 1) * 128]), pt[:], start=(mb == 0), stop=(mb == 1))
                        P.mm(den[:], ones_bf[:], pt[:], start=(mb == 0), stop=(mb == 1))
                    rd = ftmp()
                    P.recip(rd[:], den[:])
                    P.tt("dve", rd[:], rd[:], num[:], ALU.mult)
                    zb = inproj_tile(w6_, h)
                    sz = silu_from_bank(zb)
                    P.tt("dve", ymem.v(lambda t, h=h: t[:, h, :]), rd[:], sz[:], ALU.mult)

                wpm = pmem
                for j in range(8):
                    wm_ = w_next("w", l, C_MG0 + j)
                    macc = ftmp()
                    for k in range(3):
                        ysrc = (yssm, ymla, ymem)[k]
                        ba = bank_rot("mga", [4, 5, 6, 7])
                        for kc in range(4):
                            if k < 2:
                                lw = wm_[:, 3072 + k * 512 + kc * 128: 3072 + k * 512 + (kc + 1) * 128]
                            else:
                                lw = wpm[:, kc * 1024 + j * 128: kc * 1024 + (j + 1) * 128]
                            P.mm(ba[:], lw, ysrc.v(lambda t, kc=kc: t[:, kc, :]), start=(kc == 0), stop=(kc == 3))
                        bg = bank_rot("mgg", [0, 1, 2, 3])
                        for kc in range(8):
                            P.mm(bg[:], wm_[:, k * 1024 + kc * 128: k * 1024 + (kc + 1) * 128], xT.v(lambda t, kc=kc: t[:, kc, :]),
                                 start=(kc == 0), stop=(kc == 7))
                        gt = ftmp()
                        P.act(gt[:], bg[:], AF.Sigmoid, bias=bgate(k, j))
                        if k == 0:
                            P.tt("dve", macc[:], gt[:], ba[:], ALU.mult)
                        elif k == 1:
                            P.tt("dve", gt[:], gt[:], ba[:], ALU.mult)
                            P.tt("dve", macc[:], macc[:], gt[:], ALU.add)
                        else:
                            P.tt("dve", gt[:], gt[:], ba[:], ALU.mult)
                            P.tt("dve", mT.v(lambda t, j=j: t[:, j, :]), macc[:], gt[:], ALU.add)
                if dbg and l == 0 and i == 0:
                    for nm, tl in (("yssm", yssm), ("ymla", ymla), ("ymem", ymem)):
                        P.dma("pool", AV(dbg_t[nm], "d_" + nm), tl.v(lambda t: t[:].rearrange("p a b -> p (a b)")))
                    P.dma("pool", AV(dbg_t["mT2"], "d_mT2"), mT.v(lambda t: t[:, 0:4, :].rearrange("p a b -> p (a b)")))
                wo0 = w_next("w", l, C_WO0)
                wo1 = w_next("w", l, C_WO1, keep=1)
                for sub in range(4):
                    xsub = xs[sub % 2]
                    P.dma("pool", xsub[:], AV(xin[t0 + sub * 128: t0 + (sub + 1) * 128, :], (kin, i)))
                    for half, wo in ((0, wo0), (1, wo1)):
                        bo = bank_rot("wo", [0, 1, 2, 3])
                        for kc in range(8):
                            P.mm(bo[:], mT.v(lambda t, kc=kc, sub=sub: t[:, kc, sub * 128:(sub + 1) * 128]),
                                 wo[:, kc * 512:(kc + 1) * 512], start=(kc == 0), stop=(kc == 7))
                        P.stt("dve", AV(vln.ap[:, half * 512:(half + 1) * 512], vln.key), xsub[:, half * 512:(half + 1) * 512],
                              ALPHA, bo[:], ALU.mult, ALU.add)
                    P.reduce("dve", stat[:, 0:1], vln, ALU.add)
                    P.act(junk, vln, AF.Square)
                    P.reduce("dve", stat[:, 1:2], junk, ALU.add)
                    P.ts("dve", stat[:, 2:3], stat[:, 0:1], 1.0 / D, ALU.mult)
                    P.tt("dve", stat[:, 3:4], stat[:, 2:3], stat[:, 2:3], ALU.mult)
                    P.stt("dve", stat[:, 4:5], stat[:, 1:2], 1.0 / D, stat[:, 3:4], ALU.mult, ALU.subtract)
                    P.ts("dve", stat[:, 4:5], stat[:, 4:5], EPS, ALU.add)
                    P.act(stat[:, 5:6], stat[:, 4:5], AF.Sqrt)
                    P.recip(stat[:, 5:6], stat[:, 5:6])
                    P.ts("dve", oln, vln, stat[:, 2:3], ALU.subtract, stat[:, 5:6], ALU.mult)
                    P.tt("dve", oln, oln, lng[:], ALU.mult)
                    P.tt("dve", oln, oln, lnb[:], ALU.add)
                    P.dma("pool", AV(xout[t0 + sub * 128: t0 + (sub + 1) * 128, :], (kout, i)), oln)
        P.emit()
    return nc, P


_CACHE = {}


def _prep(inputs, S, NL):
    wimg = np.concatenate([host_weight_image(inputs, l) for l in range(NL)], axis=0)
    sm, ln = zip(*[host_small(inputs, l) for l in range(NL)])
    com = dict(host_consts())
    com["wimg"] = np.ascontiguousarray(wimg)
    com["small"] = np.ascontiguousarray(np.stack(sm))
    com["lnp"] = np.ascontiguousarray(np.stack(ln))
    return com


def run(inputs, S=SEQ, NL=DEPTH, cores=8, dbg=False, ret=None):
    inputs = {k: np.asarray(v) for k, v in inputs.items()}
    key = (S, NL, dbg)
    if key not in _CACHE:
        _CACHE[key] = build(S, NL, dbg)
    nc, P = _CACHE[key]
    com = _prep(inputs, S, NL)
    in_maps = []
    for b in range(cores):
        m = dict(com)
        m["x"] = np.ascontiguousarray(inputs["x"][b, :S]).astype(np.float32)
        m["mem"] = np.ascontiguousarray(inputs["mem"][b]).astype(np.float32)
        m["pos"] = np.ascontiguousarray(inputs["positions"][b, :S]).reshape(1, S).astype(np.int32)
        in_maps.append(m)
    import time as _t
    _t0 = _t.time()
    res = run_bass_kernel_spmd(nc, in_maps, core_ids=list(range(cores)))
    print("KERNEL spmd run seconds", _t.time() - _t0, "stats", P.stats, "waits", P.nwaits, flush=True)
    if ret is not None:
        ret.update({k: np.asarray(v) for k, v in res.results[0].items()})
    return np.stack([np.asarray(r["y"]) for r in res.results], axis=0).astype(np.float32)


def kernel(**inputs):
    return run(inputs)
```

```python
import contextlib
import math
import numpy as np
import ml_dtypes
import concourse.bass as bass
import concourse.mybir as mybir
from concourse.bass_utils import run_bass_kernel_spmd

F32 = mybir.dt.float32
BF16 = mybir.dt.bfloat16
I32 = mybir.dt.int32
AF = mybir.ActivationFunctionType
ALU = mybir.AluOpType
AX = mybir.AxisListType

D = 1024
SEQ = 4096
DEPTH = 4
MEM = 256
TT = 512
ALPHA = (2 * DEPTH) ** 0.25
EPS = 1e-5
MLA_SCALE = 96 ** -0.5
MEM_SCALE = 128 ** -0.5
TWO_PI = 2.0 * math.pi
MAGIC = 12582912.0
CH = 4096
(C_MEMK, C_MEMV, C_UQKV, C_IP0, C_IP1, C_IP2, C_IP3, C_GLU, C_IP4, C_IP5, C_IP6, C_PMEM) = range(12)
C_MG0 = 12
C_WO0 = 20
C_WO1 = 21
NCH = 22


class AV:
    __slots__ = ("ap", "key")

    def __init__(self, ap, key):
        self.ap = ap
        self.key = key


class Tile:
    def __init__(self, t, key):
        self.t = t
        self.key = key

    def __getitem__(self, idx):
        return AV(self.t[idx], self.key)

    def v(self, ap_fn):
        return AV(ap_fn(self.t), self.key)


class Op:
    __slots__ = ("id", "eng", "fn", "deps", "dma", "needs", "sem", "val", "waits")


class Prog:
    NSLOT = 8

    def __init__(self, nc):
        self.nc = nc
        self.ops = []
        self.res_w = {}
        self.res_r = {}
        self.stack = None
        self.ntile = 0

    def sb(self, shape, dt, name=None):
        self.ntile += 1
        name = name or f"t{self.ntile}"
        t = self.stack.enter_context(self.nc.sbuf_tensor("sb_" + name, list(shape), dt))
        return Tile(t, name)

    def ps(self, shape, dt, name):
        t = self.stack.enter_context(self.nc.psum_tensor(name, list(shape), dt))
        return Tile(t, name)

    def add(self, eng, fn, reads=(), writes=(), dma=False):
        op = Op()
        op.id = len(self.ops)
        op.eng = eng
        op.fn = fn
        op.dma = dma
        op.needs = False
        deps = set()
        for r in reads:
            if r in self.res_w:
                deps.add(self.res_w[r])
        for w in writes:
            if w in self.res_w:
                deps.add(self.res_w[w])
            for rr in self.res_r.get(w, ()):
                deps.add(rr)
        for r in reads:
            self.res_r.setdefault(r, []).append(op.id)
        for w in writes:
            self.res_w[w] = op.id
            self.res_r[w] = []
        deps.discard(op.id)
        if eng == "pe":
            deps = {d for d in deps if not (self.ops[d].eng == "pe" and not self.ops[d].dma)}
        op.deps = deps
        self.ops.append(op)
        return op.id

    def mm(self, out, lhsT, rhs, start=True, stop=True):
        return self.add("pe", lambda e: e.matmul(out.ap, lhsT.ap, rhs.ap, start=start, stop=stop),
                        reads=[lhsT.key, rhs.key], writes=[out.key])

    def tr(self, out, in_, ident):
        return self.add("pe", lambda e: e.transpose(out.ap, in_.ap, ident.ap),
                        reads=[in_.key, ident.key], writes=[out.key])

    def act(self, out, in_, func, bias=None, scale=None):
        kw = {}
        rd = [in_.key]
        if bias is not None:
            if isinstance(bias, AV):
                kw["bias"] = bias.ap
                rd.append(bias.key)
            else:
                kw["bias"] = float(bias)
        if scale is not None:
            if isinstance(scale, AV):
                kw["scale"] = scale.ap
                rd.append(scale.key)
            else:
                kw["scale"] = float(scale)
        return self.add("act", lambda e: e.activation(out.ap, in_.ap, func, **kw), reads=rd, writes=[out.key])

    def tt(self, eng, out, in0, in1, op):
        return self.add(eng, lambda e: e.tensor_tensor(out.ap, in0.ap, in1.ap, op),
                        reads=[in0.key, in1.key], writes=[out.key])

    def ts(self, eng, out, in0, s1, op0, s2=None, op1=None):
        rd = [in0.key]
        a1 = s1.ap if isinstance(s1, AV) else float(s1)
        if isinstance(s1, AV):
            rd.append(s1.key)
        a2 = None
        if s2 is not None:
            a2 = s2.ap if isinstance(s2, AV) else float(s2)
            if isinstance(s2, AV):
                rd.append(s2.key)
        if op1 is None:
            return self.add(eng, lambda e: e.tensor_scalar(out.ap, in0.ap, a1, None, op0), reads=rd, writes=[out.key])
        return self.add(eng, lambda e: e.tensor_scalar(out.ap, in0.ap, a1, a2, op0, op1), reads=rd, writes=[out.key])

    def stt(self, eng, out, in0, scalar, in1, op0, op1):
        rd = [in0.key, in1.key]
        a = scalar.ap if isinstance(scalar, AV) else float(scalar)
        if isinstance(scalar, AV):
            rd.append(scalar.key)
        return self.add(eng, lambda e: e.scalar_tensor_tensor(out.ap, in0.ap, a, in1.ap, op0, op1),
                        reads=rd, writes=[out.key])

    def copy(self, eng, out, in_):
        if eng == "act":
            return self.act(out, in_, AF.Copy)
        return self.add(eng, lambda e: e.tensor_copy(out.ap, in_.ap), reads=[in_.key], writes=[out.key])

    def memset(self, eng, out, val):
        return self.add(eng, lambda e: e.memset(out.ap, val), reads=[], writes=[out.key])

    def reduce(self, eng, out, in_, op):
        return self.add(eng, lambda e: e.tensor_reduce(out.ap, in_.ap, AX.X, op), reads=[in_.key], writes=[out.key])

    def recip(self, out, in_):
        return self.add("dve", lambda e: e.reciprocal(out.ap, in_.ap), reads=[in_.key], writes=[out.key])

    def dma(self, q, out, in_, extra_reads=(), extra_writes=()):
        return self.add(q, lambda e: e.dma_start(out=out.ap, in_=in_.ap),
                        reads=[in_.key] + list(extra_reads), writes=[out.key] + list(extra_writes), dma=True)

    def emit(self):
        nc = self.nc
        ops = self.ops
        for op in ops:
            for d in op.deps:
                ops[d].needs = True
        engs = []
        for op in ops:
            if op.eng not in engs:
                engs.append(op.eng)
        sems = {e: self.stack.enter_context(nc.semaphore(f"s_{e}")) for e in engs}
        dsem = {}
        for e in engs:
            if any(o.dma and o.eng == e for o in ops):
                dsem[e] = [self.stack.enter_context(nc.semaphore(f"d_{e}{i}")) for i in range(self.NSLOT)]
        cnt = {e: 0 for e in engs}
        dcnt = {e: [0] * self.NSLOT for e in engs}
        dn = {e: 0 for e in engs}
        for op in ops:
            op.waits = []
            if op.dma:
                slot = dn[op.eng] % self.NSLOT
                dn[op.eng] += 1
                if dcnt[op.eng][slot] > 0:
                    op.waits.append((dsem[op.eng][slot], dcnt[op.eng][slot] * 16))
                dcnt[op.eng][slot] += 1
                op.sem = dsem[op.eng][slot]
                op.val = dcnt[op.eng][slot] * 16
            elif op.needs:
                cnt[op.eng] += 1
                op.sem = sems[op.eng]
                op.val = cnt[op.eng]
            else:
                op.sem = None
                op.val = None
        known = {e: {} for e in engs}
        for op in ops:
            k = known[op.eng]
            cand = list(op.waits) + [(ops[d].sem, ops[d].val) for d in sorted(op.deps)]
            best = {}
            for s, v in cand:
                if v > best.get(id(s), (None, 0))[1]:
                    best[id(s)] = (s, v)
            ws = []
            for sid, (s, v) in best.items():
                if k.get(sid, 0) >= v:
                    continue
                k[sid] = v
                ws.append((s, v))
            op.waits = ws
        final_waits = {}
        for e in engs:
            if e in dsem:
                final_waits[e] = [(dsem[e][i], dcnt[e][i] * 16) for i in range(self.NSLOT) if dcnt[e][i] > 0]
        by_eng = {e: [o for o in ops if o.eng == e] for e in engs}
        self.stats = {e: len(v) for e, v in by_eng.items()}
        self.nwaits = sum(len(o.waits) for o in ops)
        attr = {"pe": "tensor", "act": "scalar", "dve": "vector", "pool": "gpsimd", "sp": "sync"}
        with nc.Block() as block:
            for e in engs:
                def body(eng, _ops=by_eng[e], _e=e):
                    for o in _ops:
                        for s, v in o.waits:
                            eng.wait_ge(s, v)
                        ins = o.fn(eng)
                        if o.sem is not None:
                            ins.then_inc(o.sem, 16 if o.dma else 1)
                    for s, v in final_waits.get(_e, ()):
                        eng.wait_ge(s, v)
                getattr(block, attr[e])(body)


def _img(w_rows_cols):
    K, N = w_rows_cols.shape
    return w_rows_cols.reshape(K // 128, 128, N).transpose(1, 0, 2).reshape(128, -1)


def host_weight_image(inp, l):
    w_in = inp["w_in"][l]
    o = np.cumsum([0, 512, 512, 256, 128, 32, 512, 512, 512, 3072])
    u, zs, cq, ckv, kr, zmla, qm, zm, gl = [w_in[:, o[i]:o[i + 1]] for i in range(9)]
    z128 = np.zeros((1024, 128), np.float32)
    img = np.zeros((NCH, 128, CH), np.float32)

    def put_tiles(c, tiles):
        buf = np.zeros((1024, 512), np.float32)
        for i, t in enumerate(tiles):
            buf[:, i * 128:(i + 1) * 128] = t
        img[c] = _img(buf)

    rope = z128.copy()
    rope[:, 64:96] = kr
    rope_sw = z128.copy()
    rope_sw[:, 64:80] = kr[:, 16:32]
    rope_sw[:, 80:96] = kr[:, 0:16]
    put_tiles(C_IP0, [ckv, rope, rope_sw])
    put_tiles(C_IP1, [cq[:, 0:128], cq[:, 128:256]])
    put_tiles(C_IP2, [zmla[:, i * 128:(i + 1) * 128] for i in range(4)])
    put_tiles(C_IP3, [u[:, i * 128:(i + 1) * 128] for i in range(4)])
    put_tiles(C_IP4, [zs[:, i * 128:(i + 1) * 128] for i in range(4)])
    put_tiles(C_IP5, [qm[:, i * 128:(i + 1) * 128] for i in range(4)])
    put_tiles(C_IP6, [zm[:, i * 128:(i + 1) * 128] for i in range(4)])
    wm = inp["w_mem_kv"][l]
    img[C_MEMK] = _img(wm[:, 0:512])
    img[C_MEMV] = _img(wm[:, 512:1024])
    wukv = inp["w_ukv"][l].reshape(128, 8, 128)
    wk = wukv[:, :, 0:64].reshape(128, 512)
    wv = wukv[:, :, 64:128].reshape(128, 512)
    wuq = inp["w_uq"][l].reshape(256, 8, 96)
    wuq_sw = np.zeros((256, 8, 96), np.float32)
    wuq_sw[:, :, 64:80] = wuq[:, :, 80:96]
    wuq_sw[:, :, 80:96] = wuq[:, :, 64:80]
    img[C_UQKV] = np.concatenate([wk, wv, _img(wuq.reshape(256, 768)), _img(wuq_sw.reshape(256, 768))], axis=1)
    img[C_GLU] = _img(inp["w_glu"][l])
    img[C_PMEM] = _img(inp["p_mem"][l])
    ps_, pm_ = inp["p_ssm"][l], inp["p_mla"][l]
    for j in range(8):
        parts = [_img(gl[:, k * 1024 + j * 128: k * 1024 + (j + 1) * 128]) for k in range(3)]
        parts.append(_img(ps_[:, j * 128:(j + 1) * 128]))
        parts.append(_img(pm_[:, j * 128:(j + 1) * 128]))
        img[C_MG0 + j] = np.concatenate(parts, axis=1)
    wo = inp["w_out"][l]
    img[C_WO0] = _img(wo[:, 0:512])
    img[C_WO1] = _img(wo[:, 512:1024])
    return img


def host_small(inp, l):
    d = {}
    d["bgate"] = inp["b_gate"][l].reshape(3, 8, 128).transpose(2, 0, 1).reshape(128, 24)
    d["bglu"] = inp["b_glu"][l].reshape(8, 128).T
    d["qn"] = inp["mla_q_norm"][l].reshape(2, 128).T
    d["kvn"] = inp["mla_kv_norm"][l].reshape(1, 128).T
    sm = np.concatenate([d["bgate"], d["bglu"], d["qn"], d["kvn"]], axis=1)

    def pp(a):
        sh = a.shape
        return a.reshape((16, 2, 64) + sh[2:]).transpose((1, 2, 0) + tuple(range(3, len(sh) + 1))).reshape((128, 16) + sh[2:])

    are = pp(inp["ssm_a_re"][l])
    aim = pp(inp["ssm_a_im"][l])
    ldt = pp(np.broadcast_to(inp["ssm_log_dt"][l][:, None], (32, 64)))
    bre = pp(inp["ssm_b_re"][l])
    bim = pp(inp["ssm_b_im"][l])
    cre = pp(inp["ssm_c_re"][l].transpose(0, 2, 1))
    cim = pp(inp["ssm_c_im"][l].transpose(0, 2, 1))
    ssm = np.concatenate([are, aim, ldt, bre.reshape(128, 256), bim.reshape(128, 256),
                          cre.reshape(128, 256), cim.reshape(128, 256)], axis=1)
    dtab = np.broadcast_to(inp["ssm_d"][l].reshape(32, 1, 16), (32, 8, 16)).transpose(1, 2, 0).reshape(128, 32)
    small = np.concatenate([sm, ssm, dtab], axis=1).astype(np.float32)
    ln = np.stack([inp["ln_g"][l], inp["ln_b"][l]], axis=0).astype(np.float32)
    return small, ln


NSMALL = 35 + 1072 + 32


def host_consts():
    c = {}
    c["ident"] = np.eye(128, dtype=np.float32)
    big = np.zeros((128, 8, 240), np.float32)
    for g in range(8):
        for cc in range(16):
            big[g * 16 + cc, g, cc + 112] = 1.0
    c["big"] = big.reshape(128, 8 * 240).astype(ml_dtypes.bfloat16)
    mask = np.zeros((128, 4, 512), np.float32)
    k = np.arange(128)[:, None]
    q = np.arange(512)[None, :]
    for v in range(4):
        mask[:, v, :] = (q >= 128 * v + k)
    c["cmask"] = mask.reshape(128, 2048).astype(ml_dtypes.bfloat16)
    r = np.arange(128)
    tm = ((r[None, :] // 16) >= (r[:, None] // 16)).astype(np.float32)
    c["tmask"] = tm
    invf = np.zeros((128, 4), np.float32)
    fr = (10000.0 ** (-np.arange(0, 32, 2, dtype=np.float32) / 32)).astype(np.float32)
    for rr in range(32):
        invf[64 + rr, 0] = fr[rr % 16]
        invf[64 + rr, 1] = -1.0 if rr < 16 else 1.0
    c["invf"] = invf
    return c


def build(S, NL, dbg=False):
    NT = S // TT
    nc = bass.Bass("TRN2", target_bir_lowering=False)
    P = Prog(nc)

    def din(name, shape, dt=F32):
        return nc.dram_tensor(name, list(shape), dt, kind="ExternalInput").ap()

    x_in = din("x", [S, D])
    mem_in = din("mem", [MEM, D])
    pos_in = din("pos", [1, S], I32)
    wimg = din("wimg", [NL * NCH, 128, CH])
    small_in = din("small", [NL, 128, NSMALL])
    ln_in = din("lnp", [NL, 2, D])
    ident_in = din("ident", [128, 128])
    big_in = din("big", [128, 8 * 240], BF16)
    cmask_in = din("cmask", [128, 2048], BF16)
    tmask_in = din("tmask", [128, 128])
    invf_in = din("invf", [128, 4])
    y_out = nc.dram_tensor("y", [S, D], F32, kind="ExternalOutput").ap()

    def dscr(name, shape, dt):
        return nc.dram_tensor(name, list(shape), dt, kind="Internal").ap()

    wbf = dscr("wbf", [NL * NCH, 128, CH], BF16)
    ssmw = dscr("ssmw", [NL * 4, 128, CH], BF16)
    Kc = dscr("Kc", [8, 96, S], BF16)
    Vc = dscr("Vc", [S // 128, 128, 512], BF16)
    xa = dscr("xa", [S, D], F32)
    xb = dscr("xb", [S, D], F32)
    csd = dscr("csd", [2, 32, S], F32)
    dbg_t = {}
    if dbg:
        for nm in ("yssm", "ymla", "ymem", "mT2"):
            dbg_t[nm] = nc.dram_tensor("d_" + nm, [128, 4 * TT], BF16, kind="ExternalOutput").ap()
        dbg_t["gel"] = nc.dram_tensor("d_gel", [128, 4 * TT], BF16, kind="ExternalOutput").ap()

    with contextlib.ExitStack() as st:
        P.stack = st
        bank = [P.ps([128, 512], F32, f"bank{i}") for i in range(8)]
        rot = {"ab": 0, "s": 0, "nd": 0}

        def bank_ab():
            rot["ab"] ^= 1
            return bank[rot["ab"]]

        def bank_s():
            rot["s"] ^= 1
            return bank[2 + rot["s"]]

        def bank_nd():
            rot["nd"] ^= 1
            return bank[4 + 2 * rot["nd"]], bank[5 + 2 * rot["nd"]]

        ident = P.sb([128, 128], F32, "ident")
        big = P.sb([128, 8, 240], BF16, "big")
        cmask = P.sb([128, 4, 512], BF16, "cmask")
        tmask = P.sb([128, 128], F32, "tmask")
        invf = P.sb([128, 4], F32, "invf")
        ones_bf = P.sb([128, 128], BF16, "ones_bf")
        onesk = P.sb([128, 128], F32, "onesk")
        onesq = P.sb([128, 128], F32, "onesq")
        P.dma("sp", ident[:], AV(ident_in, "c_ident"))
        P.dma("sp", big.v(lambda t: t[:].rearrange("p a b -> p (a b)")), AV(big_in, "c_big"))
        P.dma("sp", cmask.v(lambda t: t[:].rearrange("p a b -> p (a b)")), AV(cmask_in, "c_cmask"))
        P.dma("sp", tmask[:], AV(tmask_in, "c_tmask"))
        P.dma("sp", invf[:], AV(invf_in, "c_invf"))
        P.memset("dve", ones_bf[:], 1.0)
        P.memset("dve", onesk[:], 1.0 / 128)
        P.memset("dve", onesq[:], 1.0 / 256)

        for c in range(NL * NCH):
            P.dma("pool", AV(wbf[c], ("wbf", c)), AV(wimg[c], "wimg"))

        NB = 3
        wring = [P.sb([128, CH], BF16, f"wring{i}") for i in range(NB)]
        stream = []
        for l in range(NL):
            stream += [("w", l, C_MEMK), ("w", l, C_MEMV)]
            for i in range(NT):
                stream += [("w", l, C_IP0), ("w", l, C_IP1), ("w", l, C_IP3), ("s", l, 0), ("s", l, 1),
                           ("w", l, C_IP2), ("s", l, 2), ("s", l, 3),
                           ("w", l, C_GLU), ("w", l, C_IP4), ("w", l, C_IP5), ("w", l, C_IP6)]
                stream += [("w", l, C_MG0 + j) for j in range(8)]
                stream += [("w", l, C_WO0), ("w", l, C_WO1)]
        wst = {"issued": 0, "taken": 0}

        def w_issue():
            n = wst["issued"]
            kind, l, c = stream[n]
            if kind == "w":
                src = AV(wbf[l * NCH + c], ("wbf", l * NCH + c))
            else:
                src = AV(ssmw[l * 4 + c], ("ssmw", l * 4 + c))
            P.dma("sp", wring[n % NB][:], src)
            wst["issued"] += 1

        def w_next(kind, l, c, keep=0):
            n = wst["taken"]
            assert stream[n] == (kind, l, c), (stream[n], kind, l, c)
            while wst["issued"] < min(n + NB - keep, len(stream)):
                w_issue()
            wst["taken"] += 1
            return wring[n % NB]

        small = P.sb([128, NSMALL], F32, "small")
        lng = P.sb([128, D], F32, "lng")
        lnb = P.sb([128, D], F32, "lnb")
        uqkv = P.sb([128, CH], BF16, "uqkv")
        memT = P.sb([128, 8, MEM], BF16, "memT")
        memKT = P.sb([128, 4, MEM], BF16, "memKT")
        memV = P.sb([128, 2, 512], BF16, "memV")
        xs = [P.sb([128, D], F32, f"xs{i}") for i in range(2)]
        xT = P.sb([128, 8, TT], BF16, "xT")
        f32t = [P.sb([128, TT], F32, f"f32t{i}") for i in range(5)]
        fi = {"i": 0}

        def ftmp():
            fi["i"] = (fi["i"] + 1) % len(f32t)
            return f32t[fi["i"]]

        cosT = P.sb([128, TT], F32, "cosT")
        sinS = P.sb([128, TT], F32, "sinS")
        ckvn = P.sb([128, TT], BF16, "ckvn")
        cqn = P.sb([128, 2, TT], BF16, "cqn")
        krb = P.sb([128, TT], BF16, "krb")
        Qh = [P.sb([128, TT], BF16, f"Qh{i}") for i in range(2)]
        Kst = Qh
        Kbuf = [P.sb([128, 4096], BF16, f"Kbuf{i}") for i in range(2)]
        Vbuf = [P.sb([128, 32, 128], BF16, f"Vbuf{i}") for i in range(1)]
        PT = [P.sb([128, TT], BF16, f"PT{i}") for i in range(3)]
        pti = {"i": 0}

        def ptnext():
            pti["i"] = (pti["i"] + 1) % 3
            return PT[pti["i"]]

        ymla = P.sb([128, 4, TT], BF16, "ymla")
        yssm = P.sb([128, 4, TT], BF16, "yssm")
        ymem = P.sb([128, 4, TT], BF16, "ymem")
        UTb = P.sb([128, 4, TT], BF16, "UTb")
        gel = UTb
        Vg = P.sb([128, 32, 64], BF16, "Vg")
        Yg = Vg
        Vst = Vg.v(lambda t: t[:].rearrange("p a b -> p (a b)").rearrange("p (s c) -> p s c", s=4))
        Hre = P.sb([128, 16, 64], F32, "Hre")
        Him = P.sb([128, 16, 64], F32, "Him")
        hpre = P.sb([128, 16, 64], BF16, "hpre")
        hpim = P.sb([128, 16, 64], BF16, "hpim")
        carry = P.sb([128, 2, 16], F32, "carry")
        a8pw = P.sb([128, 6, 2, 16], F32, "a8pw")
        qmb = P.sb([128, TT], BF16, "qmb")
        szm = P.sb([128, TT], F32, "szm")
        mT = P.sb([128, 8, TT], BF16, "mT")
        lnbuf = P.sb([128, 4, D], F32, "lnbuf")
        vln = lnbuf.v(lambda t: t[:, 0, :])
        junk = lnbuf.v(lambda t: t[:, 1, :])
        oln = lnbuf.v(lambda t: t[:, 2, :])
        st1 = lnbuf.v(lambda t: t[:, 0, :].rearrange("p (a k) -> p a k", k=64))
        st2 = lnbuf.v(lambda t: t[:, 1, :].rearrange("p (a k) -> p a k", k=64))
        st3 = lnbuf.v(lambda t: t[:, 2, :].rearrange("p (a k) -> p a k", k=64))
        stat = P.sb([128, 8], F32, "stat")
        g_t = [P.sb([128, 16, 16], F32, f"g_t{i}") for i in range(10)]
        pw = P.sb([128, 2, 16, 16], F32, "pw")
        pwd = P.sb([128, 2, 8, 16], F32, "pwd")
        planeL = [lnbuf.v(lambda t, r=r: t[:, 2 * r:2 * r + 2, :].rearrange("p a (b c) -> p (a b) c", c=128)) for r in range(2)]
        planeK = [Kbuf[r].v(lambda t: t[:].bitcast(F32).rearrange("p (a c) -> p a c", c=128)) for r in range(2)]
        planeX = xT.v(lambda t: t[:].rearrange("p a b -> p (a b)").bitcast(F32).rearrange("p (a c) -> p a c", c=128))
        planeM = mT.v(lambda t: t[:].rearrange("p a b -> p (a b)").bitcast(F32).rearrange("p (a c) -> p a c", c=128))
        pmem = P.sb([128, CH], BF16, "pmem")
        simg = Tile(None, Vbuf[0].key)
        simg.t = Vbuf[0].t[:].rearrange("p a b -> p (a b)")

        Dx = [x_in, xa, xb]
        xsA = [AV(Kbuf[s_ // 2].t[:].bitcast(F32)[:, (s_ % 2) * D:(s_ % 2 + 1) * D], Kbuf[s_ // 2].key) for s_ in range(4)]

        def sincos(o_sin, o_cos, ang, ta, tb, tc, td):
            C1 = 6.28125
            C2 = TWO_PI - 6.28125
            P.ts("dve", ta, ang, 1.0 / TWO_PI, ALU.mult, MAGIC, ALU.add)
            P.ts("dve", ta, ta, -MAGIC, ALU.add)
            P.stt("dve", tb, ta, -C1, ang, ALU.mult, ALU.add)
            P.stt("dve", tb, ta, -C2, tb, ALU.mult, ALU.add)
            P.ts("dve", tb, tb, 0.125, ALU.mult)
            P.tt("dve", tc, tb, tb, ALU.mult)
            a = [-1.0 / 6, 1.0 / 120, -1.0 / 5040, 1.0 / 362880]
            b = [-0.5, 1.0 / 24, -1.0 / 720, 1.0 / 40320]
            P.ts("dve", td, tc, a[3], ALU.mult)
            for cf in (a[2], a[1], a[0]):
                P.stt("dve", td, td, cf, tc, ALU.add, ALU.mult)
            P.stt("dve", o_sin, td, 1.0, tb, ALU.add, ALU.mult)
            P.ts("dve", td, tc, b[3], ALU.mult)
            for cf in (b[2], b[1], b[0]):
                P.stt("dve", td, td, cf, tc, ALU.add, ALU.mult)
            P.ts("dve", o_cos, td, 1.0, ALU.add)
            for _ in range(3):
                P.tt("dve", ta, o_sin, o_sin, ALU.mult)
                P.tt("dve", tb, o_cos, o_cos, ALU.mult)
                P.stt("dve", tc, o_sin, 2.0, o_cos, ALU.mult, ALU.mult)
                P.tt("dve", o_cos, tb, ta, ALU.subtract)
                P.copy("dve", o_sin, tc)

        def cmul(o_re, o_im, a_re, a_im, b_re, b_im, t1, t2):
            P.tt("dve", t1, a_re, b_re, ALU.mult)
            P.tt("dve", t2, a_im, b_im, ALU.mult)
            P.tt("dve", o_re, t1, t2, ALU.subtract)
            P.tt("dve", t1, a_re, b_im, ALU.mult)
            P.tt("dve", t2, a_im, b_re, ALU.mult)
            P.tt("dve", o_im, t1, t2, ALU.add)

        def rms_rstd(dst, ms_bank):
            P.ts("dve", dst, ms_bank, EPS, ALU.add)
            P.act(dst, dst, AF.Sqrt)
            P.recip(dst, dst)

        def silu_from_bank(bk):
            sg = ftmp()
            P.act(sg[:], bk[:], AF.Sigmoid)
            P.tt("dve", sg[:], sg[:], bk[:], ALU.mult)
            return sg

        rotc = {}

        def bank_rot(name, ids):
            k = rotc.get(name, -1) + 1
            rotc[name] = k
            return bank[ids[k % len(ids)]]

        def inproj_tile(wt, ti, banks=None):
            bk = bank_ab() if banks is None else bank_rot("ip" + str(banks), banks)
            for kc in range(8):
                P.mm(bk[:], wt.v(lambda t, kc=kc, ti=ti: t[:, kc * 512 + ti * 128: kc * 512 + (ti + 1) * 128]),
                     xT.v(lambda t, kc=kc: t[:, kc, :]), start=(kc == 0), stop=(kc == 7))
            return bk

        for i in range(NT):
            pi32 = f32t[0].v(lambda t: t[64:96, :].bitcast(I32))
            src = AV(bass.AP(pos_in.tensor, i * TT, [[0, 32], [1, TT]]), "pos")
            P.dma("pool", pi32, src)
            angt = f32t[1]
            P.copy("dve", angt[64:96, :], pi32)
            P.ts("dve", angt[64:96, :], angt[64:96, :], invf[64:96, 0:1], ALU.mult)
            sincos(sinS[64:96, :], cosT[64:96, :], angt[64:96, :], f32t[2][64:96, :], f32t[3][64:96, :], f32t[4][64:96, :], f32t[0][64:96, :])
            P.ts("dve", sinS[64:96, :], sinS[64:96, :], invf[64:96, 1:2], ALU.mult)
            P.dma("pool", AV(csd[0, :, i * TT:(i + 1) * TT], ("csd", i)), cosT[64:96, :])
            P.dma("pool", AV(csd[1, :, i * TT:(i + 1) * TT], ("csd", i)), sinS[64:96, :])
        for mb in range(2):
            P.dma("pool", xs[mb][:], AV(mem_in[mb * 128:(mb + 1) * 128, :], "mem"))
            for half in range(2):
                bk = bank_ab()
                for q in range(4):
                    kc = half * 4 + q
                    P.tr(bk[:, q * 128:(q + 1) * 128], xs[mb][:, kc * 128:(kc + 1) * 128], ident[:])
                P.copy("dve", memT.v(lambda t, half=half, mb=mb: t[:, half * 4:half * 4 + 4, mb * 128:(mb + 1) * 128]),
                       bk.v(lambda t: t[:].rearrange("p (a b) -> p a b", a=4)))

        for l in range(NL):
            xin = Dx[0] if l == 0 else Dx[1 + ((l - 1) % 2)]
            xout = y_out if l == NL - 1 else Dx[1 + (l % 2)]
            kin = "x0" if l == 0 else ("xs", (l - 1) % 2)
            kout = "y" if l == NL - 1 else ("xs", l % 2)

            P.dma("pool", small[:], AV(small_in[l], "small_in"))
            P.dma("pool", lng[:], AV(bass.AP(ln_in.tensor, (l * 2) * D, [[0, 128], [1, D]]), "ln_in"))
            P.dma("pool", lnb[:], AV(bass.AP(ln_in.tensor, (l * 2 + 1) * D, [[0, 128], [1, D]]), "ln_in"))
            P.dma("pool", uqkv[:], AV(wbf[l * NCH + C_UQKV], ("wbf", l * NCH + C_UQKV)))
            P.dma("pool", pmem[:], AV(wbf[l * NCH + C_PMEM], ("wbf", l * NCH + C_PMEM)))
            bgate = lambda k, j: small[:, k * 8 + j: k * 8 + j + 1]
            bglu = lambda n: small[:, 24 + n: 24 + n + 1]
            qn = lambda kc: small[:, 32 + kc: 33 + kc]
            kvn = small[:, 34:35]
            o0 = 35
            s_are = small[:, o0:o0 + 16]
            s_aim = small[:, o0 + 16:o0 + 32]
            s_ldt = small[:, o0 + 32:o0 + 48]
            s_bre = small.v(lambda t: t[:, o0 + 48:o0 + 304].rearrange("p (a c) -> p a c", c=16))
            s_bim = small.v(lambda t: t[:, o0 + 304:o0 + 560].rearrange("p (a c) -> p a c", c=16))
            s_cre = small.v(lambda t: t[:, o0 + 560:o0 + 816].rearrange("p (a c) -> p a c", c=16))
            s_cim = small.v(lambda t: t[:, o0 + 816:o0 + 1072].rearrange("p (a c) -> p a c", c=16))
            dtab = lambda g: small[:, o0 + 1072 + g: o0 + 1073 + g]

            wk = w_next("w", l, C_MEMK)
            for h in range(4):
                bk = bank_ab()
                for kc in range(8):
                    P.mm(bk[:, 0:MEM], wk.v(lambda t, kc=kc, h=h: t[:, kc * 512 + h * 128: kc * 512 + (h + 1) * 128]),
                         memT.v(lambda t, kc=kc: t[:, kc, :]), start=(kc == 0), stop=(kc == 7))
                P.copy("act", memKT.v(lambda t, h=h: t[:, h, :]), bk[:, 0:MEM])
            wv = w_next("w", l, C_MEMV)
            for mb in range(2):
                bk = bank_ab()
                for kc in range(8):
                    P.mm(bk[:], memT.v(lambda t, kc=kc, mb=mb: t[:, kc, mb * 128:(mb + 1) * 128]),
                         wv.v(lambda t, kc=kc: t[:, kc * 512:(kc + 1) * 512]), start=(kc == 0), stop=(kc == 7))
                P.copy("act", memV.v(lambda t, mb=mb: t[:, mb, :]), bk[:])

            T_ = [g_t[i].v(lambda t: t[:, :, 0]) for i in range(10)]
            dt_, lrdt, ang, mag, lbre, lbim, tA, tB, tC, tD = T_
            P.act(dt_, s_ldt, AF.Exp)
            P.tt("dve", lrdt, s_are, dt_, ALU.mult)
            P.tt("dve", ang, s_aim, dt_, ALU.mult)
            P.act(mag, lrdt, AF.Exp)
            sincos(tA, tB, ang, tC, tD, g_t[8].v(lambda t: t[:, :, 2]), g_t[9].v(lambda t: t[:, :, 2]))
            P.tt("dve", lbre, mag, tB, ALU.mult)
            P.tt("dve", lbim, mag, tA, ALU.mult)
            T2 = [g_t[i].v(lambda t: t[:, :, 1]) for i in range(10)]
            nr, den, fre, fim, u1, u2, ivre, ivim, m2, u3 = T2
            P.ts("dve", nr, lbre, -1.0, ALU.add)
            P.tt("dve", u1, s_are, s_are, ALU.mult)
            P.tt("dve", u2, s_aim, s_aim, ALU.mult)
            P.tt("dve", den, u1, u2, ALU.add)
            P.recip(den, den)
            P.tt("dve", u1, nr, s_are, ALU.mult)
            P.tt("dve", u2, lbim, s_aim, ALU.mult)
            P.tt("dve", u1, u1, u2, ALU.add)
            P.tt("dve", fre, u1, den, ALU.mult)
            P.tt("dve", u1, lbim, s_are, ALU.mult)
            P.tt("dve", u2, nr, s_aim, ALU.mult)
            P.tt("dve", u1, u1, u2, ALU.subtract)
            P.tt("dve", fim, u1, den, ALU.mult)
            P.tt("dve", u1, lbre, lbre, ALU.mult)
            P.tt("dve", u2, lbim, lbim, ALU.mult)
            P.tt("dve", m2, u1, u2, ALU.add)
            P.recip(m2, m2)
            P.tt("dve", ivre, lbre, m2, ALU.mult)
            P.tt("dve", u3, lbim, m2, ALU.mult)
            P.ts("dve", ivim, u3, -1.0, ALU.mult)
            pwv = lambda ri, n: pw.v(lambda t, ri=ri, n=n: t[:, ri, n + 7, :])
            P.memset("dve", pwv(0, 0), 1.0)
            P.memset("dve", pwv(1, 0), 0.0)
            for n in range(1, 9):
                cmul(pwv(0, n), pwv(1, n), pwv(0, n - 1), pwv(1, n - 1), lbre, lbim, u1, u2)
            for n in range(-1, -8, -1):
                cmul(pwv(0, n), pwv(1, n), pwv(0, n + 1), pwv(1, n + 1), ivre, ivim, u1, u2)
            for j in range(8):
                for ri in range(2):
                    P.copy("dve", pwd.v(lambda t, ri=ri, j=j: t[:, ri, j, :]), pwv(ri, 7 - j))
            a8 = lambda j, ri: a8pw.v(lambda t, j=j, ri=ri: t[:, j, ri, :])
            P.copy("dve", a8(0, 0), pwv(0, 8))
            P.copy("dve", a8(0, 1), pwv(1, 8))
            for j in range(1, 6):
                cmul(a8(j, 0), a8(j, 1), a8(j - 1, 0), a8(j - 1, 1), a8(j - 1, 0), a8(j - 1, 1), u1, u2)
            bbre, bbim, w1, w2 = g_t[6], g_t[7], g_t[8], g_t[9]
            bc3 = lambda av: AV(av.ap.unsqueeze(2).to_broadcast([128, 16, 16]), av.key)
            P.tt("dve", w1[:], s_bre, bc3(fre), ALU.mult)
            P.tt("dve", w2[:], s_bim, bc3(fim), ALU.mult)
            P.tt("dve", bbre[:], w1[:], w2[:], ALU.subtract)
            P.tt("dve", w1[:], s_bim, bc3(fre), ALU.mult)
            P.tt("dve", w2[:], s_bre, bc3(fim), ALU.mult)
            P.tt("dve", bbim[:], w1[:], w2[:], ALU.add)

            def v4(plane):
                return AV(plane.ap.rearrange("p a (j c) -> p a j c", c=16), plane.key)

            def pwb(tl, ri, lo):
                return tl.v(lambda t, ri=ri, lo=lo: t[:, ri, lo:lo + 8, :].rearrange("p n a -> p a n").unsqueeze(3).to_broadcast([128, 16, 8, 16]))

            def cb(av):
                return AV(av.ap.unsqueeze(2).to_broadcast([128, 16, 8, 16]), av.key)

            def cgen(dst, pt, lo, cr, ci, neg_im):
                ta, tb = v4(planeX), v4(planeM)
                P.tt("dve", ta, pwb(pt, 0, lo), cb(cr), ALU.mult)
                P.tt("dve", tb, pwb(pt, 1, lo), cb(ci), ALU.mult)
                P.tt("dve", v4(dst[0]), ta, tb, ALU.subtract)
                P.tt("dve", ta, pwb(pt, 0, lo), cb(ci), ALU.mult)
                P.tt("dve", tb, pwb(pt, 1, lo), cb(cr), ALU.mult)
                P.tt("dve", v4(dst[1]), ta, tb, ALU.add)
                if neg_im:
                    P.ts("dve", v4(dst[1]), v4(dst[1]), -1.0, ALU.mult)

            cgen(planeK, pw, 8, s_cre, s_cim, True)
            for ri in range(2):
                P.copy("dve", simg.v(lambda t, ri=ri: t[:, ri * 2048:(ri + 1) * 2048].rearrange("p (a c) -> p a c", c=128)), planeK[ri])
            P.dma("pool", AV(ssmw[l * 4 + 3], ("ssmw", l * 4 + 3)), simg[:])
            cgen(planeL, pwd, 0, bbre[:], bbim[:], False)
            cgen(planeK, pw, 0, s_cre, s_cim, True)
            for g in range(32):
                a, par = g // 2, g % 2
                bk = bank_ab()
                rs = slice(par * 64, (par + 1) * 64)
                P.mm(bk[:, 0:128], AV(planeL[0].ap[rs, a, :], planeL[0].key), AV(planeK[0].ap[rs, a, :], planeK[0].key), start=True, stop=False)
                P.mm(bk[:, 0:128], AV(planeL[1].ap[rs, a, :], planeL[1].key), AV(planeK[1].ap[rs, a, :], planeK[1].key), start=False, stop=True)
                tt_ = ftmp()
                P.tt("dve", tt_[:, 0:128], bk[:, 0:128], tmask[:], ALU.mult)
                P.stt("dve", simg[:, g * 128:(g + 1) * 128], ident[:], dtab(g), tt_[:, 0:128], ALU.mult, ALU.add)
            P.dma("pool", AV(ssmw[l * 4 + 2], ("ssmw", l * 4 + 2)), simg[:])
            for half in range(2):
                P.memset("dve", simg[:], 0.0)
                for gl_ in range(16):
                    g = half * 16 + gl_
                    a, par = g // 2, g % 2
                    rs = slice(par * 64, (par + 1) * 64)
                    bk = bank_ab()
                    for ri in range(2):
                        P.tr(bk[:, ri * 64:(ri + 1) * 64], AV(planeL[ri].ap[rs, a, :], planeL[ri].key),
                             AV(ident.t[rs, rs], ident.key))
                    P.copy("dve", simg.v(lambda t, gl_=gl_, par=par: t[:, gl_ * 256:(gl_ + 1) * 256].rearrange("p (r c) -> p r c", r=2)[:, :, par * 64:(par + 1) * 64]),
                           bk.v(lambda t: t[:, 0:128].rearrange("p (r c) -> p r c", r=2)))
                P.dma("pool", AV(ssmw[l * 4 + half], ("ssmw", l * 4 + half)), simg[:])
            P.memset("dve", carry[:], 0.0)

            for i in range(NT):
                t0 = i * TT
                nblk = 4 * (i + 1)
                for sub in range(4):
                    xsub = xs[sub % 2]
                    P.dma("sp", xsub[:], AV(xin[t0 + sub * 128: t0 + (sub + 1) * 128, :], (kin, i)))
                    for half in range(2):
                        bk = bank_rot("t0", [0, 1, 2, 3])
                        for q in range(4):
                            kc = half * 4 + q
                            P.tr(bk[:, q * 128:(q + 1) * 128], xsub[:, kc * 128:(kc + 1) * 128], ident[:])
                        P.copy("act",
                               xT.v(lambda t, half=half, sub=sub: t[:, half * 4:half * 4 + 4, sub * 128:(sub + 1) * 128]),
                               bk.v(lambda t: t[:].rearrange("p (a b) -> p a b", a=4)))
                P.dma("pool", cosT[64:96, :], AV(csd[0, :, t0:t0 + TT], ("csd", i)))
                P.dma("pool", sinS[64:96, :], AV(csd[1, :, t0:t0 + TT], ("csd", i)))

                w0 = w_next("w", l, C_IP0)
                bk = inproj_tile(w0, 0)
                c32 = ftmp()
                P.copy("act", c32[:], bk[:])
                sq = ftmp()
                P.act(sq[:], bk[:], AF.Square)
                bk2 = bank_ab()
                P.mm(bk2[:], onesk[:], sq[:])
                rstd = ftmp()
                rms_rstd(rstd[:], bk2[:])
                P.stt("dve", ckvn[:], c32[:], kvn, rstd[:], ALU.mult, ALU.mult)
                bk = inproj_tile(w0, 1)
                bk2 = inproj_tile(w0, 2)
                ta = ftmp()
                tb = ftmp()
                P.tt("dve", ta[64:96, :], bk[64:96, :], cosT[64:96, :], ALU.mult)
                P.tt("dve", tb[64:96, :], bk2[64:96, :], sinS[64:96, :], ALU.mult)
                P.tt("dve", krb[64:96, :], ta[64:96, :], tb[64:96, :], ALU.add)
                for h in range(8):
                    bk = bank_ab()
                    P.mm(bk[0:64, :], uqkv[:, h * 64:(h + 1) * 64], ckvn[:])
                    ks = Kst[h % 2]
                    P.copy("act", ks[0:64, :], bk[0:64, :])
                    P.copy("dve", ks[64:96, :], krb[64:96, :])
                    P.dma("pool", AV(Kc[h, :, t0:t0 + TT], ("Kc", h, i)), ks[0:96, :])
                for sub in range(4):
                    bk = bank_ab()
                    P.mm(bk[:], ckvn[:, sub * 128:(sub + 1) * 128], uqkv[:, 512:1024])
                    P.copy("act" if sub % 2 else "dve", AV(Vst.ap[:, sub, :], Vst.key), bk[:])
                P.dma("pool", AV(Vc[4 * i:4 * i + 4].rearrange("b p c -> p b c"), ("Vc", i)), Vst)

                w1_ = w_next("w", l, C_IP1)
                c32s = []
                bk2 = bank_s()
                for kc in range(2):
                    bk = inproj_tile(w1_, kc)
                    c32 = ftmp()
                    P.copy("act", c32[:], bk[:])
                    sq = ftmp()
                    P.act(sq[:], bk[:], AF.Square)
                    P.mm(bk2[:], onesq[:], sq[:], start=(kc == 0), stop=(kc == 1))
                    c32s.append(c32)
                rstd = ftmp()
                rms_rstd(rstd[:], bk2[:])
                for kc in range(2):
                    P.stt("dve", cqn.v(lambda t, kc=kc: t[:, kc, :]), c32s[kc][:], qn(kc), rstd[:], ALU.mult, ALU.mult)
                def emit_q(h):
                    kb_ = Kbuf[h % 2]
                    P.dma("sp", kb_[0:96, 0:nblk * 128], AV(Kc[h, :, 0:nblk * 128], ("Kc", h, i)),
                          extra_reads=[("Kc", h, ii) for ii in range(i)])
                    bq = bank_ab()
                    bq2 = bank_ab()
                    for kc in range(2):
                        P.mm(bq[0:96, :], uqkv[:, 1024 + kc * 768 + h * 96: 1024 + kc * 768 + (h + 1) * 96],
                             cqn.v(lambda t, kc=kc: t[:, kc, :]), start=(kc == 0), stop=(kc == 1))
                    for kc in range(2):
                        P.mm(bq2[0:96, :], uqkv[:, 2560 + kc * 768 + h * 96: 2560 + kc * 768 + (h + 1) * 96],
                             cqn.v(lambda t, kc=kc: t[:, kc, :]), start=(kc == 0), stop=(kc == 1))
                    qh = Qh[h % 2]
                    P.copy("act", qh[0:64, :], bq[0:64, :])
                    ta = ftmp()
                    tb = ftmp()
                    P.tt("dve", ta[64:96, :], bq[64:96, :], cosT[64:96, :], ALU.mult)
                    P.tt("dve", tb[64:96, :], bq2[64:96, :], sinS[64:96, :], ALU.mult)
                    P.tt("dve", qh[64:96, :], ta[64:96, :], tb[64:96, :], ALU.add)

                emit_q(0)

                w3_ = w_next("w", l, C_IP3)
                for t_ in range(4):
                    bk = inproj_tile(w3_, t_, banks=[0, 1, 2, 3])
                    P.copy("act" if t_ % 2 else "dve", UTb.v(lambda t, t_=t_: t[:, t_, :]), bk[:])
                for t_ in range(4):
                    bk = bank_rot("sel", [4, 5, 6, 7])
                    for gp in range(8):
                        for j in range(8):
                            P.mm(bk[:, gp * 64:(gp + 1) * 64],
                                 big.v(lambda t, gp=gp, j=j: t[:, gp, 112 - 16 * j: 240 - 16 * j]),
                                 UTb.v(lambda t, t_=t_, j=j: t[:, t_, :].rearrange("p (k j) -> p k j", j=8)[:, :, j]),
                                 start=(j == 0), stop=(j == 7))
                    P.copy("act" if t_ % 2 else "dve", Vg.v(lambda t, t_=t_: t[:, 8 * t_:8 * t_ + 8, :]),
                           bk.v(lambda t: t[:].rearrange("p (a b) -> p a b", a=8)))
                for half in range(2):
                    wg = w_next("s", l, half)
                    bre_, bim_ = bank[2 * half], bank[2 * half + 1]
                    for gl_ in range(16):
                        g = half * 16 + gl_
                        par = g % 2
                        al = gl_ // 2
                        for ri, bb in ((0, bre_), (1, bim_)):
                            P.mm(bb[:, al * 64:(al + 1) * 64],
                                 wg[:, gl_ * 256 + ri * 128: gl_ * 256 + (ri + 1) * 128],
                                 Vg.v(lambda t, g=g: t[:, g, :]), start=(par == 0), stop=(par == 1))
                    P.copy("act", Hre.v(lambda t, half=half: t[:, 8 * half:8 * half + 8, :]),
                           bre_.v(lambda t: t[:].rearrange("p (a b) -> p a b", a=8)))
                    P.copy("dve", Him.v(lambda t, half=half: t[:, 8 * half:8 * half + 8, :]),
                           bim_.v(lambda t: t[:].rearrange("p (a b) -> p a b", a=8)))
                cre_ = carry.v(lambda t: t[:, 0, :])
                cim_ = carry.v(lambda t: t[:, 1, :])
                h0r = Hre.v(lambda t: t[:, :, 0])
                h0i = Him.v(lambda t: t[:, :, 0])
                q1 = stat[:, 0:1]
                sA = g_t[0].v(lambda t: t[:, :, 2])
                sB = g_t[1].v(lambda t: t[:, :, 2])
                sC = g_t[2].v(lambda t: t[:, :, 2])
                sD = g_t[3].v(lambda t: t[:, :, 2])
                cmul(sC, sD, a8(0, 0), a8(0, 1), cre_, cim_, sA, sB)
                P.tt("dve", h0r, h0r, sC, ALU.add)
                P.tt("dve", h0i, h0i, sD, ALU.add)
                for j in range(6):
                    s_ = 1 << j
                    n_ = 64 - s_
                    arb = AV(a8(j, 0).ap.unsqueeze(2).to_broadcast([128, 16, n_]), a8pw.key)
                    aib = AV(a8(j, 1).ap.unsqueeze(2).to_broadcast([128, 16, n_]), a8pw.key)
                    lo = lambda tl: tl.v(lambda t: t[:, :, 0:n_]) if isinstance(tl, Tile) else AV(tl.ap[:, :, 0:n_], tl.key)
                    hi = lambda tl: tl.v(lambda t: t[:, :, s_:64])
                    P.tt("dve", lo(st1), lo(Hre), arb, ALU.mult)
                    P.tt("dve", lo(st2), lo(Him), aib, ALU.mult)
                    P.tt("dve", lo(st1), lo(st1), lo(st2), ALU.subtract)
                    P.tt("dve", lo(st2), lo(Him), arb, ALU.mult)
                    P.tt("dve", lo(st3), lo(Hre), aib, ALU.mult)
                    P.tt("dve", lo(st2), lo(st2), lo(st3), ALU.add)
                    P.tt("dve", hi(Hre), hi(Hre), lo(st1), ALU.add)
                    P.tt("dve", hi(Him), hi(Him), lo(st2), ALU.add)
                w2_ = w_next("w", l, C_IP2)
                for a in range(4):
                    vb = Vbuf[0]
                    P.dma("sp", vb.v(lambda t: t[:, 0:nblk, :]),
                          AV(Vc[0:nblk, :, a * 128:(a + 1) * 128].rearrange("b p c -> p b c"), ("Vc", i)),
                          extra_reads=[("Vc", ii) for ii in range(i)])
                    sz = szm
                    for hh in range(2):
                        h = 2 * a + hh
                        rows = slice(hh * 64, (hh + 1) * 64)
                        kb_ = Kbuf[h % 2]
                        qh = Qh[h % 2]
                        num, den = bank_nd()
                        pend = None

                        def pv(pn):
                            kb, pt, c0 = pn
                            P.mm(num[:, c0:TT], vb.v(lambda t, kb=kb: t[:, kb, :]), pt[:, c0:TT], start=(kb == 0), stop=(kb == nblk - 1))
                            P.mm(den[:, c0:TT], ones_bf[:], pt[:, c0:TT], start=(kb == 0), stop=(kb == nblk - 1))

                        for kb in range(nblk):
                            v_ = kb - 4 * i
                            c0 = 128 * v_ if v_ > 0 else 0
                            sbk = bank_s()
                            P.mm(sbk[:, c0:TT], kb_[0:96, kb * 128:(kb + 1) * 128], qh[0:96, c0:TT])
                            pt = ptnext()
                            P.act(pt[:, c0:TT], sbk[:, c0:TT], AF.Exp, scale=MLA_SCALE)
                            if v_ >= 0:
                                P.tt("dve", pt[:, c0:TT], pt[:, c0:TT], cmask.v(lambda t, v_=v_, c0=c0: t[:, v_, c0:TT]), ALU.mult)
                            if pend is not None:
                                pv(pend)
                            pend = (kb, pt, c0)
                            if kb == 1 and h + 1 < 8:
                                emit_q(h + 1)
                        pv(pend)
                        if hh == 0:
                            zb = inproj_tile(w2_, a)
                            P.act(sz[:], zb[:], AF.Sigmoid)
                            P.tt("dve", sz[:], sz[:], zb[:], ALU.mult)
                        rd = ftmp()
                        P.recip(rd[rows, :], den[rows, :])
                        P.tt("dve", rd[rows, :], rd[rows, :], num[rows, :], ALU.mult)
                        P.tt("dve", ymla.v(lambda t, a=a, rows=rows: t[rows, a, :]), rd[rows, :], sz[rows, :], ALU.mult)


                P.copy("dve", hpre.v(lambda t: t[:, :, 0]), cre_)
                P.copy("dve", hpim.v(lambda t: t[:, :, 0]), cim_)
                P.copy("act", hpre.v(lambda t: t[:, :, 1:64]), Hre.v(lambda t: t[:, :, 0:63]))
                P.copy("act", hpim.v(lambda t: t[:, :, 1:64]), Him.v(lambda t: t[:, :, 0:63]))
                P.copy("dve", cre_, Hre.v(lambda t: t[:, :, 63]))
                P.copy("dve", cim_, Him.v(lambda t: t[:, :, 63]))
                wT = w_next("s", l, 2)
                wE = w_next("s", l, 3, keep=1)
                for t_ in range(4):
                    bk = bank_rot("y", [4, 5, 6, 7])
                    for gp in range(8):
                        g = 8 * t_ + gp
                        a, par = g // 2, g % 2
                        rs = slice(par * 64, (par + 1) * 64)
                        oo = bk[:, gp * 64:(gp + 1) * 64]
                        P.mm(oo, wT[:, g * 128:(g + 1) * 128], Vg.v(lambda t, g=g: t[:, g, :]), start=True, stop=False)
                        P.mm(oo, wE[rs, a * 128:(a + 1) * 128], hpre.v(lambda t, rs=rs, a=a: t[rs, a, :]), start=False, stop=False)
                        P.mm(oo, wE[rs, 2048 + a * 128: 2048 + (a + 1) * 128], hpim.v(lambda t, rs=rs, a=a: t[rs, a, :]), start=False, stop=True)
                    P.copy("act" if t_ % 2 else "dve", Yg.v(lambda t, t_=t_: t[:, 8 * t_:8 * t_ + 8, :]),
                           bk.v(lambda t: t[:].rearrange("p (a b) -> p a b", a=8)))
                for t_ in range(4):
                    bk = bank_rot("inv", [0, 1, 2, 3])
                    for ii in range(8):
                        for gp in range(8):
                            P.mm(bk.v(lambda t, ii=ii: t[:].rearrange("p (k i) -> p k i", i=8)[:, :, ii]),
                                 big.v(lambda t, ii=ii, gp=gp: t[:, ii, 112 - 16 * gp: 240 - 16 * gp]),
                                 Yg.v(lambda t, t_=t_, gp=gp: t[:, 8 * t_ + gp, :]), start=(gp == 0), stop=(gp == 7))
                    sq = ftmp()
                    P.act(sq[:], bk[:], AF.Square)
                    P.ts("dve", sq[:], sq[:], 0.044715, ALU.mult, 1.0, ALU.add)
                    P.tt("dve", sq[:], sq[:], bk[:], ALU.mult)
                    P.act(sq[:], sq[:], AF.Sigmoid, scale=1.5957691216057308)
                    P.tt("dve", gel.v(lambda t, t_=t_: t[:, t_, :]), sq[:], bk[:], ALU.mult)
                if dbg and l == 0 and i == 0:
                    P.dma("pool", AV(dbg_t["gel"], "d_gel"), gel.v(lambda t: t[:].rearrange("p a b -> p (a b)")))
                wgl = w_next("w", l, C_GLU)
                w4_ = w_next("w", l, C_IP4, keep=1)
                for n in range(4):
                    ba = bank_rot("glu", [2, 3, 4, 5])
                    bb = bank_rot("glu", [2, 3, 4, 5])
                    for kc in range(4):
                        P.mm(ba[:], wgl[:, kc * 1024 + n * 128: kc * 1024 + (n + 1) * 128], gel.v(lambda t, kc=kc: t[:, kc, :]),
                             start=(kc == 0), stop=(kc == 3))
                    for kc in range(4):
                        P.mm(bb[:], wgl[:, kc * 1024 + 512 + n * 128: kc * 1024 + 512 + (n + 1) * 128], gel.v(lambda t, kc=kc: t[:, kc, :]),
                             start=(kc == 0), stop=(kc == 3))
                    sg = ftmp()
                    P.act(sg[:], bb[:], AF.Sigmoid, bias=bglu(4 + n))
                    P.stt("dve", sg[:], ba[:], bglu(n), sg[:], ALU.add, ALU.mult)
                    zb = inproj_tile(w4_, n, banks=[0, 1, 6, 7])
                    sz = silu_from_bank(zb)
                    P.tt("dve", yssm.v(lambda t, n=n: t[:, n, :]), sg[:], sz[:], ALU.mult)

                w5_ = w_next("w", l, C_IP5)
                w6_ = w_next("w", l, C_IP6, keep=1)
                for h in range(4):
                    bk = inproj_tile(w5_, h)
                    P.copy("act", qmb[:], bk[:])
                    num, den = bank_nd()
                    for mb in range(2):
                        sbk = bank_s()
                        P.mm(sbk[:], memKT.v(lambda t, h=h, mb=mb: t[:, h, mb * 128:(mb + 1) * 128]), qmb[:])
                        pt = ptnext()
                        P.act(pt[:], sbk[:], AF.Exp, scale=MEM_SCALE)
                        P.mm(num[:], memV.v(lambda t, h=h, mb=mb: t[:, mb, h * 128:(h + 1) * 128]), pt[:], start=(mb == 0), stop=(mb == 1))
                        P.mm(den[:], ones_bf[:], pt[:], start=(mb == 0), stop=(mb == 1))
                    rd = ftmp()
                    P.recip(rd[:], den[:])
                    P.tt("dve", rd[:], rd[:], num[:], ALU.mult)
                    zb = inproj_tile(w6_, h)
                    sz = silu_from_bank(zb)
                    P.tt("dve", ymem.v(lambda t, h=h: t[:, h, :]), rd[:], sz[:], ALU.mult)

                wpm = pmem
                for j in range(8):
                    wm_ = w_next("w", l, C_MG0 + j)
                    macc = ftmp()
                    for k in range(3):
                        ysrc = (yssm, ymla, ymem)[k]
                        ba = bank_rot("mga", [4, 5, 6, 7])
                        for kc in range(4):
                            if k < 2:
                                lw = wm_[:, 3072 + k * 512 + kc * 128: 3072 + k * 512 + (kc + 1) * 128]
                            else:
                                lw = wpm[:, kc * 1024 + j * 128: kc * 1024 + (j + 1) * 128]
                            P.mm(ba[:], lw, ysrc.v(lambda t, kc=kc: t[:, kc, :]), start=(kc == 0), stop=(kc == 3))
                        bg = bank_rot("mgg", [0, 1, 2, 3])
                        for kc in range(8):
                            P.mm(bg[:], wm_[:, k * 1024 + kc * 128: k * 1024 + (kc + 1) * 128], xT.v(lambda t, kc=kc: t[:, kc, :]),
                                 start=(kc == 0), stop=(kc == 7))
                        gt = ftmp()
                        P.act(gt[:], bg[:], AF.Sigmoid, bias=bgate(k, j))
                        if k == 0:
                            P.tt("dve", macc[:], gt[:], ba[:], ALU.mult)
                        elif k == 1:
                            P.tt("dve", gt[:], gt[:], ba[:], ALU.mult)
                            P.tt("dve", macc[:], macc[:], gt[:], ALU.add)
                        else:
                            P.tt("dve", gt[:], gt[:], ba[:], ALU.mult)
                            P.tt("dve", mT.v(lambda t, j=j: t[:, j, :]), macc[:], gt[:], ALU.add)
                if dbg and l == 0 and i == 0:
                    for nm, tl in (("yssm", yssm), ("ymla", ymla), ("ymem", ymem)):
                        P.dma("pool", AV(dbg_t[nm], "d_" + nm), tl.v(lambda t: t[:].rearrange("p a b -> p (a b)")))
                    P.dma("pool", AV(dbg_t["mT2"], "d_mT2"), mT.v(lambda t: t[:, 0:4, :].rearrange("p a b -> p (a b)")))
                wo0 = w_next("w", l, C_WO0)
                wo1 = w_next("w", l, C_WO1, keep=1)
                for sub in range(4):
                    xsub = lnbuf.v(lambda t: t[:, 3, :])
                    P.dma("pool", xsub, AV(xin[t0 + sub * 128: t0 + (sub + 1) * 128, :], (kin, i)))
                    for half, wo in ((0, wo0), (1, wo1)):
                        bo = bank_rot("wo", [0, 1, 2, 3])
                        for kc in range(8):
                            P.mm(bo[:], mT.v(lambda t, kc=kc, sub=sub: t[:, kc, sub * 128:(sub + 1) * 128]),
                                 wo[:, kc * 512:(kc + 1) * 512], start=(kc == 0), stop=(kc == 7))
                        P.stt("dve", AV(vln.ap[:, half * 512:(half + 1) * 512], vln.key), AV(xsub.ap[:, half * 512:(half + 1) * 512], xsub.key),
                              ALPHA, bo[:], ALU.mult, ALU.add)
                    P.reduce("dve", stat[:, 0:1], vln, ALU.add)
                    P.act(junk, vln, AF.Square)
                    P.reduce("dve", stat[:, 1:2], junk, ALU.add)
                    P.ts("dve", stat[:, 2:3], stat[:, 0:1], 1.0 / D, ALU.mult)
                    P.tt("dve", stat[:, 3:4], stat[:, 2:3], stat[:, 2:3], ALU.mult)
                    P.stt("dve", stat[:, 4:5], stat[:, 1:2], 1.0 / D, stat[:, 3:4], ALU.mult, ALU.subtract)
                    P.ts("dve", stat[:, 4:5], stat[:, 4:5], EPS, ALU.add)
                    P.act(stat[:, 5:6], stat[:, 4:5], AF.Sqrt)
                    P.recip(stat[:, 5:6], stat[:, 5:6])
                    P.ts("dve", oln, vln, stat[:, 2:3], ALU.subtract, stat[:, 5:6], ALU.mult)
                    P.tt("dve", oln, oln, lng[:], ALU.mult)
                    P.tt("dve", oln, oln, lnb[:], ALU.add)
                    P.dma("pool", AV(xout[t0 + sub * 128: t0 + (sub + 1) * 128, :], (kout, i)), oln)
        P.emit()
    return nc, P


_CACHE = {}


def _prep(inputs, S, NL):
    wimg = np.concatenate([host_weight_image(inputs, l) for l in range(NL)], axis=0)
    sm, ln = zip(*[host_small(inputs, l) for l in range(NL)])
    com = dict(host_consts())
    com["wimg"] = np.ascontiguousarray(wimg)
    com["small"] = np.ascontiguousarray(np.stack(sm))
    com["lnp"] = np.ascontiguousarray(np.stack(ln))
    return com


def run(inputs, S=SEQ, NL=DEPTH, cores=8, dbg=False, ret=None):
    inputs = {k: np.asarray(v) for k, v in inputs.items()}
    key = (S, NL, dbg)
    if key not in _CACHE:
        _CACHE[key] = build(S, NL, dbg)
    nc, P = _CACHE[key]
    com = _prep(inputs, S, NL)
    in_maps = []
    for b in range(cores):
        m = dict(com)
        m["x"] = np.ascontiguousarray(inputs["x"][b, :S]).astype(np.float32)
        m["mem"] = np.ascontiguousarray(inputs["mem"][b]).astype(np.float32)
        m["pos"] = np.ascontiguousarray(inputs["positions"][b, :S]).reshape(1, S).astype(np.int32)
        in_maps.append(m)
    import time as _t
    _t0 = _t.time()
    res = run_bass_kernel_spmd(nc, in_maps, core_ids=list(range(cores)))
    print("KERNEL spmd run seconds", _t.time() - _t0, "stats", P.stats, "waits", P.nwaits, flush=True)
    if ret is not None:
        ret.update({k: np.asarray(v) for k, v in res.results[0].items()})
    return np.stack([np.asarray(r["y"]) for r in res.results], axis=0).astype(np.float32)


def kernel(**inputs):
    return run(inputs)
```
